# Optimizing a Trainium2 kernel written in Bass

```python
import jax, jax.numpy as jnp
from jax import lax
import numpy as np

D_MODEL = 1024
BATCH = 4
SEQ = 4096
DEPTH = 1
DEC_BATCH = 32
DEC_SEQ = 64
PAST_LEN = 2048

CHUNK = 64
D_CONV = 512
D_LRU = 512
D_MIX = D_CONV + D_LRU
N_LRU_HEADS = 8
LRU_HEAD_DIM = D_LRU // N_LRU_HEADS
CONV_A_WIDTH = 3
LRU_CONV_WIDTH = 4
LRU_C = 8.0
D_FF = 4 * D_MODEL
D_IN = 3 * D_CONV + 2 * D_LRU
EPS = 1e-6

kernel_name = "hymba_style_conv_rglru_stream_step"


def rms_norm(x, g):
    xf = x.astype(jnp.float32)
    y = xf * lax.rsqrt(jnp.mean(xf * xf, axis=-1, keepdims=True) + EPS)
    return (y * g.astype(jnp.float32)).astype(x.dtype)


def causal_depthwise_conv(x, buf, w):
    width = w.shape[0]
    t_len = x.shape[1]
    xp = jnp.concatenate([buf.astype(x.dtype), x], axis=1)
    y = xp[:, 0:t_len] * w[0]
    for k in range(1, width):
        y = y + xp[:, k:k + t_len] * w[k]
    return y, xp[:, t_len:]


def rg_lru(x, h0, wa, ba, wx, bx, a_param, reset_first):
    b_sz, t_len, _ = x.shape
    xh = x.reshape(b_sz, t_len, N_LRU_HEADS, LRU_HEAD_DIM)
    r = jax.nn.sigmoid(jnp.einsum("bthi,hij->bthj", xh, wa) + ba).reshape(b_sz, t_len, D_LRU)
    i = jax.nn.sigmoid(jnp.einsum("bthi,hij->bthj", xh, wx) + bx).reshape(b_sz, t_len, D_LRU)
    log_a = -LRU_C * r.astype(jnp.float32) * jax.nn.softplus(a_param.astype(jnp.float32))
    a = jnp.exp(log_a)
    mult = jnp.sqrt(-jnp.expm1(2.0 * log_a))
    if reset_first:
        mult = mult.at[:, 0].set(1.0)
    b = mult * (i * x).astype(jnp.float32)

    def step(h, ab):
        a_t, b_t = ab
        h = a_t * h + b_t
        return h, h

    h_last, hs = lax.scan(step, h0.astype(jnp.float32),
                          (jnp.swapaxes(a, 0, 1), jnp.swapaxes(b, 0, 1)))
    return jnp.swapaxes(hs, 0, 1).astype(x.dtype), h_last


def hybrid_layer(x, conv_a_buf, lru_buf, h0, reset_first, norm1_g, w_in, conv_a_w,
                 lru_conv_w, lru_conv_b, lru_wa, lru_ba, lru_wx, lru_bx, lru_a_param,
                 w_out, norm2_g, w_up, w_down):
    hn = rms_norm(x, norm1_g)
    proj = hn @ w_in
    g_b, g_c, x_a, x_l, g_l = jnp.split(
        proj, [D_CONV, 2 * D_CONV, 3 * D_CONV, 3 * D_CONV + D_LRU], axis=-1)
    c_a, new_conv_a = causal_depthwise_conv(g_c * x_a, conv_a_buf, conv_a_w)
    out_a = g_b * c_a
    c_l, new_lru_buf = causal_depthwise_conv(x_l, lru_buf, lru_conv_w)
    h_seq, h_last = rg_lru(c_l + lru_conv_b, h0, lru_wa, lru_ba, lru_wx, lru_bx,
                           lru_a_param, reset_first)
    out_b = h_seq * jax.nn.gelu(g_l)
    x = x + jnp.concatenate([out_a, out_b], axis=-1) @ w_out
    hm = rms_norm(x, norm2_g)
    x = x + jnp.square(jax.nn.relu(hm @ w_up)) @ w_down
    return x, new_conv_a, new_lru_buf, h_last


def setup_inputs(seed: int = 0) -> dict:
    key = jax.random.key(seed)
    ks = jax.random.split(key, 24)
    f32 = jnp.float32
    nrm = lambda k, s, scale: (jax.random.normal(k, s, f32) * scale)
    r0 = jax.random.uniform(ks[13], (DEPTH, D_LRU), f32, 0.9, 0.999)
    return {
        "x_prompt": nrm(ks[0], (BATCH, SEQ, D_MODEL), 1.0),
        "x_sample": nrm(ks[1], (DEC_BATCH, DEC_SEQ, D_MODEL), 1.0),
        "state_conv_a": nrm(ks[2], (DEPTH, DEC_BATCH, CONV_A_WIDTH - 1, D_CONV), 1.0),
        "state_lru_conv": nrm(ks[3], (DEPTH, DEC_BATCH, LRU_CONV_WIDTH - 1, D_LRU), 1.0),
        "state_lru_h": nrm(ks[4], (DEPTH, DEC_BATCH, D_LRU), 0.5),
        "norm1_g": 1.0 + nrm(ks[5], (DEPTH, D_MODEL), 0.02),
        "w_in": nrm(ks[6], (DEPTH, D_MODEL, D_IN), D_MODEL ** -0.5),
        "conv_a_w": nrm(ks[7], (DEPTH, CONV_A_WIDTH, D_CONV), CONV_A_WIDTH ** -0.5),
        "lru_conv_w": nrm(ks[8], (DEPTH, LRU_CONV_WIDTH, D_LRU), LRU_CONV_WIDTH ** -0.5),
        "lru_conv_b": nrm(ks[9], (DEPTH, D_LRU), 0.01),
        "lru_wa": nrm(ks[10], (DEPTH, N_LRU_HEADS, LRU_HEAD_DIM, LRU_HEAD_DIM), LRU_HEAD_DIM ** -0.5),
        "lru_ba": nrm(ks[11], (DEPTH, N_LRU_HEADS, LRU_HEAD_DIM), 0.01),
        "lru_wx": nrm(ks[12], (DEPTH, N_LRU_HEADS, LRU_HEAD_DIM, LRU_HEAD_DIM), LRU_HEAD_DIM ** -0.5),
        "lru_bx": nrm(ks[14], (DEPTH, N_LRU_HEADS, LRU_HEAD_DIM), 0.01),
        "lru_a_param": jnp.log(jnp.expm1(-jnp.log(r0))),
        "w_out": nrm(ks[15], (DEPTH, D_MIX, D_MODEL), D_MIX ** -0.5),
        "norm2_g": 1.0 + nrm(ks[16], (DEPTH, D_MODEL), 0.02),
        "w_up": nrm(ks[17], (DEPTH, D_MODEL, D_FF), D_MODEL ** -0.5),
        "w_down": nrm(ks[18], (DEPTH, D_FF, D_MODEL), D_FF ** -0.5),
        "norm_f_g": 1.0 + nrm(ks[19], (D_MODEL,), 0.02),
    }


def reference(x_prompt, x_sample, state_conv_a, state_lru_conv, state_lru_h,
              norm1_g, w_in, conv_a_w, lru_conv_w, lru_conv_b, lru_wa, lru_ba,
              lru_wx, lru_bx, lru_a_param, w_out, norm2_g, w_up, w_down, norm_f_g):
    n_p = x_prompt.shape[0]
    xp = x_prompt
    xs = x_sample
    pa, pl, ph, sa, sl, sh = [], [], [], [], [], []
    for l in range(DEPTH):
        layer_w = (norm1_g[l], w_in[l], conv_a_w[l], lru_conv_w[l], lru_conv_b[l],
                   lru_wa[l], lru_ba[l], lru_wx[l], lru_bx[l], lru_a_param[l],
                   w_out[l], norm2_g[l], w_up[l], w_down[l])
        zero_a = jnp.zeros((n_p, CONV_A_WIDTH - 1, D_CONV), xp.dtype)
        zero_l = jnp.zeros((n_p, LRU_CONV_WIDTH - 1, D_LRU), xp.dtype)
        zero_h = jnp.zeros((n_p, D_LRU), jnp.float32)
        xp, ca, cl, hl = hybrid_layer(xp, zero_a, zero_l, zero_h, True, *layer_w)
        pa.append(ca); pl.append(cl); ph.append(hl)
        xs, ca, cl, hl = hybrid_layer(xs, state_conv_a[l], state_lru_conv[l],
                                      state_lru_h[l], False, *layer_w)
        sa.append(ca); sl.append(cl); sh.append(hl)
    y_prompt = rms_norm(xp, norm_f_g)
    y_sample = rms_norm(xs, norm_f_g)
    return (y_prompt, y_sample, jnp.stack(pa), jnp.stack(pl), jnp.stack(ph),
            jnp.stack(sa), jnp.stack(sl), jnp.stack(sh))
```

```python
import numpy as np
from contextlib import ExitStack
import concourse.bass as bass
import concourse.mybir as mybir
from concourse.bass_utils import run_bass_kernel_spmd

F32, BF16 = mybir.dt.float32, mybir.dt.bfloat16
AF = mybir.ActivationFunctionType
ALU = mybir.AluOpType

NCORES = 8
D = 1024
EPS = 1e-6
R = 15
OC_ORDER = [12, 16, 13, 17, 14, 18, 15, 19, 0, 4, 8, 1, 5, 9, 2, 6, 10, 3, 7, 11]
POS_OF_OC = {oc: i for i, oc in enumerate(OC_ORDER)}
CH_OUT, CH_UP, CH_DOWN, NCHUNK = 20, 28, 60, 92
GK0 = 2.0 * 0.7978845608028654
GK1 = GK0 * 0.044715


class Buf:
    __slots__ = ("name", "w", "rs")

    def __init__(self, name):
        self.name = name
        self.w = None
        self.rs = {}


class Sched:
    ENG = ("pe", "act", "dve", "pool", "sp")

    def __init__(self, nc, es):
        self.nc, self.es = nc, es
        self.sem, self.cnt = {}, {}
        for e in self.ENG[:4]:
            self.sem[e] = es.enter_context(nc.semaphore("s_" + e))
            self.cnt[e] = 0
        self.seen = {e: {} for e in self.ENG}
        self.prog = {e: [] for e in self.ENG}
        self.pools = {}

    def _waits(self, eng, reads, writes, extra=()):
        need = {}

        def add(ev):
            if ev is not None and need.get(ev[0], 0) < ev[1]:
                need[ev[0]] = ev[1]
        for b in reads:
            add(b.w)
        for b in writes:
            add(b.w)
            for k, v in b.rs.items():
                add((k, v))
        for ev in extra:
            add(ev)
        out, seen = [], self.seen[eng]
        for k, v in need.items():
            if k == eng and eng == "pe":
                continue
            if seen.get(k, 0) >= v:
                continue
            seen[k] = v
            out.append((k, v))
        return out

    def _commit(self, ev, reads, writes):
        for b in writes:
            b.w = ev
            b.rs = {}
        for b in reads:
            if b.rs.get(ev[0], 0) < ev[1]:
                b.rs[ev[0]] = ev[1]

    def group(self, eng, fns, reads=(), writes=()):
        waits = self._waits(eng, reads, writes)
        self.cnt[eng] += 1
        ev = (eng, self.cnt[eng])
        self.prog[eng].append((waits, list(fns), (eng, 1)))
        self._commit(ev, reads, writes)
        return ev

    def op(self, eng, fn, reads=(), writes=()):
        return self.group(eng, [fn], reads, writes)

    def dma_pool(self, name, n):
        keys = []
        for i in range(n):
            k = "%s%d" % (name, i)
            self.sem[k] = self.es.enter_context(self.nc.semaphore("d_" + k))
            self.cnt[k] = 0
            keys.append(k)
        self.pools[name] = [keys, 0]

    def dma(self, q, out, in_, reads, writes, pool, **kw):
        p = self.pools[pool]
        k = p[0][p[1] % len(p[0])]
        p[1] += 1
        prev = self.cnt[k]
        waits = self._waits(q, reads, writes, [(k, prev)] if prev else [])
        self.cnt[k] = prev + 16
        ev = (k, prev + 16)
        self.prog[q].append((waits, [lambda e: e.dma_start(out=out, in_=in_, **kw)], (k, 16)))
        self._commit(ev, reads, writes)
        return ev

    def wait_events(self, eng, evs):
        waits = self._waits(eng, (), (), evs)
        self.prog[eng].append((waits, [], None))

    def check_no_deadlock(self):
        val = {k: 0 for k in self.sem}
        pos = {e: 0 for e in self.ENG}
        while True:
            prog = False
            for e in self.ENG:
                q = self.prog[e]
                while pos[e] < len(q):
                    waits, fns, inc = q[pos[e]]
                    if any(val[k] < v for k, v in waits):
                        break
                    if inc is not None:
                        val[inc[0]] += inc[1]
                    pos[e] += 1
                    prog = True
            if not prog:
                break
        stuck = {e: (pos[e], len(self.prog[e])) for e in self.ENG if pos[e] < len(self.prog[e])}
        if stuck:
            msg = []
            for e, (p, n) in stuck.items():
                msg.append("%s at %d/%d waits %s" % (e, p, n, [(k, v, val[k]) for k, v in self.prog[e][p][0] if val[k] < v]))
            raise RuntimeError("static deadlock: " + "; ".join(msg))

    def emit(self):
        self.check_no_deadlock()
        def make(name):
            def f(e):
                for waits, fns, inc in self.prog[name]:
                    for k, v in waits:
                        e.wait_ge(self.sem[k], v)
                    ins = None
                    for fn in fns:
                        ins = fn(e)
                    if inc is not None:
                        ins.then_inc(self.sem[inc[0]], inc[1])
            return f
        with self.nc.Block() as block:
            block.sync(make("sp"))
            block.scalar(make("act"))
            block.vector(make("dve"))
            block.gpsimd(make("pool"))
            block.tensor(make("pe"))


def build_program():
    nc = bass.Bass("TRN2", target_bir_lowering=False)

    def din(name, shape, dt=F32):
        return nc.dram_tensor(name, list(shape), dt, kind="ExternalInput").ap()

    def dout(name, shape, dt=F32):
        return nc.dram_tensor(name, list(shape), dt, kind="ExternalOutput").ap()

    xp, xl, xs = din("xp", [2048, D]), din("xl", [2048, D]), din("xs", [256, D])
    flag_d = din("flag", [128, 2])
    sca, slc, slh = din("sca", [8, 512]), din("slc", [12, 512]), din("slh", [4, 512])
    w_in, w_out = din("w_in", [D, 2560]), din("w_out", [D, D])
    w_up, w_down = din("w_up", [D, 4096]), din("w_down", [4096, D])
    g1_d, g2_d, gf_d = din("g1", [D]), din("g2", [D]), din("gf", [D])
    caw_d, lcw_d, lcb_d = din("caw", [3, 512]), din("lcw", [4, 512]), din("lcb", [512])
    wa_d, wx_d = din("wa", [8, 64, 64]), din("wx", [8, 64, 64])
    ba_d, bx_d, apar_d = din("ba", [512]), din("bx", [512]), din("apar", [512])
    yp, ys = dout("yp", [2048, D]), dout("ys", [256, D])
    stp_d = dout("stp", [6, 512])
    sts_d = dout("sts", [24, 512])
    wscr = nc.dram_tensor("wscr", [NCHUNK, 128, 1024], BF16).ap()

    with ExitStack() as es:
        S = Sched(nc, es)
        S.dma_pool("ring", R)
        S.dma_pool("x", 8)
        S.dma_pool("cv", 12)
        S.dma_pool("par", 8)
        S.dma_pool("st", 8)
        S.dma_pool("sto", 2)

        def sb(name, shape, dt=F32):
            return es.enter_context(nc.sbuf_tensor("sb_" + name, list(shape), dt))

        ring = sb("ring", [128, R * 1024], BF16)
        ring_b = [Buf("ring%d" % i) for i in range(R)]
        xsl = sb("xsl", [128, 8 * D])
        xsl_b = [Buf("xs%d" % i) for i in range(8)]
        hb = sb("hb", [128, 2 * D], BF16)
        hb_b = [Buf("hb0"), Buf("hb1")]
        hTs = [sb("hTa", [128, 8 * 512], BF16), sb("hTb", [128, 8 * 512], BF16)]
        hT_bs = [Buf("hTa"), Buf("hTb")]
        mixT = sb("mixT", [128, 8 * 512], BF16)
        mixT_bs = [Buf("mixT%d" % i) for i in range(8)]
        uT = sb("uT", [128, 32 * 512], BF16)
        uT_b = [Buf("uT%d" % i) for i in range(32)]
        rl = sb("rl", [128, 2 * 512], BF16)
        rl_b = [Buf("rl0"), Buf("rl1")]
        tnames = ["tA", "ya", "xc", "r", "i", "a", "om", "ixc", "h", "x2", "gl"]
        T, TB = {}, {}
        rig = [sb("t_rig%d" % q, [128, 3 * 512]) for q in range(2)]
        for n in tnames:
            cnt_ = 1 if n in ("tA", "ya") else 2
            if n in ("r", "i", "x2"):
                o_ = ("r", "i", "x2").index(n) * 512
                T[n] = [rig[q][:, o_:o_ + 512] for q in range(2)]
            else:
                T[n] = [sb("t_%s%d" % (n, q), [128, 512]) for q in range(cnt_)]
            TB[n] = [Buf("t_%s%d" % (n, q)) for q in range(cnt_)]
            if cnt_ == 1:
                T[n].append(T[n][0])
                TB[n].append(TB[n][0])
        Tgl = T["gl"] + [sb("t_gl2", [128, 512])]
        TBgl = TB["gl"] + [Buf("t_gl2")]
        Txc = [sb("t_xc_c%d" % c_, [128, 512]) for c_ in range(2, 4)]
        TBxc = [Buf("t_xc_c%d" % c_) for c_ in range(2, 4)]
        xcbs = [sb("xcb%d" % q, [128, 512], BF16) for q in range(2)]
        xcb_bs = [Buf("xcb0"), Buf("xcb1")]
        pbuf_p = sb("pbuf_p", [128, 4 * 514]); pbuf_s = sb("pbuf_s", [128, 4 * 4 * 66])
        xbuf_p = sb("xbuf_p", [128, 4 * 515]); xbuf_s = sb("xbuf_s", [128, 4 * 4 * 67])
        pb_p_b = [Buf("pbp%d" % c) for c in range(4)]; pb_s_b = [Buf("pbs%d" % c) for c in range(4)]
        xb_p_b = [Buf("xbp%d" % c) for c in range(4)]; xb_s_b = [Buf("xbs%d" % c) for c in range(4)]
        h0_p = sb("h0_p", [128, 4]); h0_s = sb("h0_s", [128, 16])
        h0_p_b = [Buf("h0p%d" % c) for c in range(4)]; h0_s_b = [Buf("h0s%d" % c) for c in range(4)]
        identb = sb("identb", [128, 128], BF16); identf = sb("identf", [128, 128])
        ident_b = Buf("ident")
        gfb = sb("gfb", [128, D]); gfb_b = Buf("gfb")
        wbd = sb("wbd", [128, 8 * 128], BF16)
        wbd_b = Buf("wbd")
        caw = sb("caw", [128, 12]); lcw = sb("lcw", [128, 16]); lcb = sb("lcb", [128, 4])
        nba = sb("nba", [128, 4]); nbx = sb("nbx", [128, 4]); apar = sb("apar", [128, 4])
        nsp = sb("nsp", [128, 4]); nsp2 = sb("nsp2", [128, 4]); spt = sb("spt", [128, 16])
        g1t = sb("g1t", [128, 8]); g2t = sb("g2t", [128, 8]); flag = sb("flagt", [128, 2])
        cst = sb("cst", [128, 4])
        par_b = Buf("params")
        stat = sb("stat", [128, 8 * 4])
        stat_b = [Buf("stat%d" % i) for i in range(8)]
        stin = T["a"][0]; stin_b = TB["a"][0]
        stT = sb("stT", [128, 4 * 24]); stT_b = Buf("stT")
        soT_p = sb("soT_p", [128, 4 * 6]); soT_s = sb("soT_s", [128, 4 * 24])
        soT_p_b = Buf("soT_p"); soT_s_b = Buf("soT_s")
        so_p = T["r"][1]; so_s = T["i"][1]
        so_p_b = TB["r"][1]; so_s_b = TB["i"][1]
        ps = es.enter_context(nc.psum_tensor("ps", [128, 8 * 512], F32))
        bank_b = [Buf("bank%d" % i) for i in range(8)]
        st = {"bank": 0, "ring": 0, "stat": 0, "hb": 0, "rl": 0, "xs": 0}
        scr_b = [Buf("scr%d" % i) for i in range(NCHUNK)]
        out_events = []

        allowed = {"banks": list(range(8))}

        bank_age = [0] * 8

        def nextbank():
            a_ = allowed["banks"]
            b = a_[st["bank"] % len(a_)]
            st["bank"] += 1
            st["alloc"] = st.get("alloc", 0) + 1
            bank_age[b] = st["alloc"]
            return b

        def bank_ap(b, n=512):
            return ps[:, b * 512:b * 512 + n]

        def v3(ap, s):
            return ap.rearrange("p (s l) -> p s l", s=s)

        def act(out, in_, func, reads, writes, **kw):
            S.op("act", lambda e: e.activation(out=out, in_=in_, func=func, **kw), reads, writes)

        def tt(eng, out, in0, in1, op, reads, writes):
            S.op(eng, lambda e: e.tensor_tensor(out=out, in0=in0, in1=in1, op=op), reads, writes)

        def ts(eng, out, in0, s1, s2, op0, op1, reads, writes):
            if s2 is None:
                S.op(eng, lambda e: e.tensor_scalar(out=out, in0=in0, scalar1=s1, scalar2=None, op0=op0), reads, writes)
            else:
                S.op(eng, lambda e: e.tensor_scalar(out=out, in0=in0, scalar1=s1, scalar2=s2, op0=op0, op1=op1), reads, writes)

        def stt(out, in0, scalar, in1, op0, op1, reads, writes):
            S.op("dve", lambda e: e.scalar_tensor_tensor(out=out, in0=in0, scalar=scalar, in1=in1, op0=op0, op1=op1), reads, writes)

        def cp(eng, out, in_, reads, writes):
            if eng == "act":
                S.op("act", lambda e: e.copy(out=out, in_=in_), reads, writes)
            else:
                S.op(eng, lambda e: e.tensor_copy(out=out, in_=in_), reads, writes)

        def mset(eng, ap, val, writes):
            S.op(eng, lambda e: e.memset(ap, val), (), writes)

        mset("pool", identf[:], 0.0, [ident_b])
        S.op("pool", lambda e: e.affine_select(out=identf[:], in_=identf[:], compare_op=ALU.not_equal, fill=1.0,
                                               base=0, pattern=[[-1, 128]], channel_multiplier=1), [ident_b], [ident_b])
        cp("pool", identb[:], identf[:], [ident_b], [ident_b])
        w_in_v = w_in.rearrange("(k p) o -> p k o", p=128)
        w_up_v = w_up.rearrange("(k p) o -> p k o", p=128)
        w_out_v = w_out.rearrange("(k p) o -> p k o", p=128)
        w_down_v = w_down.rearrange("(k p) o -> p k o", p=128)
        conv_list = []
        for oc in [12, 13, 14, 15] + [o for o in OC_ORDER if o not in (12, 13, 14, 15)]:
            conv_list.append((POS_OF_OC[oc], "col", w_in_v[:, :, oc * 128:(oc + 1) * 128]))
        for kc in range(8):
            conv_list.append((CH_OUT + kc, "row", w_out_v[:, kc, :]))
        for fc in range(32):
            conv_list.append((CH_UP + fc, "col", w_up_v[:, :, fc * 128:(fc + 1) * 128]))
        for kc in range(32):
            conv_list.append((CH_DOWN + kc, "row", w_down_v[:, kc, :]))
        cvp = {"i": 0}

        def convert_some(n):
            for _ in range(n):
                if cvp["i"] >= len(conv_list):
                    return
                chunk, kind, src = conv_list[cvp["i"]]
                cvp["i"] += 1
                dst = wscr[chunk].rearrange("p (k m) -> p k m", k=8) if kind == "col" else wscr[chunk]
                S.dma("pool", dst, src, [], [scr_b[chunk]], "cv")

        convert_some(12)

        cst_b = Buf("cst")
        mset("pool", cst[:, 0:1], 1.0, [cst_b])
        mset("pool", cst[:, 1:2], EPS, [cst_b])
        wbdf_t = [(T["r"][0], TB["r"][0]), (T["i"][0], TB["i"][0])]
        for t_, b_ in wbdf_t:
            mset("pool", t_[:], 0.0, [b_])
        for t_, bl in ((pbuf_p, pb_p_b), (xbuf_p, xb_p_b), (h0_p, h0_p_b)):
            mset("pool", t_[:], 0.0, bl)

        def pdma(out, in_, writes, **kw):
            S.dma("act", out, in_, [], writes, "par", allow_slow_non_contiguous=True, **kw)

        flag_b = Buf("flag")
        pdma(flag[:], flag_d, [flag_b])
        pst, pst_b = T["om"][0], TB["om"][0]
        rows = [(0, 8, g1_d.rearrange("(c p) -> c p", p=128)), (8, 8, g2_d.rearrange("(c p) -> c p", p=128)),
                (16, 16, lcw_d.rearrange("k (c p) -> (k c) p", p=128)), (32, 4, lcb_d.rearrange("(c p) -> c p", p=128)),
                (36, 12, caw_d.rearrange("k (c p) -> (k c) p", p=128)), (48, 4, ba_d.rearrange("(c p) -> c p", p=128)),
                (52, 4, bx_d.rearrange("(c p) -> c p", p=128)), (56, 4, apar_d.rearrange("(c p) -> c p", p=128))]
        pst_bs = [Buf("pst%d" % i) for i in range(len(rows))]
        for (r0, nr_, src), b_ in zip(rows, pst_bs):
            pdma(pst[r0:r0 + nr_, 0:128], src, [b_])
        bpar = nextbank()
        S.op("pe", lambda e: e.transpose(out=bank_ap(bpar, 60), in_=pst[0:60, 0:128], identity=identf[0:60, 0:60]),
             pst_bs + [ident_b], [bank_b[bpar], pst_b])
        for t_, c0, n_ in ((g1t, 0, 8), (g2t, 8, 8), (lcw, 16, 16), (lcb, 32, 4), (caw, 36, 12), (nba, 48, 4), (nbx, 52, 4),
                           (apar, 56, 4)):
            cp("dve", t_[:], ps[:, bpar * 512 + c0:bpar * 512 + c0 + n_], [bank_b[bpar]], [par_b])
        pdma(gfb[:], gf_d.partition_broadcast(128), [gfb_b])
        wb_bs = [[Buf("wbdf%d%d" % (gi, par)) for par in range(2)] for gi in range(2)]
        for gi, wd in enumerate((wa_d, wx_d)):
            wbdf3 = wbdf_t[gi][0][:].rearrange("p (g j) -> p g j", j=128)
            for par in range(2):
                src = wd.rearrange("(c two) i j -> two i c j", two=2)[par]
                S.dma("act", wbdf3[par * 64:(par + 1) * 64, :, par * 64:(par + 1) * 64], src, [wbdf_t[gi][1]], [wb_bs[gi][par]],
                      "par", allow_slow_non_contiguous=True)
        stin_bs = [Buf("stin%d" % i) for i in range(3)]
        pdma(stin[0:8, :], sca, [stin_bs[0]])
        pdma(stin[8:20, :], slc, [stin_bs[1]])
        pdma(stin[20:24, :], slh, [stin_bs[2]])
        for gi in range(2):
            cp("pool", wbd[:, gi * 512:(gi + 1) * 512], wbdf_t[gi][0][:], [wbdf_t[gi][1]] + wb_bs[gi], [wbd_b])
        ts("pool", nba[:], nba[:], -1.0, None, ALU.mult, None, [par_b], [par_b])
        ts("pool", nbx[:], nbx[:], -1.0, None, ALU.mult, None, [par_b], [par_b])
        sp_abs, sp_e, sp_u, sp_y = spt[:, 0:4], spt[:, 4:8], spt[:, 8:12], spt[:, 12:16]
        ts("dve", sp_e, apar[:], -1.0, None, ALU.mult, None, [par_b], [par_b])
        tt("dve", sp_abs, apar[:], sp_e, ALU.max, [par_b], [par_b])
        act(sp_e, sp_abs, AF.Exp, [par_b], [par_b], scale=-1.0)
        ts("dve", sp_u, sp_e, 1.0, None, ALU.add, None, [par_b], [par_b])
        act(sp_y, sp_u, AF.Ln, [par_b], [par_b])
        act(sp_abs, sp_y, AF.Exp, [par_b], [par_b], scale=-1.0)
        tt("dve", sp_abs, sp_abs, sp_u, ALU.mult, [par_b], [par_b])
        tt("dve", sp_y, sp_y, sp_abs, ALU.add, [par_b], [par_b])
        ts("dve", sp_y, sp_y, -1.0, None, ALU.add, None, [par_b], [par_b])
        ts("dve", sp_e, apar[:], 0.0, None, ALU.max, None, [par_b], [par_b])
        tt("dve", sp_y, sp_y, sp_e, ALU.add, [par_b], [par_b])
        ts("dve", nsp[:], sp_y, -8.0, None, ALU.mult, None, [par_b], [par_b])
        ts("dve", nsp2[:], sp_y, -16.0, None, ALU.mult, None, [par_b, cst_b, flag_b], [par_b])

        stT3 = stT[:].rearrange("p (c r) -> p c r", r=24)
        for c in range(4):
            b = nextbank()
            S.op("pe", lambda e, c=c, b=b: e.transpose(out=bank_ap(b, 24), in_=stin[0:24, c * 128:(c + 1) * 128],
                                                      identity=identf[0:24, 0:24]), stin_bs + [ident_b], [bank_b[b], stin_b])
            cp("dve", stT3[:, c, :], bank_ap(b, 24), [bank_b[b]], [stT_b])
        pbs4 = pbuf_s[:].rearrange("p (c s l) -> p c s l", c=4, s=4)
        xbs4 = xbuf_s[:].rearrange("p (c s l) -> p c s l", c=4, s=4)
        for c in range(4):
            cp("pool", pbs4[:, c, :, 0:2], stT3[:, c, 0:8].rearrange("p (s k) -> p s k", k=2), [stT_b], [pb_s_b[c]])
            cp("pool", xbs4[:, c, :, 0:3], stT3[:, c, 8:20].rearrange("p (s k) -> p s k", k=3), [stT_b], [xb_s_b[c]])
            cp("pool", h0_s[:, c * 4:(c + 1) * 4], stT3[:, c, 20:24], [stT_b], [h0_s_b[c]])

        def ring_load(chunk):
            s_ = st["ring"] % R
            st["ring"] += 1
            S.dma("sp", ring[:, s_ * 1024:(s_ + 1) * 1024], wscr[chunk], [scr_b[chunk]], [ring_b[s_]], "ring")
            return ring[:, s_ * 1024:(s_ + 1) * 1024], ring_b[s_]

        def load_x(src, row0, J):
            res = []
            for j in range(J):
                k = st["xs"] % 8
                st["xs"] += 1
                ap = xsl[:, k * D:(k + 1) * D]
                S.dma("sp", ap, src[row0 + j * 128:row0 + (j + 1) * 128, :], [], [xsl_b[k]], "x")
                res.append((ap, xsl_b[k]))
            return res

        def rstd_of(x_ap, x_buf):
            k = st["stat"] % 8
            st["stat"] += 1
            sbf = stat_b[k]
            ss, ln, rs = stat[:, 4 * k:4 * k + 1], stat[:, 4 * k + 1:4 * k + 2], stat[:, 4 * k + 2:4 * k + 3]
            kh = st["hb"] % 2
            st["hb"] += 1
            act(hb[:, kh * D:(kh + 1) * D], x_ap, AF.Square, [x_buf], [sbf, hb_b[kh]], accum_out=ss)
            act(ln, ss, AF.Ln, [sbf, par_b], [sbf], scale=1.0 / D, bias=cst[:, 1:2])
            act(rs, ln, AF.Exp, [sbf], [sbf], scale=-0.5)
            return rs, sbf, kh

        def norm_front(xs_list, j):
            x_ap, x_buf = xs_list[j]
            rs, sbf, k = rstd_of(x_ap, x_buf)
            hb_ap = hb[:, k * D:(k + 1) * D]
            ts("dve", hb_ap, x_ap, rs, None, ALU.mult, None, [x_buf, sbf], [hb_b[k]])
            return k

        def norm_back(k, gt, hi, j):
            hT3 = hTs[hi][:].rearrange("p (k t) -> p k t", k=8)
            gbc = gt[:, :, None].to_broadcast([128, 8, 128])
            hb_ap = hb[:, k * D:(k + 1) * D]
            b = nextbank()
            pb = bank_ap(b).bitcast(BF16)
            S.group("pe", [lambda e, kc=kc, hb_ap=hb_ap, pb=pb: e.transpose(
                out=pb[:, kc * 128:(kc + 1) * 128], in_=hb_ap[:, kc * 128:(kc + 1) * 128], identity=identb[:])
                for kc in range(8)], [hb_b[k], ident_b], [bank_b[b]])
            tt("dve", hT3[:, :, j * 128:(j + 1) * 128], pb.rearrange("p (k t) -> p k t", k=8), gbc, ALU.mult,
               [bank_b[b], par_b], [hT_bs[hi]])

        def norm_sub(xs_list, gt, hi, j):
            norm_back(norm_front(xs_list, j), gt, hi, j)

        def norm_T(xs_list, gt, hi):
            for j in range(len(xs_list)):
                norm_sub(xs_list, gt, hi, j)

        def proj(oc, N, hi, col0=0):
            slot, sbuf_ = ring_load(POS_OF_OC[oc])
            b = nextbank()
            S.group("pe", [lambda e, kc=kc, slot=slot, b=b: e.matmul(
                out=bank_ap(b, N), lhsT=slot[:, kc * 128:(kc + 1) * 128],
                rhs=hTs[hi][:, kc * 512 + col0:kc * 512 + col0 + N], start=(kc == 0), stop=(kc == 7))
                for kc in range(8)], [sbuf_, hT_bs[hi]], [bank_b[b]])
            return b

        def sigmoid_chain(buf_ap, bufs):
            act(buf_ap, buf_ap, AF.Ln, bufs + [par_b], bufs, bias=cst[:, 0:1])
            act(buf_ap, buf_ap, AF.Exp, bufs, bufs, scale=-1.0)

        class BCtx:
            pass

        def mixer_b_p1(c, N, Sq, L, xb_t, xb_bufs, h0_t, h0_bufs, lean, fix, so3, so_buf, hi):
            k = BCtx()
            k.c, k.N, k.Sq, k.L, k.lean, k.fix, k.so3, k.so_buf = c, N, Sq, L, lean, fix, so3, so_buf
            k.h0_t, k.h0_bufs = h0_t, h0_bufs
            q = c % 2
            k.T = {n: T[n][q] for n in tnames}
            k.TB = {n: TB[n][q] for n in tnames}
            if c >= 2:
                k.T["xc"], k.TB["xc"] = Txc[c - 2], TBxc[c - 2]
            k.T["gl"], k.TB["gl"] = Tgl[c % 3], TBgl[c % 3]
            k.xcb, k.xcb_b = xcbs[q], xcb_bs[q]
            W = L + 3
            k.xb3 = xb_t[:, c * Sq * W:(c + 1) * Sq * W].rearrange("p (s l) -> p s l", s=Sq)
            k.xbb = xb_bufs[c]
            T_, TB_, xb3, xbb = k.T, k.TB, k.xb3, k.xbb
            bL = proj(12 + c, N, hi)
            k.bG = None if lean else proj(16 + c, N, hi)
            cp("dve", xb3[:, :, 3:3 + L], v3(bank_ap(bL, N), Sq), [bank_b[bL]], [xbb])
            if not lean:
                cp("act", T_["gl"][:, 0:N], bank_ap(k.bG, N), [bank_b[k.bG]], [TB_["gl"]])
                act(T_["x2"][:, 0:N], T_["gl"][:, 0:N], AF.Square, [TB_["gl"]], [TB_["x2"]], scale=GK1 ** 0.5)
            xc3 = v3(T_["xc"][:, 0:N], Sq)
            ts("dve", xc3, xb3[:, :, 0:L], lcw[:, c:c + 1], lcb[:, c:c + 1], ALU.mult, ALU.add,
               [xbb, par_b], [TB_["xc"]])
            for kk in range(1, 4):
                stt(xc3, xb3[:, :, kk:kk + L], lcw[:, 4 * kk + c:4 * kk + c + 1], xc3, ALU.mult, ALU.add,
                    [xbb, par_b, TB_["xc"]], [TB_["xc"]])
            cp("dve", k.xcb[:, 0:N], T_["xc"][:, 0:N], [TB_["xc"]], [k.xcb_b])
            if not lean:
                stt(T_["x2"][:, 0:N], T_["x2"][:, 0:N], GK0, T_["gl"][:, 0:N], ALU.add, ALU.mult,
                    [TB_["x2"], TB_["gl"]], [TB_["x2"]])
            if so3 is not None:
                cp("pool", so3[:, c, 2 * Sq:5 * Sq].rearrange("p (s k) -> p s k", k=3), xb3[:, :, L:L + 3], [xbb], [so_buf])
            if Sq == 1:
                cp("pool", xb3[:, :, 0:3], xb3[:, :, L:L + 3], [xbb], [xbb])
            if fix == "mask":
                ts("dve", xb3[:, :, 0:3], xb3[:, :, 0:3], flag[:, 0:1], None, ALU.mult, None, [xbb, par_b], [xbb])
            return k

        def mixer_b_p2(k):
            c, N, Sq, L, T_, TB_ = k.c, k.N, k.Sq, k.L, k.T, k.TB
            bRa, bRi = nextbank(), nextbank()
            for gi, b in ((0, bRa), (1, bRi)):
                S.op("pe", lambda e, gi=gi, b=b: e.matmul(out=bank_ap(b, N), lhsT=wbd[:, (gi * 4 + c) * 128:(gi * 4 + c + 1) * 128],
                                                          rhs=k.xcb[:, 0:N], start=True, stop=True), [wbd_b, k.xcb_b], [bank_b[b]])
            r, i_, a, om, ixc = (T_[n][:, 0:N] for n in ("r", "i", "a", "om", "ixc"))
            q = c % 2
            nsig = 2 if k.lean else 3
            sig3 = rig[q][:].rearrange("p (s l) -> p s l", s=3)[:, 0:nsig, 0:N]
            sig_bufs = [TB_["r"], TB_["i"]] + ([] if k.lean else [TB_["x2"]])
            act(r, bank_ap(bRa, N), AF.Exp, [bank_b[bRa], par_b], [TB_["r"]], scale=-1.0, bias=nba[:, c:c + 1])
            act(i_, bank_ap(bRi, N), AF.Exp, [bank_b[bRi], par_b], [TB_["i"]], scale=-1.0, bias=nbx[:, c:c + 1])
            if not k.lean:
                x2, gl = T_["x2"][:, 0:N], T_["gl"][:, 0:N]
                act(x2, x2, AF.Exp, [TB_["x2"]], [TB_["x2"]], scale=-1.0)
            act(sig3, sig3, AF.Ln, sig_bufs + [par_b], sig_bufs, bias=cst[:, 0:1])
            act(sig3, sig3, AF.Exp, sig_bufs, sig_bufs, scale=-1.0)
            act(a, r, AF.Exp, [TB_["r"], par_b], [TB_["a"]], scale=nsp[:, c:c + 1])
            act(om, r, AF.Exp, [TB_["r"], par_b], [TB_["om"]], scale=nsp2[:, c:c + 1])
            if k.fix == "one":
                mset("pool", T_["om"][:, 0:1], 0.0, [TB_["om"]])
            elif k.fix == "flag":
                ts("dve", T_["om"][:, 0:1], T_["om"][:, 0:1], flag[:, 0:1], None, ALU.mult, None,
                   [TB_["om"], par_b], [TB_["om"]])
            tt("pool", ixc, i_, T_["xc"][:, 0:N], ALU.mult, [TB_["i"], TB_["xc"]], [TB_["ixc"]])
            act(om, om, AF.Ln, [TB_["om"], par_b], [TB_["om"]], scale=-1.0, bias=cst[:, 0:1])
            act(om, om, AF.Exp, [TB_["om"]], [TB_["om"]], scale=0.5)
            tt("pool", ixc, om, ixc, ALU.mult, [TB_["om"], TB_["ixc"]], [TB_["ixc"]])
            if not k.lean:
                tt("pool", gl, gl, x2, ALU.mult, [TB_["gl"], TB_["x2"]], [TB_["gl"]])

        def mixer_b_p3(k):
            c, N, Sq, L, T_, TB_, h0_t, h0_bufs = k.c, k.N, k.Sq, k.L, k.T, k.TB, k.h0_t, k.h0_bufs
            for s_ in range(Sq):
                S.op("dve", lambda e, s_=s_: e.tensor_tensor_scan(
                    out=T_["h"][:, s_ * L:(s_ + 1) * L], data0=T_["a"][:, s_ * L:(s_ + 1) * L],
                    data1=T_["ixc"][:, s_ * L:(s_ + 1) * L], initial=h0_t[:, c * Sq + s_:c * Sq + s_ + 1],
                    op0=ALU.mult, op1=ALU.add), [TB_["a"], TB_["ixc"], h0_bufs[c]], [TB_["h"]])
            h = T_["h"][:, 0:N]
            h3 = v3(h, Sq)
            cp("pool", h0_t[:, c * Sq:(c + 1) * Sq], h3[:, :, L - 1], [TB_["h"]], [h0_bufs[c]])
            if k.fix == "mask":
                ts("dve", h0_t[:, c:c + 1], h0_t[:, c:c + 1], flag[:, 0:1], None, ALU.mult, None,
                   [h0_bufs[c], par_b], [h0_bufs[c]])
            if k.so3 is not None:
                cp("pool", k.so3[:, c, 5 * Sq:6 * Sq], h3[:, :, L - 1], [TB_["h"]], [k.so_buf])
            if not k.lean:
                tt("dve", mixT[:, (4 + c) * 512:(4 + c) * 512 + N], h, T_["gl"][:, 0:N], ALU.mult,
                   [TB_["h"], TB_["gl"]], [mixT_bs[4 + c]])

        def mixers_b_gen(args, hooks=(), pre=None):
            def hook(i):
                if i < len(hooks) and hooks[i] is not None:
                    hooks[i]()
            ks = {}
            ks[0] = mixer_b_p1(0, *args)
            if pre is not None:
                pre()
            yield 1
            ks[1] = mixer_b_p1(1, *args)
            hook(0)
            yield 2
            mixer_b_p2(ks[0])
            yield 3
            ks[2] = mixer_b_p1(2, *args)
            yield 4
            hook(1)
            mixer_b_p2(ks[1])
            mixer_b_p3(ks[0])
            yield 5
            ks[3] = mixer_b_p1(3, *args)
            hook(2)
            yield 6
            mixer_b_p2(ks[2])
            mixer_b_p3(ks[1])
            yield 7
            hook(3)
            mixer_b_p2(ks[3])
            mixer_b_p3(ks[2])
            hook(4)
            mixer_b_p3(ks[3])

        def mixers_b(args, hooks=(), pre=None):
            for _ in mixers_b_gen(args, hooks, pre):
                pass

        def mixer_a(c, N, Sq, L, pb_t, pb_bufs, so3, so_buf, hi):
            q = c % 2
            T_ = {n: T[n][q] for n in tnames}
            TB_ = {n: TB[n][q] for n in tnames}
            W = L + 2
            pb3 = pb_t[:, c * Sq * W:(c + 1) * Sq * W].rearrange("p (s l) -> p s l", s=Sq)
            pbb = pb_bufs[c]
            bX = proj(8 + c, N, hi)
            bC = proj(4 + c, N, hi)
            bB = proj(c, N, hi)
            tA, ya = T_["tA"][:, 0:N], T_["ya"][:, 0:N]
            cp("act", tA, bank_ap(bX, N), [bank_b[bX]], [TB_["tA"]])
            tt("dve", pb3[:, :, 2:2 + L], v3(bank_ap(bC, N), Sq), v3(tA, Sq), ALU.mult, [bank_b[bC], TB_["tA"]], [pbb])
            ya3 = v3(ya, Sq)
            ts("dve", ya3, pb3[:, :, 0:L], caw[:, c:c + 1], None, ALU.mult, None, [pbb, par_b], [TB_["ya"]])
            for k in (1, 2):
                stt(ya3, pb3[:, :, k:k + L], caw[:, 4 * k + c:4 * k + c + 1], ya3, ALU.mult, ALU.add,
                    [pbb, par_b, TB_["ya"]], [TB_["ya"]])
            tt("dve", mixT[:, c * 512:c * 512 + N], bank_ap(bB, N), ya, ALU.mult, [bank_b[bB], TB_["ya"]], [mixT_bs[c]])
            if so3 is not None:
                cp("pool", so3[:, c, 0:2 * Sq].rearrange("p (s k) -> p s k", k=2), pb3[:, :, L:L + 2], [pbb], [so_buf])
            if Sq == 1:
                cp("pool", pb3[:, :, 0:2], pb3[:, :, L:L + 2], [pbb], [pbb])

        def epilogue_parts(xs_list, ydst, yrow0):
            J = len(xs_list)
            keep = {}

            def front(j):
                x_ap, x_buf = xs_list[j]
                keep[j] = rstd_of(x_ap, x_buf)

            def back(j):
                x_ap, x_buf = xs_list[j]
                rs, sbf, _ = keep[j]
                stt(x_ap, x_ap, rs, gfb[:], ALU.mult, ALU.mult, [x_buf, sbf, gfb_b], [x_buf])
                ev = S.dma("pool", ydst[yrow0 + j * 128:yrow0 + (j + 1) * 128, :], x_ap, [x_buf], [], "st")
                out_events.append(ev)

            def stage(i):
                if i < J:
                    front(i)
                if 1 <= i <= J:
                    back(i - 1)
            return [(lambda i=i: stage(i)) for i in range(5)]

        def make_mix_gen(N, Sq, L, pb_t, pb_bufs, xb_t, xb_bufs, h0_t, h0_bufs, fix, so3, so_buf, prev_epi, extra0=None,
                         defer_epi=False):
            pe_ = list(prev_epi) if prev_epi else [None] * 5
            if defer_epi:
                def h3():
                    for i in (0, 1, 2):
                        pe_[i]()
                    mixer_a(2, N, Sq, L, pb_t, pb_bufs, so3, so_buf, 0)

                def h4():
                    for i in (3, 4):
                        pe_[i]()
                    mixer_a(3, N, Sq, L, pb_t, pb_bufs, so3, so_buf, 0)
                return mixers_b_gen((N, Sq, L, xb_t, xb_bufs, h0_t, h0_bufs, False, fix, so3, so_buf, 0),
                                    [None, lambda: mixer_a(0, N, Sq, L, pb_t, pb_bufs, so3, so_buf, 0),
                                     lambda: mixer_a(1, N, Sq, L, pb_t, pb_bufs, so3, so_buf, 0), h3, h4], pre=None)

            def mk(i, fn2):
                def f():
                    if extra0 is not None:
                        extra0()
                    if pe_[i] is not None:
                        pe_[i]()
                    if fn2 is not None:
                        fn2()
                return f
            return mixers_b_gen((N, Sq, L, xb_t, xb_bufs, h0_t, h0_bufs, False, fix, so3, so_buf, 0),
                                [mk(1, None), mk(2, None), mk(3, None),
                                 mk(4, lambda: mixer_a(0, N, Sq, L, pb_t, pb_bufs, so3, so_buf, 0)),
                                 lambda: mixer_a(1, N, Sq, L, pb_t, pb_bufs, so3, so_buf, 0)], pre=pe_[0])

        def main_tile(xs_list, N, Sq, L, ydst, yrow0, pb_t, pb_bufs, xb_t, xb_bufs, h0_t, h0_bufs, fix, so3, so_buf,
                      mid_hook, start_hook, prev_epi, extra0=None, cur_gen=None, next_factory=None, after_mix=None):
            J = N // 128
            my_epi = epilogue_parts(xs_list, ydst, yrow0)
            was_prerun = cur_gen is not None
            if cur_gen is None:
                cur_gen = make_mix_gen(N, Sq, L, pb_t, pb_bufs, xb_t, xb_bufs, h0_t, h0_bufs, fix, so3, so_buf, prev_epi, extra0)
            for _ in cur_gen:
                pass
            wslots = [ring_load(CH_OUT + kc) for kc in range(8)]
            if not was_prerun:
                for c in (2, 3):
                    mixer_a(c, N, Sq, L, pb_t, pb_bufs, so3, so_buf, 0)
            if after_mix is not None:
                after_mix()
            P1K = (0, 1, 2, 4, 5, 6)
            for b in sorted(range(2 * J), key=lambda b_: bank_age[b_]):
                j, half = b // 2, b % 2
                S.group("pe", [lambda e, kc=kc, j=j, half=half, b=b: e.matmul(
                    out=bank_ap(b), lhsT=mixT[:, kc * 512 + j * 128:kc * 512 + (j + 1) * 128],
                    rhs=wslots[kc][0][:, half * 512:(half + 1) * 512], start=(kc == 0), stop=False)
                    for kc in P1K], [mixT_bs[kc] for kc in P1K] + [wslots[kc][1] for kc in P1K], [bank_b[b]])
            st["bank"] = 2 * J
            for j in range(J):
                for half in range(2):
                    b = 2 * j + half
                    S.group("pe", [lambda e, kc=kc, j=j, half=half, b=b: e.matmul(
                        out=bank_ap(b), lhsT=mixT[:, kc * 512 + j * 128:kc * 512 + (j + 1) * 128],
                        rhs=wslots[kc][0][:, half * 512:(half + 1) * 512], start=False, stop=(kc == 7))
                        for kc in (3, 7)], [mixT_bs[3], mixT_bs[7], wslots[3][1], wslots[7][1]], [bank_b[b]])
            for j in range(J):
                x_ap, x_buf = xs_list[j]
                tt("dve", x_ap, ps[:, 2 * j * 512:(2 * j + 2) * 512], x_ap, ALU.add,
                   [bank_b[2 * j], bank_b[2 * j + 1], x_buf], [x_buf])
            fr = {}
            for j in range(J):
                fr[j] = norm_front(xs_list, j)
                if j >= 1:
                    norm_back(fr[j - 1], g2t, 1, j - 1)
            norm_back(fr[J - 1], g2t, 1, J - 1)
            if start_hook is not None:
                start_hook()
            for fc in range(32):
                if mid_hook is not None and 8 <= fc <= 16 and fc % 2 == 0:
                    mid_hook((fc - 8) // 2)
                slot, sbuf_ = ring_load(CH_UP + fc)
                b = nextbank()
                S.group("pe", [lambda e, kc=kc, slot=slot, b=b: e.matmul(
                    out=bank_ap(b, N), lhsT=slot[:, kc * 128:(kc + 1) * 128], rhs=hTs[1][:, kc * 512:kc * 512 + N],
                    start=(kc == 0), stop=(kc == 7)) for kc in range(8)], [sbuf_, hT_bs[1]], [bank_b[b]])
                k = st["rl"] % 2
                st["rl"] += 1
                rl_ap = rl[:, k * 512:k * 512 + N]
                act(rl_ap, bank_ap(b, N), AF.Relu, [bank_b[b]], [rl_b[k]])
                tt("dve" if fc % 2 == 0 else "pool", uT[:, fc * 512:fc * 512 + N], rl_ap, rl_ap, ALU.mult, [rl_b[k]], [uT_b[fc]])
            def down_pass(js):
                for kc in range(32):
                    slot, sbuf_ = ring_load(CH_DOWN + kc)
                    fns = []
                    for j in js:
                        for half in range(2):
                            fns.append(lambda e, kc=kc, j=j, half=half, slot=slot: e.matmul(
                                out=bank_ap(2 * j + half), lhsT=uT[:, kc * 512 + j * 128:kc * 512 + (j + 1) * 128],
                                rhs=slot[:, half * 512:(half + 1) * 512], start=(kc == 0), stop=(kc == 31)))
                    S.group("pe", fns, [sbuf_, uT_b[kc]], [bank_b[2 * j + h_] for j in js for h_ in range(2)])
                    yield kc

            def down_adds(js):
                for j in js:
                    x_ap, x_buf = xs_list[j]
                    tt("dve", x_ap, ps[:, 2 * j * 512:(2 * j + 2) * 512], x_ap, ALU.add,
                       [bank_b[2 * j], bank_b[2 * j + 1], x_buf], [x_buf])

            next_gen = None
            if next_factory is not None and J == 4:
                next_gen = next_factory(my_epi)
                allowed["banks"] = [4, 5, 6, 7]
                st["bank"] = 0
                nsteps = 0
                for pi, js in enumerate(((0, 1), (2, 3))):
                    for kc in range(32):
                        slot, sbuf_ = ring_load(CH_DOWN + kc)
                        fns = []
                        for jj, j in enumerate(js):
                            for half in range(2):
                                fns.append(lambda e, kc=kc, j=j, jj=jj, half=half, slot=slot: e.matmul(
                                    out=bank_ap(2 * jj + half), lhsT=uT[:, kc * 512 + j * 128:kc * 512 + (j + 1) * 128],
                                    rhs=slot[:, half * 512:(half + 1) * 512], start=(kc == 0), stop=(kc == 31)))
                        S.group("pe", fns, [sbuf_, uT_b[kc]], [bank_b[i] for i in range(4)])
                        if kc in ((5, 13, 21) if pi == 0 else (8, 20)) and nsteps < 7:
                            next(next_gen)
                            nsteps += 1
                    for jj, j in enumerate(js):
                        x_ap, x_buf = xs_list[j]
                        tt("dve", x_ap, ps[:, 2 * jj * 512:(2 * jj + 2) * 512], x_ap, ALU.add,
                           [bank_b[2 * jj], bank_b[2 * jj + 1], x_buf], [x_buf])
                    if pi == 0:
                        for _ in range(2):
                            next(next_gen)
                            nsteps += 1
                allowed["banks"] = list(range(8))
            else:
                for _ in down_pass(tuple(range(J))):
                    pass
                st["bank"] = 2 * J
                down_adds(tuple(range(J)))
                if next_factory is not None:
                    next_gen = next_factory(my_epi)
            return my_epi, next_gen

        tiles = [("L", xl, lt * 512, 4) for lt in range(4)] + [("M", xp, t * 512, 4) for t in range(4)] + [("M", xs, 0, 2)]
        xs_of = {}

        def load_tile(i):
            if i < len(tiles):
                xs_of[i] = load_x(tiles[i][1], tiles[i][2], tiles[i][3])

        load_tile(0)
        load_tile(1)
        norm_T(xs_of[0], g1t, 0)
        for lt in range(4):
            load_tile(lt + 2)
            hi = lt % 2
            nx, nh = xs_of[lt + 1], (lt + 1) % 2
            kk = {}

            def nstage(i, nx=nx, nh=nh, kk=kk):
                convert_some(3 if cvp["i"] < 60 else 0)
                if i < 4:
                    kk[i] = norm_front(nx, i)
                if 1 <= i <= 4:
                    norm_back(kk[i - 1], g1t, nh, i - 1)
            mixers_b((512, 1, 512, xbuf_p, xb_p_b, h0_p, h0_p_b, True, "one" if lt == 0 else ("mask" if lt == 3 else None), None, None, hi),
                     [(lambda i=i, ns=nstage: ns(i)) for i in range(1, 5)], pre=lambda ns=nstage: ns(0))
            if lt == 3:
                for c in range(4):
                    q = c % 2
                    bC, bX = proj(4 + c, 2, hi, 510), proj(8 + c, 2, hi, 510)
                    cp("act", T["tA"][q][:, 0:2], bank_ap(bX, 2), [bank_b[bX]], [TB["tA"][q]])
                    tt("dve", pbuf_p[:, c * 514:c * 514 + 2], bank_ap(bC, 2), T["tA"][q][:, 0:2], ALU.mult,
                       [bank_b[bC], TB["tA"][q]], [pb_p_b[c]])
        for c in range(4):
            ts("dve", pbuf_p[:, c * 514:c * 514 + 2], pbuf_p[:, c * 514:c * 514 + 2], flag[:, 0:1], None, ALU.mult, None,
               [pb_p_b[c], par_b], [pb_p_b[c]])

        soT_p3 = soT_p[:].rearrange("p (c r) -> p c r", r=6)
        soT_s3 = soT_s[:].rearrange("p (c r) -> p c r", r=24)
        def tile_args(t):
            if t < 4:
                return (512, 1, 512, pbuf_p, pb_p_b, xbuf_p, xb_p_b, h0_p, h0_p_b, "flag" if t == 0 else None,
                        soT_p3 if t == 3 else None, soT_p_b)
            return (256, 4, 64, pbuf_s, pb_s_b, xbuf_s, xb_s_b, h0_s, h0_s_b, None, soT_s3, soT_s_b)

        def emit_state_outputs():
            for (soT3, sbuf_, so_t, so_buf2, nr, dst) in ((soT_p3, soT_p_b, so_p, so_p_b, 6, stp_d),
                                                         (soT_s3, soT_s_b, so_s, so_s_b, 24, sts_d)):
                for c in range(4):
                    b = nextbank()
                    S.op("pe", lambda e, c=c, b=b, soT3=soT3, nr=nr: e.transpose(
                        out=ps[0:nr, b * 512:b * 512 + 128], in_=soT3[:, c, :], identity=identf[:]),
                        [sbuf_, ident_b], [bank_b[b]])
                    cp("dve", so_t[0:nr, c * 128:(c + 1) * 128], ps[0:nr, b * 512:b * 512 + 128], [bank_b[b]], [so_buf2])
                out_events.append(S.dma("sp", dst, so_t[0:nr, :], [so_buf2], [], "sto"))

        epi, gen = None, None
        for t in range(5):
            ti = 4 + t
            mid = None
            if t < 4:
                kk2 = {}

                def mid(i, kk2=kk2, ti=ti):
                    nx = xs_of[ti + 1]
                    if i < len(nx):
                        kk2[i] = norm_front(nx, i)
                    if 1 <= i <= len(nx):
                        norm_back(kk2[i - 1], g1t, 0, i - 1)
            end = (lambda ti=ti: load_tile(ti + 1)) if 1 <= t < 4 else None
            N_, Sq_, L_, pbt, pbb_, xbt, xbb_, h0t, h0b, fix_, so3_, sob_ = tile_args(t)
            nf = None
            if t < 4:
                a_ = tile_args(t + 1)
                nf = (lambda epi_, a_=a_: make_mix_gen(a_[0], a_[1], a_[2], a_[3], a_[4], a_[5], a_[6], a_[7], a_[8], a_[9],
                                                       a_[10], a_[11], epi_, None, True))
            epi, gen = main_tile(xs_of[ti], N_, Sq_, L_, yp if t < 4 else ys, (t * 512) if t < 4 else 0,
                                 pbt, pbb_, xbt, xbb_, h0t, h0b, fix_, so3_, sob_, mid, end, epi,
                                 (lambda: convert_some(8)) if t == 0 else None, gen, nf,
                                 emit_state_outputs if t == 4 else None)
        for f_ in epi:
            f_()

        S.wait_events("sp", out_events)
        S.emit()
    return nc


_NC_CACHE = {}


def kernel(x_prompt, x_sample, state_conv_a, state_lru_conv, state_lru_h, norm1_g, w_in, conv_a_w, lru_conv_w,
           lru_conv_b, lru_wa, lru_ba, lru_wx, lru_bx, lru_a_param, w_out, norm2_g, w_up, w_down, norm_f_g):
    f = lambda a: np.ascontiguousarray(np.asarray(a, dtype=np.float32))
    x_prompt, x_sample = f(x_prompt), f(x_sample)
    if "nc" not in _NC_CACHE:
        _NC_CACHE["nc"] = build_program()
    nc = _NC_CACHE["nc"]
    shared = {
        "w_in": f(w_in[0]), "w_out": f(w_out[0]), "w_up": f(w_up[0]), "w_down": f(w_down[0]),
        "g1": f(norm1_g[0]), "g2": f(norm2_g[0]), "gf": f(norm_f_g),
        "caw": f(conv_a_w[0]), "lcw": f(lru_conv_w[0]), "lcb": f(lru_conv_b[0]),
        "wa": f(lru_wa[0]), "wx": f(lru_wx[0]), "ba": f(np.reshape(lru_ba[0], -1)), "bx": f(np.reshape(lru_bx[0], -1)),
        "apar": f(lru_a_param[0]),
    }
    sca, slc, slh = f(state_conv_a[0]), f(state_lru_conv[0]), f(state_lru_h[0])
    in_maps = []
    for c in range(NCORES):
        s, hf = c // 2, c % 2
        flag = np.zeros((128, 2), np.float32)
        flag[:, 0] = float(hf)
        flag[:, 1] = 1.0 - float(hf)
        m = dict(shared)
        m.update({
            "xp": f(x_prompt[s, hf * 2048:(hf + 1) * 2048]),
            "xl": f(x_prompt[s, 0:2048]),
            "xs": f(x_sample[4 * c:4 * c + 4].reshape(256, D)),
            "flag": flag,
            "sca": f(sca[4 * c:4 * c + 4].reshape(8, 512)),
            "slc": f(slc[4 * c:4 * c + 4].reshape(12, 512)),
            "slh": f(slh[4 * c:4 * c + 4].reshape(4, 512)),
        })
        in_maps.append(m)
    res = run_bass_kernel_spmd(nc, in_maps, core_ids=list(range(NCORES)))
    rs = res.results
    y_prompt = np.stack([np.concatenate([rs[2 * s]["yp"], rs[2 * s + 1]["yp"]], axis=0) for s in range(4)], axis=0)
    y_sample = np.concatenate([rs[c]["ys"].reshape(4, 64, D) for c in range(NCORES)], axis=0)
    stp = np.stack([rs[2 * s + 1]["stp"] for s in range(4)], axis=0)
    new_a_p = stp[:, 0:2][None]
    new_l_p = stp[:, 2:5][None]
    new_h_p = stp[:, 5][None]
    sts = np.stack([rs[c]["sts"] for c in range(NCORES)], axis=0)
    new_a_s = sts[:, 0:8].reshape(32, 2, 512)[None]
    new_l_s = sts[:, 8:20].reshape(32, 3, 512)[None]
    new_h_s = sts[:, 20:24].reshape(32, 512)[None]
    out = (y_prompt, y_sample, new_a_p, new_l_p, new_h_p, new_a_s, new_l_s, new_h_s)
    return tuple(np.ascontiguousarray(o, dtype=np.float32) for o in out)
```

```python
import numpy as np
from contextlib import ExitStack
import concourse.bass as bass
import concourse.mybir as mybir
from concourse.bass_utils import run_bass_kernel_spmd

F32, BF16 = mybir.dt.float32, mybir.dt.bfloat16
AF = mybir.ActivationFunctionType
ALU = mybir.AluOpType

NCORES = 8
D = 1024
EPS = 1e-6
R = 15
OC_ORDER = [12, 16, 13, 17, 14, 18, 15, 19, 0, 4, 8, 1, 5, 9, 2, 6, 10, 3, 7, 11]
POS_OF_OC = {oc: i for i, oc in enumerate(OC_ORDER)}
CH_OUT, CH_UP, CH_DOWN, NCHUNK = 20, 28, 60, 92
GK0 = 2.0 * 0.7978845608028654
GK1 = GK0 * 0.044715


class Buf:
    __slots__ = ("name", "w", "rs")

    def __init__(self, name):
        self.name = name
        self.w = None
        self.rs = {}


class Sched:
    ENG = ("pe", "act", "dve", "pool", "sp")

    def __init__(self, nc, es):
        self.nc, self.es = nc, es
        self.sem, self.cnt = {}, {}
        for e in self.ENG[:4]:
            self.sem[e] = es.enter_context(nc.semaphore("s_" + e))
            self.cnt[e] = 0
        self.seen = {e: {} for e in self.ENG}
        self.prog = {e: [] for e in self.ENG}
        self.pools = {}

    def _waits(self, eng, reads, writes, extra=()):
        need = {}

        def add(ev):
            if ev is not None and need.get(ev[0], 0) < ev[1]:
                need[ev[0]] = ev[1]
        for b in reads:
            add(b.w)
        for b in writes:
            add(b.w)
            for k, v in b.rs.items():
                add((k, v))
        for ev in extra:
            add(ev)
        out, seen = [], self.seen[eng]
        for k, v in need.items():
            if k == eng and eng == "pe":
                continue
            if seen.get(k, 0) >= v:
                continue
            seen[k] = v
            out.append((k, v))
        return out

    def _commit(self, ev, reads, writes):
        for b in writes:
            b.w = ev
            b.rs = {}
        for b in reads:
            if b.rs.get(ev[0], 0) < ev[1]:
                b.rs[ev[0]] = ev[1]

    def group(self, eng, fns, reads=(), writes=()):
        waits = self._waits(eng, reads, writes)
        self.cnt[eng] += 1
        ev = (eng, self.cnt[eng])
        self.prog[eng].append((waits, list(fns), (eng, 1)))
        self._commit(ev, reads, writes)
        return ev

    def op(self, eng, fn, reads=(), writes=()):
        return self.group(eng, [fn], reads, writes)

    def dma_pool(self, name, n):
        keys = []
        for i in range(n):
            k = "%s%d" % (name, i)
            self.sem[k] = self.es.enter_context(self.nc.semaphore("d_" + k))
            self.cnt[k] = 0
            keys.append(k)
        self.pools[name] = [keys, 0]

    def dma(self, q, out, in_, reads, writes, pool, **kw):
        p = self.pools[pool]
        k = p[0][p[1] % len(p[0])]
        p[1] += 1
        prev = self.cnt[k]
        waits = self._waits(q, reads, writes, [(k, prev)] if prev else [])
        self.cnt[k] = prev + 16
        ev = (k, prev + 16)
        self.prog[q].append((waits, [lambda e: e.dma_start(out=out, in_=in_, **kw)], (k, 16)))
        self._commit(ev, reads, writes)
        return ev

    def wait_events(self, eng, evs):
        waits = self._waits(eng, (), (), evs)
        self.prog[eng].append((waits, [], None))

    def check_no_deadlock(self):
        val = {k: 0 for k in self.sem}
        pos = {e: 0 for e in self.ENG}
        while True:
            prog = False
            for e in self.ENG:
                q = self.prog[e]
                while pos[e] < len(q):
                    waits, fns, inc = q[pos[e]]
                    if any(val[k] < v for k, v in waits):
                        break
                    if inc is not None:
                        val[inc[0]] += inc[1]
                    pos[e] += 1
                    prog = True
            if not prog:
                break
        stuck = {e: (pos[e], len(self.prog[e])) for e in self.ENG if pos[e] < len(self.prog[e])}
        if stuck:
            msg = []
            for e, (p, n) in stuck.items():
                msg.append("%s at %d/%d waits %s" % (e, p, n, [(k, v, val[k]) for k, v in self.prog[e][p][0] if val[k] < v]))
            raise RuntimeError("static deadlock: " + "; ".join(msg))

    def emit(self):
        self.check_no_deadlock()
        def make(name):
            def f(e):
                for waits, fns, inc in self.prog[name]:
                    for k, v in waits:
                        e.wait_ge(self.sem[k], v)
                    ins = None
                    for fn in fns:
                        ins = fn(e)
                    if inc is not None:
                        ins.then_inc(self.sem[inc[0]], inc[1])
            return f
        with self.nc.Block() as block:
            block.sync(make("sp"))
            block.scalar(make("act"))
            block.vector(make("dve"))
            block.gpsimd(make("pool"))
            block.tensor(make("pe"))


def build_program():
    nc = bass.Bass("TRN2", target_bir_lowering=False)

    def din(name, shape, dt=F32):
        return nc.dram_tensor(name, list(shape), dt, kind="ExternalInput").ap()

    def dout(name, shape, dt=F32):
        return nc.dram_tensor(name, list(shape), dt, kind="ExternalOutput").ap()

    xp, xl, xs = din("xp", [2048, D]), din("xl", [2048, D]), din("xs", [256, D])
    flag_d = din("flag", [128, 2])
    sca, slc, slh = din("sca", [8, 512]), din("slc", [12, 512]), din("slh", [4, 512])
    w_in, w_out = din("w_in", [D, 2560]), din("w_out", [D, D])
    w_up, w_down = din("w_up", [D, 4096]), din("w_down", [4096, D])
    g1_d, g2_d, gf_d = din("g1", [D]), din("g2", [D]), din("gf", [D])
    caw_d, lcw_d, lcb_d = din("caw", [3, 512]), din("lcw", [4, 512]), din("lcb", [512])
    wa_d, wx_d = din("wa", [8, 64, 64]), din("wx", [8, 64, 64])
    ba_d, bx_d, apar_d = din("ba", [512]), din("bx", [512]), din("apar", [512])
    yp, ys = dout("yp", [2048, D]), dout("ys", [256, D])
    stp_d = dout("stp", [6, 512])
    sts_d = dout("sts", [24, 512])
    wscr = nc.dram_tensor("wscr", [NCHUNK, 128, 1024], BF16).ap()

    with ExitStack() as es:
        S = Sched(nc, es)
        S.dma_pool("ring", R)
        S.dma_pool("x", 8)
        S.dma_pool("cv", 40)
        S.dma_pool("par", 8)
        S.dma_pool("st", 8)
        S.dma_pool("sto", 2)

        def sb(name, shape, dt=F32):
            return es.enter_context(nc.sbuf_tensor("sb_" + name, list(shape), dt))

        ring = sb("ring", [128, R * 1024], BF16)
        ring_b = [Buf("ring%d" % i) for i in range(R)]
        xsl = sb("xsl", [128, 8 * D])
        xsl_b = [Buf("xs%d" % i) for i in range(8)]
        hb = sb("hb", [128, 2 * D], BF16)
        hb_b = [Buf("hb0"), Buf("hb1")]
        hTs = [sb("hTa", [128, 8 * 512], BF16), sb("hTb", [128, 8 * 512], BF16)]
        hT_bs = [Buf("hTa"), Buf("hTb")]
        mixT = sb("mixT", [128, 8 * 512], BF16)
        mixT_bs = [Buf("mixT%d" % i) for i in range(8)]
        uT = sb("uT", [128, 32 * 512], BF16)
        uT_b = [Buf("uT%d" % i) for i in range(32)]
        rl = sb("rl", [128, 2 * 512], BF16)
        rl_b = [Buf("rl0"), Buf("rl1")]
        tnames = ["tA", "ya", "xc", "r", "i", "a", "om", "ixc", "h", "x2", "gl"]
        T, TB = {}, {}
        rig = [sb("t_rig%d" % q, [128, 3 * 512]) for q in range(2)]
        for n in tnames:
            cnt_ = 1 if n in ("tA", "ya") else 2
            if n in ("r", "i", "x2"):
                o_ = ("r", "i", "x2").index(n) * 512
                T[n] = [rig[q][:, o_:o_ + 512] for q in range(2)]
            else:
                T[n] = [sb("t_%s%d" % (n, q), [128, 512]) for q in range(cnt_)]
            TB[n] = [Buf("t_%s%d" % (n, q)) for q in range(cnt_)]
            if cnt_ == 1:
                T[n].append(T[n][0])
                TB[n].append(TB[n][0])
        Tgl = T["gl"] + [sb("t_gl2", [128, 512])]
        TBgl = TB["gl"] + [Buf("t_gl2")]
        Txc = [sb("t_xc_c%d" % c_, [128, 512]) for c_ in range(2, 4)]
        TBxc = [Buf("t_xc_c%d" % c_) for c_ in range(2, 4)]
        xcbs = [sb("xcb%d" % q, [128, 512], BF16) for q in range(2)]
        xcb_bs = [Buf("xcb0"), Buf("xcb1")]
        pbuf_p = sb("pbuf_p", [128, 4 * 514]); pbuf_s = sb("pbuf_s", [128, 4 * 4 * 66])
        xbuf_p = sb("xbuf_p", [128, 4 * 515]); xbuf_s = sb("xbuf_s", [128, 4 * 4 * 67])
        pb_p_b = [Buf("pbp%d" % c) for c in range(4)]; pb_s_b = [Buf("pbs%d" % c) for c in range(4)]
        xb_p_b = [Buf("xbp%d" % c) for c in range(4)]; xb_s_b = [Buf("xbs%d" % c) for c in range(4)]
        h0_p = sb("h0_p", [128, 4]); h0_s = sb("h0_s", [128, 16])
        h0_p_b = [Buf("h0p%d" % c) for c in range(4)]; h0_s_b = [Buf("h0s%d" % c) for c in range(4)]
        identb = sb("identb", [128, 128], BF16); identf = sb("identf", [128, 128])
        ident_b = Buf("ident")
        gfb = sb("gfb", [128, D]); gfb_b = Buf("gfb")
        wbd = sb("wbd", [128, 8 * 128], BF16)
        wbd_b = Buf("wbd")
        caw = sb("caw", [128, 12]); lcw = sb("lcw", [128, 16]); lcb = sb("lcb", [128, 4])
        nba = sb("nba", [128, 4]); nbx = sb("nbx", [128, 4]); apar = sb("apar", [128, 4])
        nsp = sb("nsp", [128, 4]); nsp2 = sb("nsp2", [128, 4]); spt = sb("spt", [128, 16])
        g1t = sb("g1t", [128, 8]); g2t = sb("g2t", [128, 8]); flag = sb("flagt", [128, 2])
        cst = sb("cst", [128, 4])
        par_b = Buf("params")
        stat = sb("stat", [128, 8 * 4])
        stat_b = [Buf("stat%d" % i) for i in range(8)]
        stin = T["a"][0]; stin_b = TB["a"][0]
        stT = sb("stT", [128, 4 * 24]); stT_b = Buf("stT")
        soT_p = sb("soT_p", [128, 4 * 6]); soT_s = sb("soT_s", [128, 4 * 24])
        soT_p_b = Buf("soT_p"); soT_s_b = Buf("soT_s")
        so_p = T["r"][1]; so_s = T["i"][1]
        so_p_b = TB["r"][1]; so_s_b = TB["i"][1]
        ps = es.enter_context(nc.psum_tensor("ps", [128, 8 * 512], F32))
        bank_b = [Buf("bank%d" % i) for i in range(8)]
        st = {"bank": 0, "ring": 0, "stat": 0, "hb": 0, "rl": 0, "xs": 0}
        scr_b = [Buf("scr%d" % i) for i in range(NCHUNK)]
        out_events = []

        allowed = {"banks": list(range(8))}

        bank_age = [0] * 8

        def nextbank():
            a_ = allowed["banks"]
            b = a_[st["bank"] % len(a_)]
            st["bank"] += 1
            st["alloc"] = st.get("alloc", 0) + 1
            bank_age[b] = st["alloc"]
            return b

        def bank_ap(b, n=512):
            return ps[:, b * 512:b * 512 + n]

        def v3(ap, s):
            return ap.rearrange("p (s l) -> p s l", s=s)

        def act(out, in_, func, reads, writes, **kw):
            S.op("act", lambda e: e.activation(out=out, in_=in_, func=func, **kw), reads, writes)

        def tt(eng, out, in0, in1, op, reads, writes):
            S.op(eng, lambda e: e.tensor_tensor(out=out, in0=in0, in1=in1, op=op), reads, writes)

        def ts(eng, out, in0, s1, s2, op0, op1, reads, writes):
            if s2 is None:
                S.op(eng, lambda e: e.tensor_scalar(out=out, in0=in0, scalar1=s1, scalar2=None, op0=op0), reads, writes)
            else:
                S.op(eng, lambda e: e.tensor_scalar(out=out, in0=in0, scalar1=s1, scalar2=s2, op0=op0, op1=op1), reads, writes)

        def stt(out, in0, scalar, in1, op0, op1, reads, writes):
            S.op("dve", lambda e: e.scalar_tensor_tensor(out=out, in0=in0, scalar=scalar, in1=in1, op0=op0, op1=op1), reads, writes)

        def cp(eng, out, in_, reads, writes):
            if eng == "act":
                S.op("act", lambda e: e.copy(out=out, in_=in_), reads, writes)
            else:
                S.op(eng, lambda e: e.tensor_copy(out=out, in_=in_), reads, writes)

        def mset(eng, ap, val, writes):
            S.op(eng, lambda e: e.memset(ap, val), (), writes)

        mset("pool", identf[:], 0.0, [ident_b])
        S.op("pool", lambda e: e.affine_select(out=identf[:], in_=identf[:], compare_op=ALU.not_equal, fill=1.0,
                                               base=0, pattern=[[-1, 128]], channel_multiplier=1), [ident_b], [ident_b])
        cp("pool", identb[:], identf[:], [ident_b], [ident_b])
        w_in_v = w_in.rearrange("(k p) o -> p k o", p=128)
        w_up_v = w_up.rearrange("(k p) o -> p k o", p=128)
        w_out_v = w_out.rearrange("(k p) o -> p k o", p=128)
        w_down_v = w_down.rearrange("(k p) o -> p k o", p=128)
        conv_list = []
        for oc in [12, 13, 14, 15] + [o for o in OC_ORDER if o not in (12, 13, 14, 15)]:
            conv_list.append((POS_OF_OC[oc], "col", w_in_v[:, :, oc * 128:(oc + 1) * 128]))
        for kc in range(8):
            conv_list.append((CH_OUT + kc, "row", w_out_v[:, kc, :]))
        for fc in range(32):
            conv_list.append((CH_UP + fc, "col", w_up_v[:, :, fc * 128:(fc + 1) * 128]))
        for kc in range(32):
            conv_list.append((CH_DOWN + kc, "row", w_down_v[:, kc, :]))
        cvp = {"i": 0}

        def convert_some(n):
            for _ in range(n):
                if cvp["i"] >= len(conv_list):
                    return
                chunk, kind, src = conv_list[cvp["i"]]
                cvp["i"] += 1
                dst = wscr[chunk].rearrange("p (k m) -> p k m", k=8) if kind == "col" else wscr[chunk]
                S.dma("pool", dst, src, [], [scr_b[chunk]], "cv")

        convert_some(12)

        cst_b = Buf("cst")
        mset("pool", cst[:, 0:1], 1.0, [cst_b])
        mset("pool", cst[:, 1:2], EPS, [cst_b])
        wbdf_t = [(T["r"][0], TB["r"][0]), (T["i"][0], TB["i"][0])]
        for t_, b_ in wbdf_t:
            mset("pool", t_[:], 0.0, [b_])
        for t_, bl in ((pbuf_p, pb_p_b), (xbuf_p, xb_p_b), (h0_p, h0_p_b)):
            mset("pool", t_[:], 0.0, bl)

        def pdma(out, in_, writes, **kw):
            S.dma("act", out, in_, [], writes, "par", allow_slow_non_contiguous=True, **kw)

        flag_b = Buf("flag")
        pdma(flag[:], flag_d, [flag_b])
        pst, pst_b = T["om"][0], TB["om"][0]
        rows = [(0, 8, g1_d.rearrange("(c p) -> c p", p=128)), (8, 8, g2_d.rearrange("(c p) -> c p", p=128)),
                (16, 16, lcw_d.rearrange("k (c p) -> (k c) p", p=128)), (32, 4, lcb_d.rearrange("(c p) -> c p", p=128)),
                (36, 12, caw_d.rearrange("k (c p) -> (k c) p", p=128)), (48, 4, ba_d.rearrange("(c p) -> c p", p=128)),
                (52, 4, bx_d.rearrange("(c p) -> c p", p=128)), (56, 4, apar_d.rearrange("(c p) -> c p", p=128))]
        pst_bs = [Buf("pst%d" % i) for i in range(len(rows))]
        for (r0, nr_, src), b_ in zip(rows, pst_bs):
            pdma(pst[r0:r0 + nr_, 0:128], src, [b_])
        bpar = nextbank()
        S.op("pe", lambda e: e.transpose(out=bank_ap(bpar, 60), in_=pst[0:60, 0:128], identity=identf[0:60, 0:60]),
             pst_bs + [ident_b], [bank_b[bpar], pst_b])
        for t_, c0, n_ in ((g1t, 0, 8), (g2t, 8, 8), (lcw, 16, 16), (lcb, 32, 4), (caw, 36, 12), (nba, 48, 4), (nbx, 52, 4),
                           (apar, 56, 4)):
            cp("dve", t_[:], ps[:, bpar * 512 + c0:bpar * 512 + c0 + n_], [bank_b[bpar]], [par_b])
        pdma(gfb[:], gf_d.partition_broadcast(128), [gfb_b])
        wb_bs = [[Buf("wbdf%d%d" % (gi, par)) for par in range(2)] for gi in range(2)]
        for gi, wd in enumerate((wa_d, wx_d)):
            wbdf3 = wbdf_t[gi][0][:].rearrange("p (g j) -> p g j", j=128)
            for par in range(2):
                src = wd.rearrange("(c two) i j -> two i c j", two=2)[par]
                S.dma("act", wbdf3[par * 64:(par + 1) * 64, :, par * 64:(par + 1) * 64], src, [wbdf_t[gi][1]], [wb_bs[gi][par]],
                      "par", allow_slow_non_contiguous=True)
        stin_bs = [Buf("stin%d" % i) for i in range(3)]
        pdma(stin[0:8, :], sca, [stin_bs[0]])
        pdma(stin[8:20, :], slc, [stin_bs[1]])
        pdma(stin[20:24, :], slh, [stin_bs[2]])
        for gi in range(2):
            cp("pool", wbd[:, gi * 512:(gi + 1) * 512], wbdf_t[gi][0][:], [wbdf_t[gi][1]] + wb_bs[gi], [wbd_b])
        ts("pool", nba[:], nba[:], -1.0, None, ALU.mult, None, [par_b], [par_b])
        ts("pool", nbx[:], nbx[:], -1.0, None, ALU.mult, None, [par_b], [par_b])
        sp_abs, sp_e, sp_u, sp_y = spt[:, 0:4], spt[:, 4:8], spt[:, 8:12], spt[:, 12:16]
        ts("dve", sp_e, apar[:], -1.0, None, ALU.mult, None, [par_b], [par_b])
        tt("dve", sp_abs, apar[:], sp_e, ALU.max, [par_b], [par_b])
        act(sp_e, sp_abs, AF.Exp, [par_b], [par_b], scale=-1.0)
        ts("dve", sp_u, sp_e, 1.0, None, ALU.add, None, [par_b], [par_b])
        act(sp_y, sp_u, AF.Ln, [par_b], [par_b])
        act(sp_abs, sp_y, AF.Exp, [par_b], [par_b], scale=-1.0)
        tt("dve", sp_abs, sp_abs, sp_u, ALU.mult, [par_b], [par_b])
        tt("dve", sp_y, sp_y, sp_abs, ALU.add, [par_b], [par_b])
        ts("dve", sp_y, sp_y, -1.0, None, ALU.add, None, [par_b], [par_b])
        ts("dve", sp_e, apar[:], 0.0, None, ALU.max, None, [par_b], [par_b])
        tt("dve", sp_y, sp_y, sp_e, ALU.add, [par_b], [par_b])
        ts("dve", nsp[:], sp_y, -8.0, None, ALU.mult, None, [par_b], [par_b])
        ts("dve", nsp2[:], sp_y, -16.0, None, ALU.mult, None, [par_b, cst_b, flag_b], [par_b])

        stT3 = stT[:].rearrange("p (c r) -> p c r", r=24)
        for c in range(4):
            b = nextbank()
            S.op("pe", lambda e, c=c, b=b: e.transpose(out=bank_ap(b, 24), in_=stin[0:24, c * 128:(c + 1) * 128],
                                                      identity=identf[0:24, 0:24]), stin_bs + [ident_b], [bank_b[b], stin_b])
            cp("dve", stT3[:, c, :], bank_ap(b, 24), [bank_b[b]], [stT_b])
        pbs4 = pbuf_s[:].rearrange("p (c s l) -> p c s l", c=4, s=4)
        xbs4 = xbuf_s[:].rearrange("p (c s l) -> p c s l", c=4, s=4)
        for c in range(4):
            cp("pool", pbs4[:, c, :, 0:2], stT3[:, c, 0:8].rearrange("p (s k) -> p s k", k=2), [stT_b], [pb_s_b[c]])
            cp("pool", xbs4[:, c, :, 0:3], stT3[:, c, 8:20].rearrange("p (s k) -> p s k", k=3), [stT_b], [xb_s_b[c]])
            cp("pool", h0_s[:, c * 4:(c + 1) * 4], stT3[:, c, 20:24], [stT_b], [h0_s_b[c]])

        def ring_load(chunk):
            s_ = st["ring"] % R
            st["ring"] += 1
            S.dma("sp", ring[:, s_ * 1024:(s_ + 1) * 1024], wscr[chunk], [scr_b[chunk]], [ring_b[s_]], "ring")
            return ring[:, s_ * 1024:(s_ + 1) * 1024], ring_b[s_]

        def load_x(src, row0, J):
            res = []
            for j in range(J):
                k = st["xs"] % 8
                st["xs"] += 1
                ap = xsl[:, k * D:(k + 1) * D]
                S.dma("sp", ap, src[row0 + j * 128:row0 + (j + 1) * 128, :], [], [xsl_b[k]], "x")
                res.append((ap, xsl_b[k]))
            return res

        def rstd_of(x_ap, x_buf):
            k = st["stat"] % 8
            st["stat"] += 1
            sbf = stat_b[k]
            ss, ln, rs = stat[:, 4 * k:4 * k + 1], stat[:, 4 * k + 1:4 * k + 2], stat[:, 4 * k + 2:4 * k + 3]
            kh = st["hb"] % 2
            st["hb"] += 1
            act(hb[:, kh * D:(kh + 1) * D], x_ap, AF.Square, [x_buf], [sbf, hb_b[kh]], accum_out=ss)
            act(ln, ss, AF.Ln, [sbf, par_b], [sbf], scale=1.0 / D, bias=cst[:, 1:2])
            act(rs, ln, AF.Exp, [sbf], [sbf], scale=-0.5)
            return rs, sbf, kh

        def norm_front(xs_list, j):
            x_ap, x_buf = xs_list[j]
            rs, sbf, k = rstd_of(x_ap, x_buf)
            hb_ap = hb[:, k * D:(k + 1) * D]
            ts("dve", hb_ap, x_ap, rs, None, ALU.mult, None, [x_buf, sbf], [hb_b[k]])
            return k

        def norm_back(k, gt, hi, j):
            hT3 = hTs[hi][:].rearrange("p (k t) -> p k t", k=8)
            gbc = gt[:, :, None].to_broadcast([128, 8, 128])
            hb_ap = hb[:, k * D:(k + 1) * D]
            b = nextbank()
            pb = bank_ap(b).bitcast(BF16)
            S.group("pe", [lambda e, kc=kc, hb_ap=hb_ap, pb=pb: e.transpose(
                out=pb[:, kc * 128:(kc + 1) * 128], in_=hb_ap[:, kc * 128:(kc + 1) * 128], identity=identb[:])
                for kc in range(8)], [hb_b[k], ident_b], [bank_b[b]])
            tt("dve", hT3[:, :, j * 128:(j + 1) * 128], pb.rearrange("p (k t) -> p k t", k=8), gbc, ALU.mult,
               [bank_b[b], par_b], [hT_bs[hi]])

        def norm_sub(xs_list, gt, hi, j):
            norm_back(norm_front(xs_list, j), gt, hi, j)

        def norm_T(xs_list, gt, hi):
            for j in range(len(xs_list)):
                norm_sub(xs_list, gt, hi, j)

        def proj(oc, N, hi, col0=0):
            slot, sbuf_ = ring_load(POS_OF_OC[oc])
            b = nextbank()
            S.group("pe", [lambda e, kc=kc, slot=slot, b=b: e.matmul(
                out=bank_ap(b, N), lhsT=slot[:, kc * 128:(kc + 1) * 128],
                rhs=hTs[hi][:, kc * 512 + col0:kc * 512 + col0 + N], start=(kc == 0), stop=(kc == 7))
                for kc in range(8)], [sbuf_, hT_bs[hi]], [bank_b[b]])
            return b

        def sigmoid_chain(buf_ap, bufs):
            act(buf_ap, buf_ap, AF.Ln, bufs + [par_b], bufs, bias=cst[:, 0:1])
            act(buf_ap, buf_ap, AF.Exp, bufs, bufs, scale=-1.0)

        class BCtx:
            pass

        def mixer_b_p1(c, N, Sq, L, xb_t, xb_bufs, h0_t, h0_bufs, lean, fix, so3, so_buf, hi):
            k = BCtx()
            k.c, k.N, k.Sq, k.L, k.lean, k.fix, k.so3, k.so_buf = c, N, Sq, L, lean, fix, so3, so_buf
            k.h0_t, k.h0_bufs = h0_t, h0_bufs
            q = c % 2
            k.T = {n: T[n][q] for n in tnames}
            k.TB = {n: TB[n][q] for n in tnames}
            if c >= 2:
                k.T["xc"], k.TB["xc"] = Txc[c - 2], TBxc[c - 2]
            k.T["gl"], k.TB["gl"] = Tgl[c % 3], TBgl[c % 3]
            k.xcb, k.xcb_b = xcbs[q], xcb_bs[q]
            W = L + 3
            k.xb3 = xb_t[:, c * Sq * W:(c + 1) * Sq * W].rearrange("p (s l) -> p s l", s=Sq)
            k.xbb = xb_bufs[c]
            T_, TB_, xb3, xbb = k.T, k.TB, k.xb3, k.xbb
            bL = proj(12 + c, N, hi)
            k.bG = None if lean else proj(16 + c, N, hi)
            cp("dve", xb3[:, :, 3:3 + L], v3(bank_ap(bL, N), Sq), [bank_b[bL]], [xbb])
            if not lean:
                cp("act", T_["gl"][:, 0:N], bank_ap(k.bG, N), [bank_b[k.bG]], [TB_["gl"]])
                act(T_["x2"][:, 0:N], T_["gl"][:, 0:N], AF.Square, [TB_["gl"]], [TB_["x2"]], scale=GK1 ** 0.5)
            xc3 = v3(T_["xc"][:, 0:N], Sq)
            ts("dve", xc3, xb3[:, :, 0:L], lcw[:, c:c + 1], lcb[:, c:c + 1], ALU.mult, ALU.add,
               [xbb, par_b], [TB_["xc"]])
            for kk in range(1, 4):
                stt(xc3, xb3[:, :, kk:kk + L], lcw[:, 4 * kk + c:4 * kk + c + 1], xc3, ALU.mult, ALU.add,
                    [xbb, par_b, TB_["xc"]], [TB_["xc"]])
            cp("dve", k.xcb[:, 0:N], T_["xc"][:, 0:N], [TB_["xc"]], [k.xcb_b])
            if not lean:
                stt(T_["x2"][:, 0:N], T_["x2"][:, 0:N], GK0, T_["gl"][:, 0:N], ALU.add, ALU.mult,
                    [TB_["x2"], TB_["gl"]], [TB_["x2"]])
            if so3 is not None:
                cp("pool", so3[:, c, 2 * Sq:5 * Sq].rearrange("p (s k) -> p s k", k=3), xb3[:, :, L:L + 3], [xbb], [so_buf])
            if Sq == 1:
                cp("pool", xb3[:, :, 0:3], xb3[:, :, L:L + 3], [xbb], [xbb])
            if fix == "mask":
                ts("dve", xb3[:, :, 0:3], xb3[:, :, 0:3], flag[:, 0:1], None, ALU.mult, None, [xbb, par_b], [xbb])
            return k

        def mixer_b_p2(k):
            c, N, Sq, L, T_, TB_ = k.c, k.N, k.Sq, k.L, k.T, k.TB
            bRa, bRi = nextbank(), nextbank()
            for gi, b in ((0, bRa), (1, bRi)):
                S.op("pe", lambda e, gi=gi, b=b: e.matmul(out=bank_ap(b, N), lhsT=wbd[:, (gi * 4 + c) * 128:(gi * 4 + c + 1) * 128],
                                                          rhs=k.xcb[:, 0:N], start=True, stop=True), [wbd_b, k.xcb_b], [bank_b[b]])
            r, i_, a, om, ixc = (T_[n][:, 0:N] for n in ("r", "i", "a", "om", "ixc"))
            q = c % 2
            nsig = 2 if k.lean else 3
            sig3 = rig[q][:].rearrange("p (s l) -> p s l", s=3)[:, 0:nsig, 0:N]
            sig_bufs = [TB_["r"], TB_["i"]] + ([] if k.lean else [TB_["x2"]])
            act(r, bank_ap(bRa, N), AF.Exp, [bank_b[bRa], par_b], [TB_["r"]], scale=-1.0, bias=nba[:, c:c + 1])
            act(i_, bank_ap(bRi, N), AF.Exp, [bank_b[bRi], par_b], [TB_["i"]], scale=-1.0, bias=nbx[:, c:c + 1])
            if not k.lean:
                x2, gl = T_["x2"][:, 0:N], T_["gl"][:, 0:N]
                act(x2, x2, AF.Exp, [TB_["x2"]], [TB_["x2"]], scale=-1.0)
            act(sig3, sig3, AF.Ln, sig_bufs + [par_b], sig_bufs, bias=cst[:, 0:1])
            act(sig3, sig3, AF.Exp, sig_bufs, sig_bufs, scale=-1.0)
            act(a, r, AF.Exp, [TB_["r"], par_b], [TB_["a"]], scale=nsp[:, c:c + 1])
            act(om, r, AF.Exp, [TB_["r"], par_b], [TB_["om"]], scale=nsp2[:, c:c + 1])
            if k.fix == "one":
                mset("pool", T_["om"][:, 0:1], 0.0, [TB_["om"]])
            elif k.fix == "flag":
                ts("dve", T_["om"][:, 0:1], T_["om"][:, 0:1], flag[:, 0:1], None, ALU.mult, None,
                   [TB_["om"], par_b], [TB_["om"]])
            tt("pool", ixc, i_, T_["xc"][:, 0:N], ALU.mult, [TB_["i"], TB_["xc"]], [TB_["ixc"]])
            act(om, om, AF.Ln, [TB_["om"], par_b], [TB_["om"]], scale=-1.0, bias=cst[:, 0:1])
            act(om, om, AF.Exp, [TB_["om"]], [TB_["om"]], scale=0.5)
            tt("pool", ixc, om, ixc, ALU.mult, [TB_["om"], TB_["ixc"]], [TB_["ixc"]])
            if not k.lean:
                tt("pool", gl, gl, x2, ALU.mult, [TB_["gl"], TB_["x2"]], [TB_["gl"]])

        def mixer_b_p3(k):
            c, N, Sq, L, T_, TB_, h0_t, h0_bufs = k.c, k.N, k.Sq, k.L, k.T, k.TB, k.h0_t, k.h0_bufs
            for s_ in range(Sq):
                S.op("dve", lambda e, s_=s_: e.tensor_tensor_scan(
                    out=T_["h"][:, s_ * L:(s_ + 1) * L], data0=T_["a"][:, s_ * L:(s_ + 1) * L],
                    data1=T_["ixc"][:, s_ * L:(s_ + 1) * L], initial=h0_t[:, c * Sq + s_:c * Sq + s_ + 1],
                    op0=ALU.mult, op1=ALU.add), [TB_["a"], TB_["ixc"], h0_bufs[c]], [TB_["h"]])
            h = T_["h"][:, 0:N]
            h3 = v3(h, Sq)
            cp("pool", h0_t[:, c * Sq:(c + 1) * Sq], h3[:, :, L - 1], [TB_["h"]], [h0_bufs[c]])
            if k.fix == "mask":
                ts("dve", h0_t[:, c:c + 1], h0_t[:, c:c + 1], flag[:, 0:1], None, ALU.mult, None,
                   [h0_bufs[c], par_b], [h0_bufs[c]])
            if k.so3 is not None:
                cp("pool", k.so3[:, c, 5 * Sq:6 * Sq], h3[:, :, L - 1], [TB_["h"]], [k.so_buf])
            if not k.lean:
                tt("dve", mixT[:, (4 + c) * 512:(4 + c) * 512 + N], h, T_["gl"][:, 0:N], ALU.mult,
                   [TB_["h"], TB_["gl"]], [mixT_bs[4 + c]])

        def mixers_b_gen(args, hooks=(), pre=None):
            def hook(i):
                if i < len(hooks) and hooks[i] is not None:
                    hooks[i]()
            ks = {}
            ks[0] = mixer_b_p1(0, *args)
            if pre is not None:
                pre()
            yield 1
            ks[1] = mixer_b_p1(1, *args)
            hook(0)
            yield 2
            mixer_b_p2(ks[0])
            yield 3
            ks[2] = mixer_b_p1(2, *args)
            yield 4
            hook(1)
            mixer_b_p2(ks[1])
            mixer_b_p3(ks[0])
            yield 5
            ks[3] = mixer_b_p1(3, *args)
            hook(2)
            yield 6
            mixer_b_p2(ks[2])
            mixer_b_p3(ks[1])
            yield 7
            hook(3)
            mixer_b_p2(ks[3])
            mixer_b_p3(ks[2])
            hook(4)
            mixer_b_p3(ks[3])

        def mixers_b(args, hooks=(), pre=None):
            for _ in mixers_b_gen(args, hooks, pre):
                pass

        def mixer_a(c, N, Sq, L, pb_t, pb_bufs, so3, so_buf, hi):
            q = c % 2
            T_ = {n: T[n][q] for n in tnames}
            TB_ = {n: TB[n][q] for n in tnames}
            W = L + 2
            pb3 = pb_t[:, c * Sq * W:(c + 1) * Sq * W].rearrange("p (s l) -> p s l", s=Sq)
            pbb = pb_bufs[c]
            bX = proj(8 + c, N, hi)
            bC = proj(4 + c, N, hi)
            bB = proj(c, N, hi)
            tA, ya = T_["tA"][:, 0:N], T_["ya"][:, 0:N]
            cp("act", tA, bank_ap(bX, N), [bank_b[bX]], [TB_["tA"]])
            tt("dve", pb3[:, :, 2:2 + L], v3(bank_ap(bC, N), Sq), v3(tA, Sq), ALU.mult, [bank_b[bC], TB_["tA"]], [pbb])
            ya3 = v3(ya, Sq)
            ts("dve", ya3, pb3[:, :, 0:L], caw[:, c:c + 1], None, ALU.mult, None, [pbb, par_b], [TB_["ya"]])
            for k in (1, 2):
                stt(ya3, pb3[:, :, k:k + L], caw[:, 4 * k + c:4 * k + c + 1], ya3, ALU.mult, ALU.add,
                    [pbb, par_b, TB_["ya"]], [TB_["ya"]])
            tt("dve", mixT[:, c * 512:c * 512 + N], bank_ap(bB, N), ya, ALU.mult, [bank_b[bB], TB_["ya"]], [mixT_bs[c]])
            if so3 is not None:
                cp("pool", so3[:, c, 0:2 * Sq].rearrange("p (s k) -> p s k", k=2), pb3[:, :, L:L + 2], [pbb], [so_buf])
            if Sq == 1:
                cp("pool", pb3[:, :, 0:2], pb3[:, :, L:L + 2], [pbb], [pbb])

        def epilogue_parts(xs_list, ydst, yrow0):
            J = len(xs_list)
            keep = {}

            def front(j):
                x_ap, x_buf = xs_list[j]
                keep[j] = rstd_of(x_ap, x_buf)

            def back(j):
                x_ap, x_buf = xs_list[j]
                rs, sbf, _ = keep[j]
                stt(x_ap, x_ap, rs, gfb[:], ALU.mult, ALU.mult, [x_buf, sbf, gfb_b], [x_buf])
                ev = S.dma("pool", ydst[yrow0 + j * 128:yrow0 + (j + 1) * 128, :], x_ap, [x_buf], [], "st")
                out_events.append(ev)

            def stage(i):
                if i < J:
                    front(i)
                if 1 <= i <= J:
                    back(i - 1)
            return [(lambda i=i: stage(i)) for i in range(5)]

        def make_mix_gen(N, Sq, L, pb_t, pb_bufs, xb_t, xb_bufs, h0_t, h0_bufs, fix, so3, so_buf, prev_epi, extra0=None,
                         defer_epi=False):
            pe_ = list(prev_epi) if prev_epi else [None] * 5
            if defer_epi:
                def h3():
                    for i in (0, 1, 2):
                        pe_[i]()
                    mixer_a(2, N, Sq, L, pb_t, pb_bufs, so3, so_buf, 0)

                def h4():
                    for i in (3, 4):
                        pe_[i]()
                    mixer_a(3, N, Sq, L, pb_t, pb_bufs, so3, so_buf, 0)
                return mixers_b_gen((N, Sq, L, xb_t, xb_bufs, h0_t, h0_bufs, False, fix, so3, so_buf, 0),
                                    [None, lambda: mixer_a(0, N, Sq, L, pb_t, pb_bufs, so3, so_buf, 0),
                                     lambda: mixer_a(1, N, Sq, L, pb_t, pb_bufs, so3, so_buf, 0), h3, h4], pre=None)

            def mk(i, fn2):
                def f():
                    if extra0 is not None:
                        extra0()
                    if pe_[i] is not None:
                        pe_[i]()
                    if fn2 is not None:
                        fn2()
                return f
            return mixers_b_gen((N, Sq, L, xb_t, xb_bufs, h0_t, h0_bufs, False, fix, so3, so_buf, 0),
                                [mk(1, None), mk(2, None), mk(3, None),
                                 mk(4, lambda: mixer_a(0, N, Sq, L, pb_t, pb_bufs, so3, so_buf, 0)),
                                 lambda: mixer_a(1, N, Sq, L, pb_t, pb_bufs, so3, so_buf, 0)], pre=pe_[0])

        def main_tile(xs_list, N, Sq, L, ydst, yrow0, pb_t, pb_bufs, xb_t, xb_bufs, h0_t, h0_bufs, fix, so3, so_buf,
                      mid_hook, start_hook, prev_epi, extra0=None, cur_gen=None, next_factory=None, after_mix=None):
            J = N // 128
            my_epi = epilogue_parts(xs_list, ydst, yrow0)
            was_prerun = cur_gen is not None
            if cur_gen is None:
                cur_gen = make_mix_gen(N, Sq, L, pb_t, pb_bufs, xb_t, xb_bufs, h0_t, h0_bufs, fix, so3, so_buf, prev_epi, extra0)
            for _ in cur_gen:
                pass
            wslots = [ring_load(CH_OUT + kc) for kc in range(8)]
            if not was_prerun:
                for c in (2, 3):
                    mixer_a(c, N, Sq, L, pb_t, pb_bufs, so3, so_buf, 0)
            if after_mix is not None:
                after_mix()
            P1K = (0, 1, 2, 4, 5, 6)
            for b in sorted(range(2 * J), key=lambda b_: bank_age[b_]):
                j, half = b // 2, b % 2
                S.group("pe", [lambda e, kc=kc, j=j, half=half, b=b: e.matmul(
                    out=bank_ap(b), lhsT=mixT[:, kc * 512 + j * 128:kc * 512 + (j + 1) * 128],
                    rhs=wslots[kc][0][:, half * 512:(half + 1) * 512], start=(kc == 0), stop=False)
                    for kc in P1K], [mixT_bs[kc] for kc in P1K] + [wslots[kc][1] for kc in P1K], [bank_b[b]])
            st["bank"] = 2 * J
            for j in range(J):
                for half in range(2):
                    b = 2 * j + half
                    S.group("pe", [lambda e, kc=kc, j=j, half=half, b=b: e.matmul(
                        out=bank_ap(b), lhsT=mixT[:, kc * 512 + j * 128:kc * 512 + (j + 1) * 128],
                        rhs=wslots[kc][0][:, half * 512:(half + 1) * 512], start=False, stop=(kc == 7))
                        for kc in (3, 7)], [mixT_bs[3], mixT_bs[7], wslots[3][1], wslots[7][1]], [bank_b[b]])
            for j in range(J):
                x_ap, x_buf = xs_list[j]
                tt("dve", x_ap, ps[:, 2 * j * 512:(2 * j + 2) * 512], x_ap, ALU.add,
                   [bank_b[2 * j], bank_b[2 * j + 1], x_buf], [x_buf])
            fr = {}
            for j in range(J):
                fr[j] = norm_front(xs_list, j)
                if j >= 1:
                    norm_back(fr[j - 1], g2t, 1, j - 1)
            norm_back(fr[J - 1], g2t, 1, J - 1)
            if start_hook is not None:
                start_hook()
            for fc in range(32):
                if mid_hook is not None and 8 <= fc <= 16 and fc % 2 == 0:
                    mid_hook((fc - 8) // 2)
                slot, sbuf_ = ring_load(CH_UP + fc)
                b = nextbank()
                S.group("pe", [lambda e, kc=kc, slot=slot, b=b: e.matmul(
                    out=bank_ap(b, N), lhsT=slot[:, kc * 128:(kc + 1) * 128], rhs=hTs[1][:, kc * 512:kc * 512 + N],
                    start=(kc == 0), stop=(kc == 7)) for kc in range(8)], [sbuf_, hT_bs[1]], [bank_b[b]])
                k = st["rl"] % 2
                st["rl"] += 1
                rl_ap = rl[:, k * 512:k * 512 + N]
                act(rl_ap, bank_ap(b, N), AF.Relu, [bank_b[b]], [rl_b[k]])
                tt("dve" if fc % 2 == 0 else "pool", uT[:, fc * 512:fc * 512 + N], rl_ap, rl_ap, ALU.mult, [rl_b[k]], [uT_b[fc]])
            def down_pass(js):
                for kc in range(32):
                    slot, sbuf_ = ring_load(CH_DOWN + kc)
                    fns = []
                    for j in js:
                        for half in range(2):
                            fns.append(lambda e, kc=kc, j=j, half=half, slot=slot: e.matmul(
                                out=bank_ap(2 * j + half), lhsT=uT[:, kc * 512 + j * 128:kc * 512 + (j + 1) * 128],
                                rhs=slot[:, half * 512:(half + 1) * 512], start=(kc == 0), stop=(kc == 31)))
                    S.group("pe", fns, [sbuf_, uT_b[kc]], [bank_b[2 * j + h_] for j in js for h_ in range(2)])
                    yield kc

            def down_adds(js):
                for j in js:
                    x_ap, x_buf = xs_list[j]
                    tt("dve", x_ap, ps[:, 2 * j * 512:(2 * j + 2) * 512], x_ap, ALU.add,
                       [bank_b[2 * j], bank_b[2 * j + 1], x_buf], [x_buf])

            next_gen = None
            if next_factory is not None and J == 4:
                next_gen = next_factory(my_epi)
                allowed["banks"] = [4, 5, 6, 7]
                st["bank"] = 0
                nsteps = 0
                for pi, js in enumerate(((0, 1), (2, 3))):
                    for kc in range(32):
                        slot, sbuf_ = ring_load(CH_DOWN + kc)
                        fns = []
                        for jj, j in enumerate(js):
                            for half in range(2):
                                fns.append(lambda e, kc=kc, j=j, jj=jj, half=half, slot=slot: e.matmul(
                                    out=bank_ap(2 * jj + half), lhsT=uT[:, kc * 512 + j * 128:kc * 512 + (j + 1) * 128],
                                    rhs=slot[:, half * 512:(half + 1) * 512], start=(kc == 0), stop=(kc == 31)))
                        S.group("pe", fns, [sbuf_, uT_b[kc]], [bank_b[i] for i in range(4)])
                        if kc in ((5, 13, 21) if pi == 0 else (8, 20)) and nsteps < 7:
                            next(next_gen)
                            nsteps += 1
                    for jj, j in enumerate(js):
                        x_ap, x_buf = xs_list[j]
                        tt("dve", x_ap, ps[:, 2 * jj * 512:(2 * jj + 2) * 512], x_ap, ALU.add,
                           [bank_b[2 * jj], bank_b[2 * jj + 1], x_buf], [x_buf])
                    if pi == 0:
                        for _ in range(2):
                            next(next_gen)
                            nsteps += 1
                allowed["banks"] = list(range(8))
            else:
                for _ in down_pass(tuple(range(J))):
                    pass
                st["bank"] = 2 * J
                down_adds(tuple(range(J)))
                if next_factory is not None:
                    next_gen = next_factory(my_epi)
            return my_epi, next_gen

        tiles = [("L", xl, lt * 512, 4) for lt in range(4)] + [("M", xp, t * 512, 4) for t in range(4)] + [("M", xs, 0, 2)]
        xs_of = {}

        def load_tile(i):
            if i < len(tiles):
                xs_of[i] = load_x(tiles[i][1], tiles[i][2], tiles[i][3])

        load_tile(0)
        load_tile(1)
        norm_T(xs_of[0], g1t, 0)
        for lt in range(4):
            load_tile(lt + 2)
            hi = lt % 2
            nx, nh = xs_of[lt + 1], (lt + 1) % 2
            kk = {}

            def nstage(i, nx=nx, nh=nh, kk=kk):
                convert_some(3 if cvp["i"] < 60 else 0)
                if i < 4:
                    kk[i] = norm_front(nx, i)
                if 1 <= i <= 4:
                    norm_back(kk[i - 1], g1t, nh, i - 1)
            mixers_b((512, 1, 512, xbuf_p, xb_p_b, h0_p, h0_p_b, True, "one" if lt == 0 else ("mask" if lt == 3 else None), None, None, hi),
                     [(lambda i=i, ns=nstage: ns(i)) for i in range(1, 5)], pre=lambda ns=nstage: ns(0))
            if lt == 3:
                for c in range(4):
                    q = c % 2
                    bC, bX = proj(4 + c, 2, hi, 510), proj(8 + c, 2, hi, 510)
                    cp("act", T["tA"][q][:, 0:2], bank_ap(bX, 2), [bank_b[bX]], [TB["tA"][q]])
                    tt("dve", pbuf_p[:, c * 514:c * 514 + 2], bank_ap(bC, 2), T["tA"][q][:, 0:2], ALU.mult,
                       [bank_b[bC], TB["tA"][q]], [pb_p_b[c]])
        for c in range(4):
            ts("dve", pbuf_p[:, c * 514:c * 514 + 2], pbuf_p[:, c * 514:c * 514 + 2], flag[:, 0:1], None, ALU.mult, None,
               [pb_p_b[c], par_b], [pb_p_b[c]])

        soT_p3 = soT_p[:].rearrange("p (c r) -> p c r", r=6)
        soT_s3 = soT_s[:].rearrange("p (c r) -> p c r", r=24)
        def tile_args(t):
            if t < 4:
                return (512, 1, 512, pbuf_p, pb_p_b, xbuf_p, xb_p_b, h0_p, h0_p_b, "flag" if t == 0 else None,
                        soT_p3 if t == 3 else None, soT_p_b)
            return (256, 4, 64, pbuf_s, pb_s_b, xbuf_s, xb_s_b, h0_s, h0_s_b, None, soT_s3, soT_s_b)

        def emit_state_outputs():
            for (soT3, sbuf_, so_t, so_buf2, nr, dst) in ((soT_p3, soT_p_b, so_p, so_p_b, 6, stp_d),
                                                         (soT_s3, soT_s_b, so_s, so_s_b, 24, sts_d)):
                for c in range(4):
                    b = nextbank()
                    S.op("pe", lambda e, c=c, b=b, soT3=soT3, nr=nr: e.transpose(
                        out=ps[0:nr, b * 512:b * 512 + 128], in_=soT3[:, c, :], identity=identf[:]),
                        [sbuf_, ident_b], [bank_b[b]])
                    cp("dve", so_t[0:nr, c * 128:(c + 1) * 128], ps[0:nr, b * 512:b * 512 + 128], [bank_b[b]], [so_buf2])
                out_events.append(S.dma("sp", dst, so_t[0:nr, :], [so_buf2], [], "sto"))

        epi, gen = None, None
        for t in range(5):
            ti = 4 + t
            mid = None
            if t < 4:
                kk2 = {}

                def mid(i, kk2=kk2, ti=ti, t=t):
                    if t == 0:
                        convert_some(4)
                    nx = xs_of[ti + 1]
                    if i < len(nx):
                        kk2[i] = norm_front(nx, i)
                    if 1 <= i <= len(nx):
                        norm_back(kk2[i - 1], g1t, 0, i - 1)
            end = (lambda ti=ti: load_tile(ti + 1)) if 1 <= t < 4 else None
            N_, Sq_, L_, pbt, pbb_, xbt, xbb_, h0t, h0b, fix_, so3_, sob_ = tile_args(t)
            nf = None
            if t < 4:
                a_ = tile_args(t + 1)
                nf = (lambda epi_, a_=a_: make_mix_gen(a_[0], a_[1], a_[2], a_[3], a_[4], a_[5], a_[6], a_[7], a_[8], a_[9],
                                                       a_[10], a_[11], epi_, None, True))
            epi, gen = main_tile(xs_of[ti], N_, Sq_, L_, yp if t < 4 else ys, (t * 512) if t < 4 else 0,
                                 pbt, pbb_, xbt, xbb_, h0t, h0b, fix_, so3_, sob_, mid, end, epi,
                                 (lambda: convert_some(4)) if t == 0 else None, gen, nf,
                                 emit_state_outputs if t == 4 else None)
        for f_ in epi:
            f_()

        S.wait_events("sp", out_events)
        S.emit()
    return nc


_NC_CACHE = {}


def kernel(x_prompt, x_sample, state_conv_a, state_lru_conv, state_lru_h, norm1_g, w_in, conv_a_w, lru_conv_w,
           lru_conv_b, lru_wa, lru_ba, lru_wx, lru_bx, lru_a_param, w_out, norm2_g, w_up, w_down, norm_f_g):
    f = lambda a: np.ascontiguousarray(np.asarray(a, dtype=np.float32))
    x_prompt, x_sample = f(x_prompt), f(x_sample)
    if "nc" not in _NC_CACHE:
        _NC_CACHE["nc"] = build_program()
    nc = _NC_CACHE["nc"]
    shared = {
        "w_in": f(w_in[0]), "w_out": f(w_out[0]), "w_up": f(w_up[0]), "w_down": f(w_down[0]),
        "g1": f(norm1_g[0]), "g2": f(norm2_g[0]), "gf": f(norm_f_g),
        "caw": f(conv_a_w[0]), "lcw": f(lru_conv_w[0]), "lcb": f(lru_conv_b[0]),
        "wa": f(lru_wa[0]), "wx": f(lru_wx[0]), "ba": f(np.reshape(lru_ba[0], -1)), "bx": f(np.reshape(lru_bx[0], -1)),
        "apar": f(lru_a_param[0]),
    }
    sca, slc, slh = f(state_conv_a[0]), f(state_lru_conv[0]), f(state_lru_h[0])
    in_maps = []
    for c in range(NCORES):
        s, hf = c // 2, c % 2
        flag = np.zeros((128, 2), np.float32)
        flag[:, 0] = float(hf)
        flag[:, 1] = 1.0 - float(hf)
        m = dict(shared)
        m.update({
            "xp": f(x_prompt[s, hf * 2048:(hf + 1) * 2048]),
            "xl": f(x_prompt[s, 0:2048]),
            "xs": f(x_sample[4 * c:4 * c + 4].reshape(256, D)),
            "flag": flag,
            "sca": f(sca[4 * c:4 * c + 4].reshape(8, 512)),
            "slc": f(slc[4 * c:4 * c + 4].reshape(12, 512)),
            "slh": f(slh[4 * c:4 * c + 4].reshape(4, 512)),
        })
        in_maps.append(m)
    res = run_bass_kernel_spmd(nc, in_maps, core_ids=list(range(NCORES)))
    rs = res.results
    y_prompt = np.stack([np.concatenate([rs[2 * s]["yp"], rs[2 * s + 1]["yp"]], axis=0) for s in range(4)], axis=0)
    y_sample = np.concatenate([rs[c]["ys"].reshape(4, 64, D) for c in range(NCORES)], axis=0)
    stp = np.stack([rs[2 * s + 1]["stp"] for s in range(4)], axis=0)
    new_a_p = stp[:, 0:2][None]
    new_l_p = stp[:, 2:5][None]
    new_h_p = stp[:, 5][None]
    sts = np.stack([rs[c]["sts"] for c in range(NCORES)], axis=0)
    new_a_s = sts[:, 0:8].reshape(32, 2, 512)[None]
    new_l_s = sts[:, 8:20].reshape(32, 3, 512)[None]
    new_h_s = sts[:, 20:24].reshape(32, 512)[None]
    out = (y_prompt, y_sample, new_a_p, new_l_p, new_h_p, new_a_s, new_l_s, new_h_s)
    return tuple(np.ascontiguousarray(o, dtype=np.float32) for o in out)
```

```python
import numpy as np
from contextlib import ExitStack
import concourse.bass as bass
import concourse.mybir as mybir
from concourse.bass_utils import run_bass_kernel_spmd

F32, BF16 = mybir.dt.float32, mybir.dt.bfloat16
AF = mybir.ActivationFunctionType
ALU = mybir.AluOpType

NCORES = 8
D = 1024
EPS = 1e-6
R = 15
OC_ORDER = [12, 16, 13, 17, 14, 18, 15, 19, 0, 4, 8, 1, 5, 9, 2, 6, 10, 3, 7, 11]
POS_OF_OC = {oc: i for i, oc in enumerate(OC_ORDER)}
CH_OUT, CH_UP, CH_DOWN, NCHUNK = 20, 28, 60, 92
GK0 = 2.0 * 0.7978845608028654
GK1 = GK0 * 0.044715


class Buf:
    __slots__ = ("name", "w", "rs")

    def __init__(self, name):
        self.name = name
        self.w = None
        self.rs = {}


class Sched:
    ENG = ("pe", "act", "dve", "pool", "sp")

    def __init__(self, nc, es):
        self.nc, self.es = nc, es
        self.sem, self.cnt = {}, {}
        for e in self.ENG[:4]:
            self.sem[e] = es.enter_context(nc.semaphore("s_" + e))
            self.cnt[e] = 0
        self.seen = {e: {} for e in self.ENG}
        self.prog = {e: [] for e in self.ENG}
        self.pools = {}

    def _waits(self, eng, reads, writes, extra=()):
        need = {}

        def add(ev):
            if ev is not None and need.get(ev[0], 0) < ev[1]:
                need[ev[0]] = ev[1]
        for b in reads:
            add(b.w)
        for b in writes:
            add(b.w)
            for k, v in b.rs.items():
                add((k, v))
        for ev in extra:
            add(ev)
        out, seen = [], self.seen[eng]
        for k, v in need.items():
            if k == eng and eng == "pe":
                continue
            if seen.get(k, 0) >= v:
                continue
            seen[k] = v
            out.append((k, v))
        return out

    def _commit(self, ev, reads, writes):
        for b in writes:
            b.w = ev
            b.rs = {}
        for b in reads:
            if b.rs.get(ev[0], 0) < ev[1]:
                b.rs[ev[0]] = ev[1]

    def group(self, eng, fns, reads=(), writes=()):
        waits = self._waits(eng, reads, writes)
        self.cnt[eng] += 1
        ev = (eng, self.cnt[eng])
        self.prog[eng].append((waits, list(fns), (eng, 1)))
        self._commit(ev, reads, writes)
        return ev

    def op(self, eng, fn, reads=(), writes=()):
        return self.group(eng, [fn], reads, writes)

    def dma_pool(self, name, n):
        keys = []
        for i in range(n):
            k = "%s%d" % (name, i)
            self.sem[k] = self.es.enter_context(self.nc.semaphore("d_" + k))
            self.cnt[k] = 0
            keys.append(k)
        self.pools[name] = [keys, 0]

    def dma(self, q, out, in_, reads, writes, pool, **kw):
        p = self.pools[pool]
        k = p[0][p[1] % len(p[0])]
        p[1] += 1
        prev = self.cnt[k]
        waits = self._waits(q, reads, writes, [(k, prev)] if prev else [])
        self.cnt[k] = prev + 16
        ev = (k, prev + 16)
        self.prog[q].append((waits, [lambda e: e.dma_start(out=out, in_=in_, **kw)], (k, 16)))
        self._commit(ev, reads, writes)
        return ev

    def wait_events(self, eng, evs):
        waits = self._waits(eng, (), (), evs)
        self.prog[eng].append((waits, [], None))

    def check_no_deadlock(self):
        val = {k: 0 for k in self.sem}
        pos = {e: 0 for e in self.ENG}
        while True:
            prog = False
            for e in self.ENG:
                q = self.prog[e]
                while pos[e] < len(q):
                    waits, fns, inc = q[pos[e]]
                    if any(val[k] < v for k, v in waits):
                        break
                    if inc is not None:
                        val[inc[0]] += inc[1]
                    pos[e] += 1
                    prog = True
            if not prog:
                break
        stuck = {e: (pos[e], len(self.prog[e])) for e in self.ENG if pos[e] < len(self.prog[e])}
        if stuck:
            msg = []
            for e, (p, n) in stuck.items():
                msg.append("%s at %d/%d waits %s" % (e, p, n, [(k, v, val[k]) for k, v in self.prog[e][p][0] if val[k] < v]))
            raise RuntimeError("static deadlock: " + "; ".join(msg))

    def emit(self):
        self.check_no_deadlock()
        def make(name):
            def f(e):
                for waits, fns, inc in self.prog[name]:
                    for k, v in waits:
                        e.wait_ge(self.sem[k], v)
                    ins = None
                    for fn in fns:
                        ins = fn(e)
                    if inc is not None:
                        ins.then_inc(self.sem[inc[0]], inc[1])
            return f
        with self.nc.Block() as block:
            block.sync(make("sp"))
            block.scalar(make("act"))
            block.vector(make("dve"))
            block.gpsimd(make("pool"))
            block.tensor(make("pe"))


def build_program():
    nc = bass.Bass("TRN2", target_bir_lowering=False)

    def din(name, shape, dt=F32):
        return nc.dram_tensor(name, list(shape), dt, kind="ExternalInput").ap()

    def dout(name, shape, dt=F32):
        return nc.dram_tensor(name, list(shape), dt, kind="ExternalOutput").ap()

    xp, xl, xs = din("xp", [2048, D]), din("xl", [2048, D]), din("xs", [256, D])
    flag_d = din("flag", [128, 2])
    sca, slc, slh = din("sca", [8, 512]), din("slc", [12, 512]), din("slh", [4, 512])
    w_in, w_out = din("w_in", [D, 2560]), din("w_out", [D, D])
    w_up, w_down = din("w_up", [D, 4096]), din("w_down", [4096, D])
    g1_d, g2_d, gf_d = din("g1", [D]), din("g2", [D]), din("gf", [D])
    caw_d, lcw_d, lcb_d = din("caw", [3, 512]), din("lcw", [4, 512]), din("lcb", [512])
    wa_d, wx_d = din("wa", [8, 64, 64]), din("wx", [8, 64, 64])
    ba_d, bx_d, apar_d = din("ba", [512]), din("bx", [512]), din("apar", [512])
    yp, ys = dout("yp", [2048, D]), dout("ys", [256, D])
    stp_d = dout("stp", [6, 512])
    sts_d = dout("sts", [24, 512])
    wscr = nc.dram_tensor("wscr", [NCHUNK, 128, 1024], BF16).ap()

    with ExitStack() as es:
        S = Sched(nc, es)
        S.dma_pool("ring", R)
        S.dma_pool("x", 8)
        S.dma_pool("cv", 40)
        S.dma_pool("par", 8)
        S.dma_pool("st", 8)
        S.dma_pool("sto", 2)

        def sb(name, shape, dt=F32):
            return es.enter_context(nc.sbuf_tensor("sb_" + name, list(shape), dt))

        ring = sb("ring", [128, R * 1024], BF16)
        ring_b = [Buf("ring%d" % i) for i in range(R)]
        xsl = sb("xsl", [128, 8 * D])
        xsl_b = [Buf("xs%d" % i) for i in range(8)]
        hb = sb("hb", [128, 2 * D], BF16)
        hb_b = [Buf("hb0"), Buf("hb1")]
        hTs = [sb("hTa", [128, 8 * 512], BF16), sb("hTb", [128, 8 * 512], BF16)]
        hT_bs = [Buf("hTa"), Buf("hTb")]
        mixT = sb("mixT", [128, 8 * 512], BF16)
        mixT_bs = [Buf("mixT%d" % i) for i in range(8)]
        uT = sb("uT", [128, 32 * 512], BF16)
        uT_b = [Buf("uT%d" % i) for i in range(32)]
        rl = sb("rl", [128, 2 * 512], BF16)
        rl_b = [Buf("rl0"), Buf("rl1")]
        tnames = ["tA", "ya", "xc", "r", "i", "a", "om", "ixc", "h", "x2", "gl"]
        T, TB = {}, {}
        rig = [sb("t_rig%d" % q, [128, 3 * 512]) for q in range(2)]
        for n in tnames:
            cnt_ = 1 if n in ("tA", "ya") else 2
            if n in ("r", "i", "x2"):
                o_ = ("r", "i", "x2").index(n) * 512
                T[n] = [rig[q][:, o_:o_ + 512] for q in range(2)]
            else:
                T[n] = [sb("t_%s%d" % (n, q), [128, 512]) for q in range(cnt_)]
            TB[n] = [Buf("t_%s%d" % (n, q)) for q in range(cnt_)]
            if cnt_ == 1:
                T[n].append(T[n][0])
                TB[n].append(TB[n][0])
        Tgl = T["gl"] + [sb("t_gl2", [128, 512])]
        TBgl = TB["gl"] + [Buf("t_gl2")]
        Txc = [sb("t_xc_c%d" % c_, [128, 512]) for c_ in range(2, 4)]
        TBxc = [Buf("t_xc_c%d" % c_) for c_ in range(2, 4)]
        xcbs = [sb("xcb%d" % q, [128, 512], BF16) for q in range(2)]
        xcb_bs = [Buf("xcb0"), Buf("xcb1")]
        pbuf_p = sb("pbuf_p", [128, 4 * 514]); pbuf_s = sb("pbuf_s", [128, 4 * 4 * 66])
        xbuf_p = sb("xbuf_p", [128, 4 * 515]); xbuf_s = sb("xbuf_s", [128, 4 * 4 * 67])
        pb_p_b = [Buf("pbp%d" % c) for c in range(4)]; pb_s_b = [Buf("pbs%d" % c) for c in range(4)]
        xb_p_b = [Buf("xbp%d" % c) for c in range(4)]; xb_s_b = [Buf("xbs%d" % c) for c in range(4)]
        h0_p = sb("h0_p", [128, 4]); h0_s = sb("h0_s", [128, 16])
        h0_p_b = [Buf("h0p%d" % c) for c in range(4)]; h0_s_b = [Buf("h0s%d" % c) for c in range(4)]
        identb = sb("identb", [128, 128], BF16); identf = sb("identf", [128, 128])
        ident_b = Buf("ident")
        gfb = sb("gfb", [128, D]); gfb_b = Buf("gfb")
        wbd = sb("wbd", [128, 8 * 128], BF16)
        wbd_b = Buf("wbd")
        caw = sb("caw", [128, 12]); lcw = sb("lcw", [128, 16]); lcb = sb("lcb", [128, 4])
        nba = sb("nba", [128, 4]); nbx = sb("nbx", [128, 4]); apar = sb("apar", [128, 4])
        nsp = sb("nsp", [128, 4]); nsp2 = sb("nsp2", [128, 4]); spt = sb("spt", [128, 16])
        g1t = sb("g1t", [128, 8]); g2t = sb("g2t", [128, 8]); flag = sb("flagt", [128, 2])
        cst = sb("cst", [128, 4])
        par_b = Buf("params")
        stat = sb("stat", [128, 8 * 4])
        stat_b = [Buf("stat%d" % i) for i in range(8)]
        stin = T["a"][0]; stin_b = TB["a"][0]
        stT = sb("stT", [128, 4 * 24]); stT_b = Buf("stT")
        soT_p = sb("soT_p", [128, 4 * 6]); soT_s = sb("soT_s", [128, 4 * 24])
        soT_p_b = Buf("soT_p"); soT_s_b = Buf("soT_s")
        so_p = T["r"][1]; so_s = T["i"][1]
        so_p_b = TB["r"][1]; so_s_b = TB["i"][1]
        ps = es.enter_context(nc.psum_tensor("ps", [128, 8 * 512], F32))
        bank_b = [Buf("bank%d" % i) for i in range(8)]
        st = {"bank": 0, "ring": 0, "stat": 0, "hb": 0, "rl": 0, "xs": 0}
        scr_b = [Buf("scr%d" % i) for i in range(NCHUNK)]
        out_events = []

        allowed = {"banks": list(range(8))}

        bank_age = [0] * 8

        def nextbank():
            a_ = allowed["banks"]
            b = a_[st["bank"] % len(a_)]
            st["bank"] += 1
            st["alloc"] = st.get("alloc", 0) + 1
            bank_age[b] = st["alloc"]
            return b

        def bank_ap(b, n=512):
            return ps[:, b * 512:b * 512 + n]

        def v3(ap, s):
            return ap.rearrange("p (s l) -> p s l", s=s)

        def act(out, in_, func, reads, writes, **kw):
            S.op("act", lambda e: e.activation(out=out, in_=in_, func=func, **kw), reads, writes)

        def tt(eng, out, in0, in1, op, reads, writes):
            S.op(eng, lambda e: e.tensor_tensor(out=out, in0=in0, in1=in1, op=op), reads, writes)

        def ts(eng, out, in0, s1, s2, op0, op1, reads, writes):
            if s2 is None:
                S.op(eng, lambda e: e.tensor_scalar(out=out, in0=in0, scalar1=s1, scalar2=None, op0=op0), reads, writes)
            else:
                S.op(eng, lambda e: e.tensor_scalar(out=out, in0=in0, scalar1=s1, scalar2=s2, op0=op0, op1=op1), reads, writes)

        def stt(out, in0, scalar, in1, op0, op1, reads, writes):
            S.op("dve", lambda e: e.scalar_tensor_tensor(out=out, in0=in0, scalar=scalar, in1=in1, op0=op0, op1=op1), reads, writes)

        def cp(eng, out, in_, reads, writes):
            if eng == "act":
                S.op("act", lambda e: e.copy(out=out, in_=in_), reads, writes)
            else:
                S.op(eng, lambda e: e.tensor_copy(out=out, in_=in_), reads, writes)

        def mset(eng, ap, val, writes):
            S.op(eng, lambda e: e.memset(ap, val), (), writes)

        mset("pool", identf[:], 0.0, [ident_b])
        S.op("pool", lambda e: e.affine_select(out=identf[:], in_=identf[:], compare_op=ALU.not_equal, fill=1.0,
                                               base=0, pattern=[[-1, 128]], channel_multiplier=1), [ident_b], [ident_b])
        cp("pool", identb[:], identf[:], [ident_b], [ident_b])
        w_in_v = w_in.rearrange("(k p) o -> p k o", p=128)
        w_up_v = w_up.rearrange("(k p) o -> p k o", p=128)
        w_out_v = w_out.rearrange("(k p) o -> p k o", p=128)
        w_down_v = w_down.rearrange("(k p) o -> p k o", p=128)
        conv_list = []
        for oc in [12, 13, 14, 15] + [o for o in OC_ORDER if o not in (12, 13, 14, 15)]:
            conv_list.append((POS_OF_OC[oc], "col", w_in_v[:, :, oc * 128:(oc + 1) * 128]))
        for kc in range(8):
            conv_list.append((CH_OUT + kc, "row", w_out_v[:, kc, :]))
        for fc in range(32):
            conv_list.append((CH_UP + fc, "col", w_up_v[:, :, fc * 128:(fc + 1) * 128]))
        for kc in range(32):
            conv_list.append((CH_DOWN + kc, "row", w_down_v[:, kc, :]))
        cvp = {"i": 0}

        def convert_some(n):
            for _ in range(n):
                if cvp["i"] >= len(conv_list):
                    return
                chunk, kind, src = conv_list[cvp["i"]]
                cvp["i"] += 1
                dst = wscr[chunk].rearrange("p (k m) -> p k m", k=8) if kind == "col" else wscr[chunk]
                S.dma("pool", dst, src, [], [scr_b[chunk]], "cv")

        convert_some(12)

        cst_b = Buf("cst")
        mset("pool", cst[:, 0:1], 1.0, [cst_b])
        mset("pool", cst[:, 1:2], EPS, [cst_b])
        wbdf_t = [(T["r"][0], TB["r"][0]), (T["i"][0], TB["i"][0])]
        for t_, b_ in wbdf_t:
            mset("pool", t_[:], 0.0, [b_])
        for t_, bl in ((pbuf_p, pb_p_b), (xbuf_p, xb_p_b), (h0_p, h0_p_b)):
            mset("pool", t_[:], 0.0, bl)

        def pdma(out, in_, writes, **kw):
            S.dma("act", out, in_, [], writes, "par", allow_slow_non_contiguous=True, **kw)

        flag_b = Buf("flag")
        pdma(flag[:], flag_d, [flag_b])
        pst, pst_b = T["om"][0], TB["om"][0]
        rows = [(0, 8, g1_d.rearrange("(c p) -> c p", p=128)), (8, 8, g2_d.rearrange("(c p) -> c p", p=128)),
                (16, 16, lcw_d.rearrange("k (c p) -> (k c) p", p=128)), (32, 4, lcb_d.rearrange("(c p) -> c p", p=128)),
                (36, 12, caw_d.rearrange("k (c p) -> (k c) p", p=128)), (48, 4, ba_d.rearrange("(c p) -> c p", p=128)),
                (52, 4, bx_d.rearrange("(c p) -> c p", p=128)), (56, 4, apar_d.rearrange("(c p) -> c p", p=128))]
        pst_bs = [Buf("pst%d" % i) for i in range(len(rows))]
        for (r0, nr_, src), b_ in zip(rows, pst_bs):
            pdma(pst[r0:r0 + nr_, 0:128], src, [b_])
        bpar = nextbank()
        S.op("pe", lambda e: e.transpose(out=bank_ap(bpar, 60), in_=pst[0:60, 0:128], identity=identf[0:60, 0:60]),
             pst_bs + [ident_b], [bank_b[bpar], pst_b])
        for t_, c0, n_ in ((g1t, 0, 8), (g2t, 8, 8), (lcw, 16, 16), (lcb, 32, 4), (caw, 36, 12), (nba, 48, 4), (nbx, 52, 4),
                           (apar, 56, 4)):
            cp("dve", t_[:], ps[:, bpar * 512 + c0:bpar * 512 + c0 + n_], [bank_b[bpar]], [par_b])
        pdma(gfb[:], gf_d.partition_broadcast(128), [gfb_b])
        wb_bs = [[Buf("wbdf%d%d" % (gi, par)) for par in range(2)] for gi in range(2)]
        for gi, wd in enumerate((wa_d, wx_d)):
            wbdf3 = wbdf_t[gi][0][:].rearrange("p (g j) -> p g j", j=128)
            for par in range(2):
                src = wd.rearrange("(c two) i j -> two i c j", two=2)[par]
                S.dma("act", wbdf3[par * 64:(par + 1) * 64, :, par * 64:(par + 1) * 64], src, [wbdf_t[gi][1]], [wb_bs[gi][par]],
                      "par", allow_slow_non_contiguous=True)
        stin_bs = [Buf("stin%d" % i) for i in range(3)]
        pdma(stin[0:8, :], sca, [stin_bs[0]])
        pdma(stin[8:20, :], slc, [stin_bs[1]])
        pdma(stin[20:24, :], slh, [stin_bs[2]])
        for gi in range(2):
            cp("pool", wbd[:, gi * 512:(gi + 1) * 512], wbdf_t[gi][0][:], [wbdf_t[gi][1]] + wb_bs[gi], [wbd_b])
        ts("pool", nba[:], nba[:], -1.0, None, ALU.mult, None, [par_b], [par_b])
        ts("pool", nbx[:], nbx[:], -1.0, None, ALU.mult, None, [par_b], [par_b])
        sp_abs, sp_e, sp_u, sp_y = spt[:, 0:4], spt[:, 4:8], spt[:, 8:12], spt[:, 12:16]
        ts("dve", sp_e, apar[:], -1.0, None, ALU.mult, None, [par_b], [par_b])
        tt("dve", sp_abs, apar[:], sp_e, ALU.max, [par_b], [par_b])
        act(sp_e, sp_abs, AF.Exp, [par_b], [par_b], scale=-1.0)
        ts("dve", sp_u, sp_e, 1.0, None, ALU.add, None, [par_b], [par_b])
        act(sp_y, sp_u, AF.Ln, [par_b], [par_b])
        act(sp_abs, sp_y, AF.Exp, [par_b], [par_b], scale=-1.0)
        tt("dve", sp_abs, sp_abs, sp_u, ALU.mult, [par_b], [par_b])
        tt("dve", sp_y, sp_y, sp_abs, ALU.add, [par_b], [par_b])
        ts("dve", sp_y, sp_y, -1.0, None, ALU.add, None, [par_b], [par_b])
        ts("dve", sp_e, apar[:], 0.0, None, ALU.max, None, [par_b], [par_b])
        tt("dve", sp_y, sp_y, sp_e, ALU.add, [par_b], [par_b])
        ts("dve", nsp[:], sp_y, -8.0, None, ALU.mult, None, [par_b], [par_b])
        ts("dve", nsp2[:], sp_y, -16.0, None, ALU.mult, None, [par_b, cst_b, flag_b], [par_b])

        stT3 = stT[:].rearrange("p (c r) -> p c r", r=24)
        for c in range(4):
            b = nextbank()
            S.op("pe", lambda e, c=c, b=b: e.transpose(out=bank_ap(b, 24), in_=stin[0:24, c * 128:(c + 1) * 128],
                                                      identity=identf[0:24, 0:24]), stin_bs + [ident_b], [bank_b[b], stin_b])
            cp("dve", stT3[:, c, :], bank_ap(b, 24), [bank_b[b]], [stT_b])
        pbs4 = pbuf_s[:].rearrange("p (c s l) -> p c s l", c=4, s=4)
        xbs4 = xbuf_s[:].rearrange("p (c s l) -> p c s l", c=4, s=4)
        for c in range(4):
            cp("pool", pbs4[:, c, :, 0:2], stT3[:, c, 0:8].rearrange("p (s k) -> p s k", k=2), [stT_b], [pb_s_b[c]])
            cp("pool", xbs4[:, c, :, 0:3], stT3[:, c, 8:20].rearrange("p (s k) -> p s k", k=3), [stT_b], [xb_s_b[c]])
            cp("pool", h0_s[:, c * 4:(c + 1) * 4], stT3[:, c, 20:24], [stT_b], [h0_s_b[c]])

        def ring_load(chunk):
            s_ = st["ring"] % R
            st["ring"] += 1
            S.dma("sp", ring[:, s_ * 1024:(s_ + 1) * 1024], wscr[chunk], [scr_b[chunk]], [ring_b[s_]], "ring")
            return ring[:, s_ * 1024:(s_ + 1) * 1024], ring_b[s_]

        def load_x(src, row0, J):
            res = []
            for j in range(J):
                k = st["xs"] % 8
                st["xs"] += 1
                ap = xsl[:, k * D:(k + 1) * D]
                S.dma("sp", ap, src[row0 + j * 128:row0 + (j + 1) * 128, :], [], [xsl_b[k]], "x")
                res.append((ap, xsl_b[k]))
            return res

        def rstd_of(x_ap, x_buf):
            k = st["stat"] % 8
            st["stat"] += 1
            sbf = stat_b[k]
            ss, ln, rs = stat[:, 4 * k:4 * k + 1], stat[:, 4 * k + 1:4 * k + 2], stat[:, 4 * k + 2:4 * k + 3]
            kh = st["hb"] % 2
            st["hb"] += 1
            act(hb[:, kh * D:(kh + 1) * D], x_ap, AF.Square, [x_buf], [sbf, hb_b[kh]], accum_out=ss)
            act(ln, ss, AF.Ln, [sbf, par_b], [sbf], scale=1.0 / D, bias=cst[:, 1:2])
            act(rs, ln, AF.Exp, [sbf], [sbf], scale=-0.5)
            return rs, sbf, kh

        def norm_front(xs_list, j):
            x_ap, x_buf = xs_list[j]
            rs, sbf, k = rstd_of(x_ap, x_buf)
            hb_ap = hb[:, k * D:(k + 1) * D]
            ts("dve", hb_ap, x_ap, rs, None, ALU.mult, None, [x_buf, sbf], [hb_b[k]])
            return k

        def norm_back(k, gt, hi, j):
            hT3 = hTs[hi][:].rearrange("p (k t) -> p k t", k=8)
            gbc = gt[:, :, None].to_broadcast([128, 8, 128])
            hb_ap = hb[:, k * D:(k + 1) * D]
            b = nextbank()
            pb = bank_ap(b).bitcast(BF16)
            S.group("pe", [lambda e, kc=kc, hb_ap=hb_ap, pb=pb: e.transpose(
                out=pb[:, kc * 128:(kc + 1) * 128], in_=hb_ap[:, kc * 128:(kc + 1) * 128], identity=identb[:])
                for kc in range(8)], [hb_b[k], ident_b], [bank_b[b]])
            tt("dve", hT3[:, :, j * 128:(j + 1) * 128], pb.rearrange("p (k t) -> p k t", k=8), gbc, ALU.mult,
               [bank_b[b], par_b], [hT_bs[hi]])

        def norm_sub(xs_list, gt, hi, j):
            norm_back(norm_front(xs_list, j), gt, hi, j)

        def norm_T(xs_list, gt, hi):
            for j in range(len(xs_list)):
                norm_sub(xs_list, gt, hi, j)

        def proj(oc, N, hi, col0=0):
            slot, sbuf_ = ring_load(POS_OF_OC[oc])
            b = nextbank()
            S.group("pe", [lambda e, kc=kc, slot=slot, b=b: e.matmul(
                out=bank_ap(b, N), lhsT=slot[:, kc * 128:(kc + 1) * 128],
                rhs=hTs[hi][:, kc * 512 + col0:kc * 512 + col0 + N], start=(kc == 0), stop=(kc == 7))
                for kc in range(8)], [sbuf_, hT_bs[hi]], [bank_b[b]])
            return b

        def sigmoid_chain(buf_ap, bufs):
            act(buf_ap, buf_ap, AF.Ln, bufs + [par_b], bufs, bias=cst[:, 0:1])
            act(buf_ap, buf_ap, AF.Exp, bufs, bufs, scale=-1.0)

        class BCtx:
            pass

        def mixer_b_p1(c, N, Sq, L, xb_t, xb_bufs, h0_t, h0_bufs, lean, fix, so3, so_buf, hi):
            k = BCtx()
            k.c, k.N, k.Sq, k.L, k.lean, k.fix, k.so3, k.so_buf = c, N, Sq, L, lean, fix, so3, so_buf
            k.h0_t, k.h0_bufs = h0_t, h0_bufs
            q = c % 2
            k.T = {n: T[n][q] for n in tnames}
            k.TB = {n: TB[n][q] for n in tnames}
            if c >= 2:
                k.T["xc"], k.TB["xc"] = Txc[c - 2], TBxc[c - 2]
            k.T["gl"], k.TB["gl"] = Tgl[c % 3], TBgl[c % 3]
            k.xcb, k.xcb_b = xcbs[q], xcb_bs[q]
            W = L + 3
            k.xb3 = xb_t[:, c * Sq * W:(c + 1) * Sq * W].rearrange("p (s l) -> p s l", s=Sq)
            k.xbb = xb_bufs[c]
            T_, TB_, xb3, xbb = k.T, k.TB, k.xb3, k.xbb
            bL = proj(12 + c, N, hi)
            k.bG = None if lean else proj(16 + c, N, hi)
            cp("dve", xb3[:, :, 3:3 + L], v3(bank_ap(bL, N), Sq), [bank_b[bL]], [xbb])
            if not lean:
                cp("act", T_["gl"][:, 0:N], bank_ap(k.bG, N), [bank_b[k.bG]], [TB_["gl"]])
                act(T_["x2"][:, 0:N], T_["gl"][:, 0:N], AF.Square, [TB_["gl"]], [TB_["x2"]], scale=GK1 ** 0.5)
            xc3 = v3(T_["xc"][:, 0:N], Sq)
            ts("dve", xc3, xb3[:, :, 0:L], lcw[:, c:c + 1], lcb[:, c:c + 1], ALU.mult, ALU.add,
               [xbb, par_b], [TB_["xc"]])
            for kk in range(1, 4):
                stt(xc3, xb3[:, :, kk:kk + L], lcw[:, 4 * kk + c:4 * kk + c + 1], xc3, ALU.mult, ALU.add,
                    [xbb, par_b, TB_["xc"]], [TB_["xc"]])
            cp("dve", k.xcb[:, 0:N], T_["xc"][:, 0:N], [TB_["xc"]], [k.xcb_b])
            if not lean:
                stt(T_["x2"][:, 0:N], T_["x2"][:, 0:N], GK0, T_["gl"][:, 0:N], ALU.add, ALU.mult,
                    [TB_["x2"], TB_["gl"]], [TB_["x2"]])
            if so3 is not None:
                cp("pool", so3[:, c, 2 * Sq:5 * Sq].rearrange("p (s k) -> p s k", k=3), xb3[:, :, L:L + 3], [xbb], [so_buf])
            if Sq == 1:
                cp("pool", xb3[:, :, 0:3], xb3[:, :, L:L + 3], [xbb], [xbb])
            if fix == "mask":
                ts("dve", xb3[:, :, 0:3], xb3[:, :, 0:3], flag[:, 0:1], None, ALU.mult, None, [xbb, par_b], [xbb])
            return k

        def mixer_b_p2(k):
            c, N, Sq, L, T_, TB_ = k.c, k.N, k.Sq, k.L, k.T, k.TB
            bRa, bRi = nextbank(), nextbank()
            for gi, b in ((0, bRa), (1, bRi)):
                S.op("pe", lambda e, gi=gi, b=b: e.matmul(out=bank_ap(b, N), lhsT=wbd[:, (gi * 4 + c) * 128:(gi * 4 + c + 1) * 128],
                                                          rhs=k.xcb[:, 0:N], start=True, stop=True), [wbd_b, k.xcb_b], [bank_b[b]])
            r, i_, a, om, ixc = (T_[n][:, 0:N] for n in ("r", "i", "a", "om", "ixc"))
            q = c % 2
            nsig = 2 if k.lean else 3
            sig3 = rig[q][:].rearrange("p (s l) -> p s l", s=3)[:, 0:nsig, 0:N]
            sig_bufs = [TB_["r"], TB_["i"]] + ([] if k.lean else [TB_["x2"]])
            act(r, bank_ap(bRa, N), AF.Exp, [bank_b[bRa], par_b], [TB_["r"]], scale=-1.0, bias=nba[:, c:c + 1])
            act(i_, bank_ap(bRi, N), AF.Exp, [bank_b[bRi], par_b], [TB_["i"]], scale=-1.0, bias=nbx[:, c:c + 1])
            if not k.lean:
                x2, gl = T_["x2"][:, 0:N], T_["gl"][:, 0:N]
                act(x2, x2, AF.Exp, [TB_["x2"]], [TB_["x2"]], scale=-1.0)
            act(sig3, sig3, AF.Ln, sig_bufs + [par_b], sig_bufs, bias=cst[:, 0:1])
            act(sig3, sig3, AF.Exp, sig_bufs, sig_bufs, scale=-1.0)
            act(a, r, AF.Exp, [TB_["r"], par_b], [TB_["a"]], scale=nsp[:, c:c + 1])
            act(om, r, AF.Exp, [TB_["r"], par_b], [TB_["om"]], scale=nsp2[:, c:c + 1])
            if k.fix == "one":
                mset("pool", T_["om"][:, 0:1], 0.0, [TB_["om"]])
            elif k.fix == "flag":
                ts("dve", T_["om"][:, 0:1], T_["om"][:, 0:1], flag[:, 0:1], None, ALU.mult, None,
                   [TB_["om"], par_b], [TB_["om"]])
            tt("pool", ixc, i_, T_["xc"][:, 0:N], ALU.mult, [TB_["i"], TB_["xc"]], [TB_["ixc"]])
            act(om, om, AF.Ln, [TB_["om"], par_b], [TB_["om"]], scale=-1.0, bias=cst[:, 0:1])
            act(om, om, AF.Exp, [TB_["om"]], [TB_["om"]], scale=0.5)
            tt("pool", ixc, om, ixc, ALU.mult, [TB_["om"], TB_["ixc"]], [TB_["ixc"]])
            if not k.lean:
                tt("pool", gl, gl, x2, ALU.mult, [TB_["gl"], TB_["x2"]], [TB_["gl"]])

        def mixer_b_p3(k):
            c, N, Sq, L, T_, TB_, h0_t, h0_bufs = k.c, k.N, k.Sq, k.L, k.T, k.TB, k.h0_t, k.h0_bufs
            for s_ in range(Sq):
                S.op("dve", lambda e, s_=s_: e.tensor_tensor_scan(
                    out=T_["h"][:, s_ * L:(s_ + 1) * L], data0=T_["a"][:, s_ * L:(s_ + 1) * L],
                    data1=T_["ixc"][:, s_ * L:(s_ + 1) * L], initial=h0_t[:, c * Sq + s_:c * Sq + s_ + 1],
                    op0=ALU.mult, op1=ALU.add), [TB_["a"], TB_["ixc"], h0_bufs[c]], [TB_["h"]])
            h = T_["h"][:, 0:N]
            h3 = v3(h, Sq)
            cp("pool", h0_t[:, c * Sq:(c + 1) * Sq], h3[:, :, L - 1], [TB_["h"]], [h0_bufs[c]])
            if k.fix == "mask":
                ts("dve", h0_t[:, c:c + 1], h0_t[:, c:c + 1], flag[:, 0:1], None, ALU.mult, None,
                   [h0_bufs[c], par_b], [h0_bufs[c]])
            if k.so3 is not None:
                cp("pool", k.so3[:, c, 5 * Sq:6 * Sq], h3[:, :, L - 1], [TB_["h"]], [k.so_buf])
            if not k.lean:
                tt("dve", mixT[:, (4 + c) * 512:(4 + c) * 512 + N], h, T_["gl"][:, 0:N], ALU.mult,
                   [TB_["h"], TB_["gl"]], [mixT_bs[4 + c]])

        def mixers_b_gen(args, hooks=(), pre=None):
            def hook(i):
                if i < len(hooks) and hooks[i] is not None:
                    hooks[i]()
            ks = {}
            ks[0] = mixer_b_p1(0, *args)
            if pre is not None:
                pre()
            yield 1
            ks[1] = mixer_b_p1(1, *args)
            hook(0)
            yield 2
            mixer_b_p2(ks[0])
            yield 3
            ks[2] = mixer_b_p1(2, *args)
            yield 4
            hook(1)
            mixer_b_p2(ks[1])
            mixer_b_p3(ks[0])
            yield 5
            ks[3] = mixer_b_p1(3, *args)
            hook(2)
            yield 6
            mixer_b_p2(ks[2])
            mixer_b_p3(ks[1])
            yield 7
            hook(3)
            mixer_b_p2(ks[3])
            mixer_b_p3(ks[2])
            hook(4)
            mixer_b_p3(ks[3])

        def mixers_b(args, hooks=(), pre=None):
            for _ in mixers_b_gen(args, hooks, pre):
                pass

        def mixer_a(c, N, Sq, L, pb_t, pb_bufs, so3, so_buf, hi):
            q = c % 2
            T_ = {n: T[n][q] for n in tnames}
            TB_ = {n: TB[n][q] for n in tnames}
            W = L + 2
            pb3 = pb_t[:, c * Sq * W:(c + 1) * Sq * W].rearrange("p (s l) -> p s l", s=Sq)
            pbb = pb_bufs[c]
            bX = proj(8 + c, N, hi)
            bC = proj(4 + c, N, hi)
            bB = proj(c, N, hi)
            tA, ya = T_["tA"][:, 0:N], T_["ya"][:, 0:N]
            cp("act", tA, bank_ap(bX, N), [bank_b[bX]], [TB_["tA"]])
            tt("dve", pb3[:, :, 2:2 + L], v3(bank_ap(bC, N), Sq), v3(tA, Sq), ALU.mult, [bank_b[bC], TB_["tA"]], [pbb])
            ya3 = v3(ya, Sq)
            ts("dve", ya3, pb3[:, :, 0:L], caw[:, c:c + 1], None, ALU.mult, None, [pbb, par_b], [TB_["ya"]])
            for k in (1, 2):
                stt(ya3, pb3[:, :, k:k + L], caw[:, 4 * k + c:4 * k + c + 1], ya3, ALU.mult, ALU.add,
                    [pbb, par_b, TB_["ya"]], [TB_["ya"]])
            tt("dve", mixT[:, c * 512:c * 512 + N], bank_ap(bB, N), ya, ALU.mult, [bank_b[bB], TB_["ya"]], [mixT_bs[c]])
            if so3 is not None:
                cp("pool", so3[:, c, 0:2 * Sq].rearrange("p (s k) -> p s k", k=2), pb3[:, :, L:L + 2], [pbb], [so_buf])
            if Sq == 1:
                cp("pool", pb3[:, :, 0:2], pb3[:, :, L:L + 2], [pbb], [pbb])

        def epilogue_parts(xs_list, ydst, yrow0):
            J = len(xs_list)
            keep = {}

            def front(j):
                x_ap, x_buf = xs_list[j]
                keep[j] = rstd_of(x_ap, x_buf)

            def back(j):
                x_ap, x_buf = xs_list[j]
                rs, sbf, _ = keep[j]
                stt(x_ap, x_ap, rs, gfb[:], ALU.mult, ALU.mult, [x_buf, sbf, gfb_b], [x_buf])
                ev = S.dma("pool", ydst[yrow0 + j * 128:yrow0 + (j + 1) * 128, :], x_ap, [x_buf], [], "st")
                out_events.append(ev)

            def stage(i):
                if i < J:
                    front(i)
                if 1 <= i <= J:
                    back(i - 1)
            return [(lambda i=i: stage(i)) for i in range(5)]

        def make_mix_gen(N, Sq, L, pb_t, pb_bufs, xb_t, xb_bufs, h0_t, h0_bufs, fix, so3, so_buf, prev_epi, extra0=None,
                         defer_epi=False):
            pe_ = list(prev_epi) if prev_epi else [None] * 5
            if defer_epi:
                def h3():
                    for i in (0, 1, 2):
                        pe_[i]()
                    mixer_a(2, N, Sq, L, pb_t, pb_bufs, so3, so_buf, 0)

                def h4():
                    for i in (3, 4):
                        pe_[i]()
                    mixer_a(3, N, Sq, L, pb_t, pb_bufs, so3, so_buf, 0)
                return mixers_b_gen((N, Sq, L, xb_t, xb_bufs, h0_t, h0_bufs, False, fix, so3, so_buf, 0),
                                    [None, lambda: mixer_a(0, N, Sq, L, pb_t, pb_bufs, so3, so_buf, 0),
                                     lambda: mixer_a(1, N, Sq, L, pb_t, pb_bufs, so3, so_buf, 0), h3, h4], pre=None)

            def mk(i, fn2):
                def f():
                    if extra0 is not None:
                        extra0()
                    if pe_[i] is not None:
                        pe_[i]()
                    if fn2 is not None:
                        fn2()
                return f
            return mixers_b_gen((N, Sq, L, xb_t, xb_bufs, h0_t, h0_bufs, False, fix, so3, so_buf, 0),
                                [mk(1, None), mk(2, None), mk(3, None),
                                 mk(4, lambda: mixer_a(0, N, Sq, L, pb_t, pb_bufs, so3, so_buf, 0)),
                                 lambda: mixer_a(1, N, Sq, L, pb_t, pb_bufs, so3, so_buf, 0)], pre=pe_[0])

        def main_tile(xs_list, N, Sq, L, ydst, yrow0, pb_t, pb_bufs, xb_t, xb_bufs, h0_t, h0_bufs, fix, so3, so_buf,
                      mid_hook, start_hook, prev_epi, extra0=None, cur_gen=None, next_factory=None, after_mix=None):
            J = N // 128
            my_epi = epilogue_parts(xs_list, ydst, yrow0)
            was_prerun = cur_gen is not None
            if cur_gen is None:
                cur_gen = make_mix_gen(N, Sq, L, pb_t, pb_bufs, xb_t, xb_bufs, h0_t, h0_bufs, fix, so3, so_buf, prev_epi, extra0)
            for _ in cur_gen:
                pass
            wslots = [ring_load(CH_OUT + kc) for kc in range(8)]
            if not was_prerun:
                for c in (2, 3):
                    mixer_a(c, N, Sq, L, pb_t, pb_bufs, so3, so_buf, 0)
            if after_mix is not None:
                after_mix()
            P1K = (0, 1, 2, 4, 5, 6)
            for b in sorted(range(2 * J), key=lambda b_: bank_age[b_]):
                j, half = b // 2, b % 2
                S.group("pe", [lambda e, kc=kc, j=j, half=half, b=b: e.matmul(
                    out=bank_ap(b), lhsT=mixT[:, kc * 512 + j * 128:kc * 512 + (j + 1) * 128],
                    rhs=wslots[kc][0][:, half * 512:(half + 1) * 512], start=(kc == 0), stop=False)
                    for kc in P1K], [mixT_bs[kc] for kc in P1K] + [wslots[kc][1] for kc in P1K], [bank_b[b]])
            st["bank"] = 2 * J
            for j in range(J):
                for half in range(2):
                    b = 2 * j + half
                    S.group("pe", [lambda e, kc=kc, j=j, half=half, b=b: e.matmul(
                        out=bank_ap(b), lhsT=mixT[:, kc * 512 + j * 128:kc * 512 + (j + 1) * 128],
                        rhs=wslots[kc][0][:, half * 512:(half + 1) * 512], start=False, stop=(kc == 7))
                        for kc in (3, 7)], [mixT_bs[3], mixT_bs[7], wslots[3][1], wslots[7][1]], [bank_b[b]])
            for j in range(J):
                x_ap, x_buf = xs_list[j]
                tt("dve", x_ap, ps[:, 2 * j * 512:(2 * j + 2) * 512], x_ap, ALU.add,
                   [bank_b[2 * j], bank_b[2 * j + 1], x_buf], [x_buf])
            fr = {}
            for j in range(J):
                fr[j] = norm_front(xs_list, j)
                if j >= 1:
                    norm_back(fr[j - 1], g2t, 1, j - 1)
            norm_back(fr[J - 1], g2t, 1, J - 1)
            if start_hook is not None:
                start_hook()
            for fc in range(32):
                if mid_hook is not None and 8 <= fc <= 16 and fc % 2 == 0:
                    mid_hook((fc - 8) // 2)
                slot, sbuf_ = ring_load(CH_UP + fc)
                b = nextbank()
                S.group("pe", [lambda e, kc=kc, slot=slot, b=b: e.matmul(
                    out=bank_ap(b, N), lhsT=slot[:, kc * 128:(kc + 1) * 128], rhs=hTs[1][:, kc * 512:kc * 512 + N],
                    start=(kc == 0), stop=(kc == 7)) for kc in range(8)], [sbuf_, hT_bs[1]], [bank_b[b]])
                k = st["rl"] % 2
                st["rl"] += 1
                rl_ap = rl[:, k * 512:k * 512 + N]
                act(rl_ap, bank_ap(b, N), AF.Relu, [bank_b[b]], [rl_b[k]])
                tt("dve" if fc % 2 == 0 else "pool", uT[:, fc * 512:fc * 512 + N], rl_ap, rl_ap, ALU.mult, [rl_b[k]], [uT_b[fc]])
            def down_pass(js):
                for kc in range(32):
                    slot, sbuf_ = ring_load(CH_DOWN + kc)
                    fns = []
                    for j in js:
                        for half in range(2):
                            fns.append(lambda e, kc=kc, j=j, half=half, slot=slot: e.matmul(
                                out=bank_ap(2 * j + half), lhsT=uT[:, kc * 512 + j * 128:kc * 512 + (j + 1) * 128],
                                rhs=slot[:, half * 512:(half + 1) * 512], start=(kc == 0), stop=(kc == 31)))
                    S.group("pe", fns, [sbuf_, uT_b[kc]], [bank_b[2 * j + h_] for j in js for h_ in range(2)])
                    yield kc

            def down_adds(js):
                for j in js:
                    x_ap, x_buf = xs_list[j]
                    tt("dve", x_ap, ps[:, 2 * j * 512:(2 * j + 2) * 512], x_ap, ALU.add,
                       [bank_b[2 * j], bank_b[2 * j + 1], x_buf], [x_buf])

            next_gen = None
            if next_factory is not None and J == 4:
                next_gen = next_factory(my_epi)
                allowed["banks"] = [4, 5, 6, 7]
                st["bank"] = 0
                nsteps = 0
                for pi, js in enumerate(((0, 1), (2, 3))):
                    for kc in range(32):
                        slot, sbuf_ = ring_load(CH_DOWN + kc)
                        fns = []
                        for jj, j in enumerate(js):
                            for half in range(2):
                                fns.append(lambda e, kc=kc, j=j, jj=jj, half=half, slot=slot: e.matmul(
                                    out=bank_ap(2 * jj + half), lhsT=uT[:, kc * 512 + j * 128:kc * 512 + (j + 1) * 128],
                                    rhs=slot[:, half * 512:(half + 1) * 512], start=(kc == 0), stop=(kc == 31)))
                        S.group("pe", fns, [sbuf_, uT_b[kc]], [bank_b[i] for i in range(4)])
                        if kc in ((5, 13, 21) if pi == 0 else (8, 20)) and nsteps < 7:
                            next(next_gen)
                            nsteps += 1
                    for jj, j in enumerate(js):
                        x_ap, x_buf = xs_list[j]
                        tt("dve", x_ap, ps[:, 2 * jj * 512:(2 * jj + 2) * 512], x_ap, ALU.add,
                           [bank_b[2 * jj], bank_b[2 * jj + 1], x_buf], [x_buf])
                    if pi == 0:
                        for _ in range(2):
                            next(next_gen)
                            nsteps += 1
                allowed["banks"] = list(range(8))
            else:
                for _ in down_pass(tuple(range(J))):
                    pass
                st["bank"] = 2 * J
                down_adds(tuple(range(J)))
                if next_factory is not None:
                    next_gen = next_factory(my_epi)
            return my_epi, next_gen

        tiles = [("L", xl, lt * 512, 4) for lt in range(4)] + [("M", xp, t * 512, 4) for t in range(4)] + [("M", xs, 0, 2)]
        xs_of = {}

        def load_tile(i):
            if i < len(tiles):
                xs_of[i] = load_x(tiles[i][1], tiles[i][2], tiles[i][3])

        load_tile(0)
        load_tile(1)
        norm_T(xs_of[0], g1t, 0)
        for lt in range(4):
            load_tile(lt + 2)
            hi = lt % 2
            nx, nh = xs_of[lt + 1], (lt + 1) % 2
            kk = {}

            def nstage(i, nx=nx, nh=nh, kk=kk):
                convert_some(2 if cvp["i"] < 52 else 0)
                if i < 4:
                    kk[i] = norm_front(nx, i)
                if 1 <= i <= 4:
                    norm_back(kk[i - 1], g1t, nh, i - 1)
            mixers_b((512, 1, 512, xbuf_p, xb_p_b, h0_p, h0_p_b, True, "one" if lt == 0 else ("mask" if lt == 3 else None), None, None, hi),
                     [(lambda i=i, ns=nstage: ns(i)) for i in range(1, 5)], pre=lambda ns=nstage: ns(0))
            if lt == 3:
                for c in range(4):
                    q = c % 2
                    bC, bX = proj(4 + c, 2, hi, 510), proj(8 + c, 2, hi, 510)
                    cp("act", T["tA"][q][:, 0:2], bank_ap(bX, 2), [bank_b[bX]], [TB["tA"][q]])
                    tt("dve", pbuf_p[:, c * 514:c * 514 + 2], bank_ap(bC, 2), T["tA"][q][:, 0:2], ALU.mult,
                       [bank_b[bC], TB["tA"][q]], [pb_p_b[c]])
        for c in range(4):
            ts("dve", pbuf_p[:, c * 514:c * 514 + 2], pbuf_p[:, c * 514:c * 514 + 2], flag[:, 0:1], None, ALU.mult, None,
               [pb_p_b[c], par_b], [pb_p_b[c]])

        soT_p3 = soT_p[:].rearrange("p (c r) -> p c r", r=6)
        soT_s3 = soT_s[:].rearrange("p (c r) -> p c r", r=24)
        def tile_args(t):
            if t < 4:
                return (512, 1, 512, pbuf_p, pb_p_b, xbuf_p, xb_p_b, h0_p, h0_p_b, "flag" if t == 0 else None,
                        soT_p3 if t == 3 else None, soT_p_b)
            return (256, 4, 64, pbuf_s, pb_s_b, xbuf_s, xb_s_b, h0_s, h0_s_b, None, soT_s3, soT_s_b)

        def emit_state_outputs():
            for (soT3, sbuf_, so_t, so_buf2, nr, dst) in ((soT_p3, soT_p_b, so_p, so_p_b, 6, stp_d),
                                                         (soT_s3, soT_s_b, so_s, so_s_b, 24, sts_d)):
                for c in range(4):
                    b = nextbank()
                    S.op("pe", lambda e, c=c, b=b, soT3=soT3, nr=nr: e.transpose(
                        out=ps[0:nr, b * 512:b * 512 + 128], in_=soT3[:, c, :], identity=identf[:]),
                        [sbuf_, ident_b], [bank_b[b]])
                    cp("dve", so_t[0:nr, c * 128:(c + 1) * 128], ps[0:nr, b * 512:b * 512 + 128], [bank_b[b]], [so_buf2])
                out_events.append(S.dma("sp", dst, so_t[0:nr, :], [so_buf2], [], "sto"))

        epi, gen = None, None
        for t in range(5):
            ti = 4 + t
            mid = None
            if t < 4:
                kk2 = {}

                def mid(i, kk2=kk2, ti=ti, t=t):
                    if t == 0:
                        convert_some(4)
                    nx = xs_of[ti + 1]
                    if i < len(nx):
                        kk2[i] = norm_front(nx, i)
                    if 1 <= i <= len(nx):
                        norm_back(kk2[i - 1], g1t, 0, i - 1)
            end = (lambda ti=ti: load_tile(ti + 1)) if 1 <= t < 4 else None
            N_, Sq_, L_, pbt, pbb_, xbt, xbb_, h0t, h0b, fix_, so3_, sob_ = tile_args(t)
            nf = None
            if t < 4:
                a_ = tile_args(t + 1)
                nf = (lambda epi_, a_=a_: make_mix_gen(a_[0], a_[1], a_[2], a_[3], a_[4], a_[5], a_[6], a_[7], a_[8], a_[9],
                                                       a_[10], a_[11], epi_, None, True))
            epi, gen = main_tile(xs_of[ti], N_, Sq_, L_, yp if t < 4 else ys, (t * 512) if t < 4 else 0,
                                 pbt, pbb_, xbt, xbb_, h0t, h0b, fix_, so3_, sob_, mid, end, epi,
                                 (lambda: convert_some(5)) if t == 0 else None, gen, nf,
                                 emit_state_outputs if t == 4 else None)
        for f_ in epi:
            f_()

        S.wait_events("sp", out_events)
        S.emit()
    return nc


_NC_CACHE = {}


def kernel(x_prompt, x_sample, state_conv_a, state_lru_conv, state_lru_h, norm1_g, w_in, conv_a_w, lru_conv_w,
           lru_conv_b, lru_wa, lru_ba, lru_wx, lru_bx, lru_a_param, w_out, norm2_g, w_up, w_down, norm_f_g):
    f = lambda a: np.ascontiguousarray(np.asarray(a, dtype=np.float32))
    x_prompt, x_sample = f(x_prompt), f(x_sample)
    if "nc" not in _NC_CACHE:
        _NC_CACHE["nc"] = build_program()
    nc = _NC_CACHE["nc"]
    shared = {
        "w_in": f(w_in[0]), "w_out": f(w_out[0]), "w_up": f(w_up[0]), "w_down": f(w_down[0]),
        "g1": f(norm1_g[0]), "g2": f(norm2_g[0]), "gf": f(norm_f_g),
        "caw": f(conv_a_w[0]), "lcw": f(lru_conv_w[0]), "lcb": f(lru_conv_b[0]),
        "wa": f(lru_wa[0]), "wx": f(lru_wx[0]), "ba": f(np.reshape(lru_ba[0], -1)), "bx": f(np.reshape(lru_bx[0], -1)),
        "apar": f(lru_a_param[0]),
    }
    sca, slc, slh = f(state_conv_a[0]), f(state_lru_conv[0]), f(state_lru_h[0])
    in_maps = []
    for c in range(NCORES):
        s, hf = c // 2, c % 2
        flag = np.zeros((128, 2), np.float32)
        flag[:, 0] = float(hf)
        flag[:, 1] = 1.0 - float(hf)
        m = dict(shared)
        m.update({
            "xp": f(x_prompt[s, hf * 2048:(hf + 1) * 2048]),
            "xl": f(x_prompt[s, 0:2048]),
            "xs": f(x_sample[4 * c:4 * c + 4].reshape(256, D)),
            "flag": flag,
            "sca": f(sca[4 * c:4 * c + 4].reshape(8, 512)),
            "slc": f(slc[4 * c:4 * c + 4].reshape(12, 512)),
            "slh": f(slh[4 * c:4 * c + 4].reshape(4, 512)),
        })
        in_maps.append(m)
    res = run_bass_kernel_spmd(nc, in_maps, core_ids=list(range(NCORES)))
    rs = res.results
    y_prompt = np.stack([np.concatenate([rs[2 * s]["yp"], rs[2 * s + 1]["yp"]], axis=0) for s in range(4)], axis=0)
    y_sample = np.concatenate([rs[c]["ys"].reshape(4, 64, D) for c in range(NCORES)], axis=0)
    stp = np.stack([rs[2 * s + 1]["stp"] for s in range(4)], axis=0)
    new_a_p = stp[:, 0:2][None]
    new_l_p = stp[:, 2:5][None]
    new_h_p = stp[:, 5][None]
    sts = np.stack([rs[c]["sts"] for c in range(NCORES)], axis=0)
    new_a_s = sts[:, 0:8].reshape(32, 2, 512)[None]
    new_l_s = sts[:, 8:20].reshape(32, 3, 512)[None]
    new_h_s = sts[:, 20:24].reshape(32, 512)[None]
    out = (y_prompt, y_sample, new_a_p, new_l_p, new_h_p, new_a_s, new_l_s, new_h_s)
    return tuple(np.ascontiguousarray(o, dtype=np.float32) for o in out)
```

```python
import numpy as np
from contextlib import ExitStack
import concourse.bass as bass
import concourse.mybir as mybir
from concourse.bass_utils import run_bass_kernel_spmd

F32, BF16 = mybir.dt.float32, mybir.dt.bfloat16
AF = mybir.ActivationFunctionType
ALU = mybir.AluOpType

NCORES = 8
D = 1024
EPS = 1e-6
R = 15
OC_ORDER = [12, 16, 13, 17, 14, 18, 15, 19, 0, 4, 8, 1, 5, 9, 2, 6, 10, 3, 7, 11]
POS_OF_OC = {oc: i for i, oc in enumerate(OC_ORDER)}
CH_OUT, CH_UP, CH_DOWN, NCHUNK = 20, 28, 60, 92
GK0 = 2.0 * 0.7978845608028654
GK1 = GK0 * 0.044715


class Buf:
    __slots__ = ("name", "w", "rs")

    def __init__(self, name):
        self.name = name
        self.w = None
        self.rs = {}


class Sched:
    ENG = ("pe", "act", "dve", "pool", "sp")

    def __init__(self, nc, es):
        self.nc, self.es = nc, es
        self.sem, self.cnt = {}, {}
        for e in self.ENG[:4]:
            self.sem[e] = es.enter_context(nc.semaphore("s_" + e))
            self.cnt[e] = 0
        self.seen = {e: {} for e in self.ENG}
        self.prog = {e: [] for e in self.ENG}
        self.pools = {}

    def _waits(self, eng, reads, writes, extra=()):
        need = {}

        def add(ev):
            if ev is not None and need.get(ev[0], 0) < ev[1]:
                need[ev[0]] = ev[1]
        for b in reads:
            add(b.w)
        for b in writes:
            add(b.w)
            for k, v in b.rs.items():
                add((k, v))
        for ev in extra:
            add(ev)
        out, seen = [], self.seen[eng]
        for k, v in need.items():
            if k == eng and eng == "pe":
                continue
            if seen.get(k, 0) >= v:
                continue
            seen[k] = v
            out.append((k, v))
        return out

    def _commit(self, ev, reads, writes):
        for b in writes:
            b.w = ev
            b.rs = {}
        for b in reads:
            if b.rs.get(ev[0], 0) < ev[1]:
                b.rs[ev[0]] = ev[1]

    def group(self, eng, fns, reads=(), writes=()):
        waits = self._waits(eng, reads, writes)
        self.cnt[eng] += 1
        ev = (eng, self.cnt[eng])
        self.prog[eng].append((waits, list(fns), (eng, 1)))
        self._commit(ev, reads, writes)
        return ev

    def op(self, eng, fn, reads=(), writes=()):
        return self.group(eng, [fn], reads, writes)

    def dma_pool(self, name, n):
        keys = []
        for i in range(n):
            k = "%s%d" % (name, i)
            self.sem[k] = self.es.enter_context(self.nc.semaphore("d_" + k))
            self.cnt[k] = 0
            keys.append(k)
        self.pools[name] = [keys, 0]

    def dma(self, q, out, in_, reads, writes, pool, **kw):
        p = self.pools[pool]
        k = p[0][p[1] % len(p[0])]
        p[1] += 1
        prev = self.cnt[k]
        waits = self._waits(q, reads, writes, [(k, prev)] if prev else [])
        self.cnt[k] = prev + 16
        ev = (k, prev + 16)
        self.prog[q].append((waits, [lambda e: e.dma_start(out=out, in_=in_, **kw)], (k, 16)))
        self._commit(ev, reads, writes)
        return ev

    def wait_events(self, eng, evs):
        waits = self._waits(eng, (), (), evs)
        self.prog[eng].append((waits, [], None))

    def check_no_deadlock(self):
        val = {k: 0 for k in self.sem}
        pos = {e: 0 for e in self.ENG}
        while True:
            prog = False
            for e in self.ENG:
                q = self.prog[e]
                while pos[e] < len(q):
                    waits, fns, inc = q[pos[e]]
                    if any(val[k] < v for k, v in waits):
                        break
                    if inc is not None:
                        val[inc[0]] += inc[1]
                    pos[e] += 1
                    prog = True
            if not prog:
                break
        stuck = {e: (pos[e], len(self.prog[e])) for e in self.ENG if pos[e] < len(self.prog[e])}
        if stuck:
            msg = []
            for e, (p, n) in stuck.items():
                msg.append("%s at %d/%d waits %s" % (e, p, n, [(k, v, val[k]) for k, v in self.prog[e][p][0] if val[k] < v]))
            raise RuntimeError("static deadlock: " + "; ".join(msg))

    def emit(self):
        self.check_no_deadlock()
        def make(name):
            def f(e):
                for waits, fns, inc in self.prog[name]:
                    for k, v in waits:
                        e.wait_ge(self.sem[k], v)
                    ins = None
                    for fn in fns:
                        ins = fn(e)
                    if inc is not None:
                        ins.then_inc(self.sem[inc[0]], inc[1])
            return f
        with self.nc.Block() as block:
            block.sync(make("sp"))
            block.scalar(make("act"))
            block.vector(make("dve"))
            block.gpsimd(make("pool"))
            block.tensor(make("pe"))


def build_program():
    nc = bass.Bass("TRN2", target_bir_lowering=False)

    def din(name, shape, dt=F32):
        return nc.dram_tensor(name, list(shape), dt, kind="ExternalInput").ap()

    def dout(name, shape, dt=F32):
        return nc.dram_tensor(name, list(shape), dt, kind="ExternalOutput").ap()

    xp, xl, xs = din("xp", [2048, D]), din("xl", [2048, D]), din("xs", [256, D])
    flag_d = din("flag", [128, 2])
    sca, slc, slh = din("sca", [8, 512]), din("slc", [12, 512]), din("slh", [4, 512])
    w_in, w_out = din("w_in", [D, 2560]), din("w_out", [D, D])
    w_up, w_down = din("w_up", [D, 4096]), din("w_down", [4096, D])
    g1_d, g2_d, gf_d = din("g1", [D]), din("g2", [D]), din("gf", [D])
    caw_d, lcw_d, lcb_d = din("caw", [3, 512]), din("lcw", [4, 512]), din("lcb", [512])
    wa_d, wx_d = din("wa", [8, 64, 64]), din("wx", [8, 64, 64])
    ba_d, bx_d, apar_d = din("ba", [512]), din("bx", [512]), din("apar", [512])
    yp, ys = dout("yp", [2048, D]), dout("ys", [256, D])
    stp_d = dout("stp", [6, 512])
    sts_d = dout("sts", [24, 512])
    wscr = nc.dram_tensor("wscr", [NCHUNK, 128, 1024], BF16).ap()

    with ExitStack() as es:
        S = Sched(nc, es)
        S.dma_pool("ring", R)
        S.dma_pool("x", 8)
        S.dma_pool("cv", 40)
        S.dma_pool("par", 8)
        S.dma_pool("st", 8)
        S.dma_pool("sto", 2)

        def sb(name, shape, dt=F32):
            return es.enter_context(nc.sbuf_tensor("sb_" + name, list(shape), dt))

        ring = sb("ring", [128, R * 1024], BF16)
        ring_b = [Buf("ring%d" % i) for i in range(R)]
        xsl = sb("xsl", [128, 8 * D])
        xsl_b = [Buf("xs%d" % i) for i in range(8)]
        hb = sb("hb", [128, 2 * D], BF16)
        hb_b = [Buf("hb0"), Buf("hb1")]
        hTs = [sb("hTa", [128, 8 * 512], BF16), sb("hTb", [128, 8 * 512], BF16)]
        hT_bs = [Buf("hTa"), Buf("hTb")]
        mixT = sb("mixT", [128, 8 * 512], BF16)
        mixT_bs = [Buf("mixT%d" % i) for i in range(8)]
        uT = sb("uT", [128, 32 * 512], BF16)
        uT_b = [Buf("uT%d" % i) for i in range(32)]
        rl = sb("rl", [128, 2 * 512], BF16)
        rl_b = [Buf("rl0"), Buf("rl1")]
        tnames = ["tA", "ya", "xc", "r", "i", "a", "om", "ixc", "h", "x2", "gl"]
        T, TB = {}, {}
        rig = [sb("t_rig%d" % q, [128, 3 * 512]) for q in range(2)]
        for n in tnames:
            cnt_ = 1 if n in ("tA", "ya") else 2
            if n in ("r", "i", "x2"):
                o_ = ("r", "i", "x2").index(n) * 512
                T[n] = [rig[q][:, o_:o_ + 512] for q in range(2)]
            else:
                T[n] = [sb("t_%s%d" % (n, q), [128, 512]) for q in range(cnt_)]
            TB[n] = [Buf("t_%s%d" % (n, q)) for q in range(cnt_)]
            if cnt_ == 1:
                T[n].append(T[n][0])
                TB[n].append(TB[n][0])
        Tgl = T["gl"] + [sb("t_gl2", [128, 512])]
        TBgl = TB["gl"] + [Buf("t_gl2")]
        Txc = [sb("t_xc_c%d" % c_, [128, 512]) for c_ in range(2, 4)]
        TBxc = [Buf("t_xc_c%d" % c_) for c_ in range(2, 4)]
        xcbs = [sb("xcb%d" % q, [128, 512], BF16) for q in range(2)]
        xcb_bs = [Buf("xcb0"), Buf("xcb1")]
        pbuf_p = sb("pbuf_p", [128, 4 * 514]); pbuf_s = sb("pbuf_s", [128, 4 * 4 * 66])
        xbuf_p = sb("xbuf_p", [128, 4 * 515]); xbuf_s = sb("xbuf_s", [128, 4 * 4 * 67])
        pb_p_b = [Buf("pbp%d" % c) for c in range(4)]; pb_s_b = [Buf("pbs%d" % c) for c in range(4)]
        xb_p_b = [Buf("xbp%d" % c) for c in range(4)]; xb_s_b = [Buf("xbs%d" % c) for c in range(4)]
        h0_p = sb("h0_p", [128, 4]); h0_s = sb("h0_s", [128, 16])
        h0_p_b = [Buf("h0p%d" % c) for c in range(4)]; h0_s_b = [Buf("h0s%d" % c) for c in range(4)]
        identb = sb("identb", [128, 128], BF16); identf = sb("identf", [128, 128])
        ident_b = Buf("ident")
        gfb = sb("gfb", [128, D]); gfb_b = Buf("gfb")
        wbd = sb("wbd", [128, 8 * 128], BF16)
        wbd_b = Buf("wbd")
        caw = sb("caw", [128, 12]); lcw = sb("lcw", [128, 16]); lcb = sb("lcb", [128, 4])
        nba = sb("nba", [128, 4]); nbx = sb("nbx", [128, 4]); apar = sb("apar", [128, 4])
        nsp = sb("nsp", [128, 4]); nsp2 = sb("nsp2", [128, 4]); spt = sb("spt", [128, 16])
        g1t = sb("g1t", [128, 8]); g2t = sb("g2t", [128, 8]); flag = sb("flagt", [128, 2])
        cst = sb("cst", [128, 4])
        par_b = Buf("params")
        stat = sb("stat", [128, 8 * 4])
        stat_b = [Buf("stat%d" % i) for i in range(8)]
        stin = T["a"][0]; stin_b = TB["a"][0]
        stT = sb("stT", [128, 4 * 24]); stT_b = Buf("stT")
        soT_p = sb("soT_p", [128, 4 * 6]); soT_s = sb("soT_s", [128, 4 * 24])
        soT_p_b = Buf("soT_p"); soT_s_b = Buf("soT_s")
        so_p = T["r"][1]; so_s = T["i"][1]
        so_p_b = TB["r"][1]; so_s_b = TB["i"][1]
        ps = es.enter_context(nc.psum_tensor("ps", [128, 8 * 512], F32))
        bank_b = [Buf("bank%d" % i) for i in range(8)]
        st = {"bank": 0, "ring": 0, "stat": 0, "hb": 0, "rl": 0, "xs": 0}
        scr_b = [Buf("scr%d" % i) for i in range(NCHUNK)]
        out_events = []

        allowed = {"banks": list(range(8))}

        bank_age = [0] * 8

        def nextbank():
            a_ = allowed["banks"]
            b = a_[st["bank"] % len(a_)]
            st["bank"] += 1
            st["alloc"] = st.get("alloc", 0) + 1
            bank_age[b] = st["alloc"]
            return b

        def bank_ap(b, n=512):
            return ps[:, b * 512:b * 512 + n]

        def v3(ap, s):
            return ap.rearrange("p (s l) -> p s l", s=s)

        def act(out, in_, func, reads, writes, **kw):
            S.op("act", lambda e: e.activation(out=out, in_=in_, func=func, **kw), reads, writes)

        def tt(eng, out, in0, in1, op, reads, writes):
            S.op(eng, lambda e: e.tensor_tensor(out=out, in0=in0, in1=in1, op=op), reads, writes)

        def ts(eng, out, in0, s1, s2, op0, op1, reads, writes):
            if s2 is None:
                S.op(eng, lambda e: e.tensor_scalar(out=out, in0=in0, scalar1=s1, scalar2=None, op0=op0), reads, writes)
            else:
                S.op(eng, lambda e: e.tensor_scalar(out=out, in0=in0, scalar1=s1, scalar2=s2, op0=op0, op1=op1), reads, writes)

        def stt(out, in0, scalar, in1, op0, op1, reads, writes):
            S.op("dve", lambda e: e.scalar_tensor_tensor(out=out, in0=in0, scalar=scalar, in1=in1, op0=op0, op1=op1), reads, writes)

        def cp(eng, out, in_, reads, writes):
            if eng == "act":
                S.op("act", lambda e: e.copy(out=out, in_=in_), reads, writes)
            else:
                S.op(eng, lambda e: e.tensor_copy(out=out, in_=in_), reads, writes)

        def mset(eng, ap, val, writes):
            S.op(eng, lambda e: e.memset(ap, val), (), writes)

        mset("pool", identf[:], 0.0, [ident_b])
        S.op("pool", lambda e: e.affine_select(out=identf[:], in_=identf[:], compare_op=ALU.not_equal, fill=1.0,
                                               base=0, pattern=[[-1, 128]], channel_multiplier=1), [ident_b], [ident_b])
        cp("pool", identb[:], identf[:], [ident_b], [ident_b])
        w_in_v = w_in.rearrange("(k p) o -> p k o", p=128)
        w_up_v = w_up.rearrange("(k p) o -> p k o", p=128)
        w_out_v = w_out.rearrange("(k p) o -> p k o", p=128)
        w_down_v = w_down.rearrange("(k p) o -> p k o", p=128)
        conv_list = []
        for oc in [12, 13, 14, 15] + [o for o in OC_ORDER if o not in (12, 13, 14, 15)]:
            conv_list.append((POS_OF_OC[oc], "col", w_in_v[:, :, oc * 128:(oc + 1) * 128]))
        for kc in range(8):
            conv_list.append((CH_OUT + kc, "row", w_out_v[:, kc, :]))
        for fc in range(32):
            conv_list.append((CH_UP + fc, "col", w_up_v[:, :, fc * 128:(fc + 1) * 128]))
        for kc in range(32):
            conv_list.append((CH_DOWN + kc, "row", w_down_v[:, kc, :]))
        cvp = {"i": 0}

        def convert_some(n):
            for _ in range(n):
                if cvp["i"] >= len(conv_list):
                    return
                chunk, kind, src = conv_list[cvp["i"]]
                cvp["i"] += 1
                dst = wscr[chunk].rearrange("p (k m) -> p k m", k=8) if kind == "col" else wscr[chunk]
                S.dma("pool", dst, src, [], [scr_b[chunk]], "cv")


        cst_b = Buf("cst")
        mset("pool", cst[:, 0:1], 1.0, [cst_b])
        mset("pool", cst[:, 1:2], EPS, [cst_b])
        wbdf_t = [(T["r"][0], TB["r"][0]), (T["i"][0], TB["i"][0])]
        for t_, b_ in wbdf_t:
            mset("pool", t_[:], 0.0, [b_])
        convert_some(12)
        for t_, bl in ((pbuf_p, pb_p_b), (xbuf_p, xb_p_b), (h0_p, h0_p_b)):
            mset("pool", t_[:], 0.0, bl)

        def pdma(out, in_, writes, **kw):
            S.dma("act", out, in_, [], writes, "par", allow_slow_non_contiguous=True, **kw)

        flag_b = Buf("flag")
        pdma(flag[:], flag_d, [flag_b])
        pst, pst_b = T["om"][0], TB["om"][0]
        rows = [(0, 8, g1_d.rearrange("(c p) -> c p", p=128)), (8, 8, g2_d.rearrange("(c p) -> c p", p=128)),
                (16, 16, lcw_d.rearrange("k (c p) -> (k c) p", p=128)), (32, 4, lcb_d.rearrange("(c p) -> c p", p=128)),
                (36, 12, caw_d.rearrange("k (c p) -> (k c) p", p=128)), (48, 4, ba_d.rearrange("(c p) -> c p", p=128)),
                (52, 4, bx_d.rearrange("(c p) -> c p", p=128)), (56, 4, apar_d.rearrange("(c p) -> c p", p=128))]
        pst_bs = [Buf("pst%d" % i) for i in range(len(rows))]
        for (r0, nr_, src), b_ in zip(rows, pst_bs):
            pdma(pst[r0:r0 + nr_, 0:128], src, [b_])
        bpar = nextbank()
        S.op("pe", lambda e: e.transpose(out=bank_ap(bpar, 60), in_=pst[0:60, 0:128], identity=identf[0:60, 0:60]),
             pst_bs + [ident_b], [bank_b[bpar], pst_b])
        for t_, c0, n_ in ((g1t, 0, 8), (g2t, 8, 8), (lcw, 16, 16), (lcb, 32, 4), (caw, 36, 12), (nba, 48, 4), (nbx, 52, 4),
                           (apar, 56, 4)):
            cp("dve", t_[:], ps[:, bpar * 512 + c0:bpar * 512 + c0 + n_], [bank_b[bpar]], [par_b])
        pdma(gfb[:], gf_d.partition_broadcast(128), [gfb_b])
        wb_bs = [[Buf("wbdf%d%d" % (gi, par)) for par in range(2)] for gi in range(2)]
        for gi, wd in enumerate((wa_d, wx_d)):
            wbdf3 = wbdf_t[gi][0][:].rearrange("p (g j) -> p g j", j=128)
            for par in range(2):
                src = wd.rearrange("(c two) i j -> two i c j", two=2)[par]
                S.dma("act", wbdf3[par * 64:(par + 1) * 64, :, par * 64:(par + 1) * 64], src, [wbdf_t[gi][1]], [wb_bs[gi][par]],
                      "par", allow_slow_non_contiguous=True)
        stin_bs = [Buf("stin%d" % i) for i in range(3)]
        pdma(stin[0:8, :], sca, [stin_bs[0]])
        pdma(stin[8:20, :], slc, [stin_bs[1]])
        pdma(stin[20:24, :], slh, [stin_bs[2]])
        for gi in range(2):
            cp("pool", wbd[:, gi * 512:(gi + 1) * 512], wbdf_t[gi][0][:], [wbdf_t[gi][1]] + wb_bs[gi], [wbd_b])
        ts("pool", nba[:], nba[:], -1.0, None, ALU.mult, None, [par_b], [par_b])
        ts("pool", nbx[:], nbx[:], -1.0, None, ALU.mult, None, [par_b], [par_b])
        sp_abs, sp_e, sp_u, sp_y = spt[:, 0:4], spt[:, 4:8], spt[:, 8:12], spt[:, 12:16]
        ts("dve", sp_e, apar[:], -1.0, None, ALU.mult, None, [par_b], [par_b])
        tt("dve", sp_abs, apar[:], sp_e, ALU.max, [par_b], [par_b])
        act(sp_e, sp_abs, AF.Exp, [par_b], [par_b], scale=-1.0)
        ts("dve", sp_u, sp_e, 1.0, None, ALU.add, None, [par_b], [par_b])
        act(sp_y, sp_u, AF.Ln, [par_b], [par_b])
        act(sp_abs, sp_y, AF.Exp, [par_b], [par_b], scale=-1.0)
        tt("dve", sp_abs, sp_abs, sp_u, ALU.mult, [par_b], [par_b])
        tt("dve", sp_y, sp_y, sp_abs, ALU.add, [par_b], [par_b])
        ts("dve", sp_y, sp_y, -1.0, None, ALU.add, None, [par_b], [par_b])
        ts("dve", sp_e, apar[:], 0.0, None, ALU.max, None, [par_b], [par_b])
        tt("dve", sp_y, sp_y, sp_e, ALU.add, [par_b], [par_b])
        ts("dve", nsp[:], sp_y, -8.0, None, ALU.mult, None, [par_b], [par_b])
        ts("dve", nsp2[:], sp_y, -16.0, None, ALU.mult, None, [par_b, cst_b, flag_b], [par_b])

        stT3 = stT[:].rearrange("p (c r) -> p c r", r=24)
        for c in range(4):
            b = nextbank()
            S.op("pe", lambda e, c=c, b=b: e.transpose(out=bank_ap(b, 24), in_=stin[0:24, c * 128:(c + 1) * 128],
                                                      identity=identf[0:24, 0:24]), stin_bs + [ident_b], [bank_b[b], stin_b])
            cp("dve", stT3[:, c, :], bank_ap(b, 24), [bank_b[b]], [stT_b])
        pbs4 = pbuf_s[:].rearrange("p (c s l) -> p c s l", c=4, s=4)
        xbs4 = xbuf_s[:].rearrange("p (c s l) -> p c s l", c=4, s=4)
        for c in range(4):
            cp("pool", pbs4[:, c, :, 0:2], stT3[:, c, 0:8].rearrange("p (s k) -> p s k", k=2), [stT_b], [pb_s_b[c]])
            cp("pool", xbs4[:, c, :, 0:3], stT3[:, c, 8:20].rearrange("p (s k) -> p s k", k=3), [stT_b], [xb_s_b[c]])
            cp("pool", h0_s[:, c * 4:(c + 1) * 4], stT3[:, c, 20:24], [stT_b], [h0_s_b[c]])

        def ring_load(chunk):
            s_ = st["ring"] % R
            st["ring"] += 1
            S.dma("sp", ring[:, s_ * 1024:(s_ + 1) * 1024], wscr[chunk], [scr_b[chunk]], [ring_b[s_]], "ring")
            return ring[:, s_ * 1024:(s_ + 1) * 1024], ring_b[s_]

        def load_x(src, row0, J):
            res = []
            for j in range(J):
                k = st["xs"] % 8
                st["xs"] += 1
                ap = xsl[:, k * D:(k + 1) * D]
                S.dma("sp", ap, src[row0 + j * 128:row0 + (j + 1) * 128, :], [], [xsl_b[k]], "x")
                res.append((ap, xsl_b[k]))
            return res

        def rstd_of(x_ap, x_buf):
            k = st["stat"] % 8
            st["stat"] += 1
            sbf = stat_b[k]
            ss, ln, rs = stat[:, 4 * k:4 * k + 1], stat[:, 4 * k + 1:4 * k + 2], stat[:, 4 * k + 2:4 * k + 3]
            kh = st["hb"] % 2
            st["hb"] += 1
            act(hb[:, kh * D:(kh + 1) * D], x_ap, AF.Square, [x_buf], [sbf, hb_b[kh]], accum_out=ss)
            act(ln, ss, AF.Ln, [sbf, par_b], [sbf], scale=1.0 / D, bias=cst[:, 1:2])
            act(rs, ln, AF.Exp, [sbf], [sbf], scale=-0.5)
            return rs, sbf, kh

        def norm_front(xs_list, j):
            x_ap, x_buf = xs_list[j]
            rs, sbf, k = rstd_of(x_ap, x_buf)
            hb_ap = hb[:, k * D:(k + 1) * D]
            ts("dve", hb_ap, x_ap, rs, None, ALU.mult, None, [x_buf, sbf], [hb_b[k]])
            return k

        def norm_back(k, gt, hi, j):
            hT3 = hTs[hi][:].rearrange("p (k t) -> p k t", k=8)
            gbc = gt[:, :, None].to_broadcast([128, 8, 128])
            hb_ap = hb[:, k * D:(k + 1) * D]
            b = nextbank()
            pb = bank_ap(b).bitcast(BF16)
            S.group("pe", [lambda e, kc=kc, hb_ap=hb_ap, pb=pb: e.transpose(
                out=pb[:, kc * 128:(kc + 1) * 128], in_=hb_ap[:, kc * 128:(kc + 1) * 128], identity=identb[:])
                for kc in range(8)], [hb_b[k], ident_b], [bank_b[b]])
            tt("dve", hT3[:, :, j * 128:(j + 1) * 128], pb.rearrange("p (k t) -> p k t", k=8), gbc, ALU.mult,
               [bank_b[b], par_b], [hT_bs[hi]])

        def norm_sub(xs_list, gt, hi, j):
            norm_back(norm_front(xs_list, j), gt, hi, j)

        def norm_T(xs_list, gt, hi):
            for j in range(len(xs_list)):
                norm_sub(xs_list, gt, hi, j)

        def proj(oc, N, hi, col0=0):
            slot, sbuf_ = ring_load(POS_OF_OC[oc])
            b = nextbank()
            S.group("pe", [lambda e, kc=kc, slot=slot, b=b: e.matmul(
                out=bank_ap(b, N), lhsT=slot[:, kc * 128:(kc + 1) * 128],
                rhs=hTs[hi][:, kc * 512 + col0:kc * 512 + col0 + N], start=(kc == 0), stop=(kc == 7))
                for kc in range(8)], [sbuf_, hT_bs[hi]], [bank_b[b]])
            return b

        def sigmoid_chain(buf_ap, bufs):
            act(buf_ap, buf_ap, AF.Ln, bufs + [par_b], bufs, bias=cst[:, 0:1])
            act(buf_ap, buf_ap, AF.Exp, bufs, bufs, scale=-1.0)

        class BCtx:
            pass

        def mixer_b_p1(c, N, Sq, L, xb_t, xb_bufs, h0_t, h0_bufs, lean, fix, so3, so_buf, hi):
            k = BCtx()
            k.c, k.N, k.Sq, k.L, k.lean, k.fix, k.so3, k.so_buf = c, N, Sq, L, lean, fix, so3, so_buf
            k.h0_t, k.h0_bufs = h0_t, h0_bufs
            q = c % 2
            k.T = {n: T[n][q] for n in tnames}
            k.TB = {n: TB[n][q] for n in tnames}
            if c >= 2:
                k.T["xc"], k.TB["xc"] = Txc[c - 2], TBxc[c - 2]
            k.T["gl"], k.TB["gl"] = Tgl[c % 3], TBgl[c % 3]
            k.xcb, k.xcb_b = xcbs[q], xcb_bs[q]
            W = L + 3
            k.xb3 = xb_t[:, c * Sq * W:(c + 1) * Sq * W].rearrange("p (s l) -> p s l", s=Sq)
            k.xbb = xb_bufs[c]
            T_, TB_, xb3, xbb = k.T, k.TB, k.xb3, k.xbb
            bL = proj(12 + c, N, hi)
            k.bG = None if lean else proj(16 + c, N, hi)
            cp("dve", xb3[:, :, 3:3 + L], v3(bank_ap(bL, N), Sq), [bank_b[bL]], [xbb])
            if not lean:
                cp("act", T_["gl"][:, 0:N], bank_ap(k.bG, N), [bank_b[k.bG]], [TB_["gl"]])
                act(T_["x2"][:, 0:N], T_["gl"][:, 0:N], AF.Square, [TB_["gl"]], [TB_["x2"]], scale=GK1 ** 0.5)
            xc3 = v3(T_["xc"][:, 0:N], Sq)
            ts("dve", xc3, xb3[:, :, 0:L], lcw[:, c:c + 1], lcb[:, c:c + 1], ALU.mult, ALU.add,
               [xbb, par_b], [TB_["xc"]])
            for kk in range(1, 4):
                stt(xc3, xb3[:, :, kk:kk + L], lcw[:, 4 * kk + c:4 * kk + c + 1], xc3, ALU.mult, ALU.add,
                    [xbb, par_b, TB_["xc"]], [TB_["xc"]])
            cp("dve", k.xcb[:, 0:N], T_["xc"][:, 0:N], [TB_["xc"]], [k.xcb_b])
            if not lean:
                stt(T_["x2"][:, 0:N], T_["x2"][:, 0:N], GK0, T_["gl"][:, 0:N], ALU.add, ALU.mult,
                    [TB_["x2"], TB_["gl"]], [TB_["x2"]])
            if so3 is not None:
                cp("pool", so3[:, c, 2 * Sq:5 * Sq].rearrange("p (s k) -> p s k", k=3), xb3[:, :, L:L + 3], [xbb], [so_buf])
            if Sq == 1:
                cp("pool", xb3[:, :, 0:3], xb3[:, :, L:L + 3], [xbb], [xbb])
            if fix == "mask":
                ts("dve", xb3[:, :, 0:3], xb3[:, :, 0:3], flag[:, 0:1], None, ALU.mult, None, [xbb, par_b], [xbb])
            return k

        def mixer_b_p2(k):
            c, N, Sq, L, T_, TB_ = k.c, k.N, k.Sq, k.L, k.T, k.TB
            bRa, bRi = nextbank(), nextbank()
            for gi, b in ((0, bRa), (1, bRi)):
                S.op("pe", lambda e, gi=gi, b=b: e.matmul(out=bank_ap(b, N), lhsT=wbd[:, (gi * 4 + c) * 128:(gi * 4 + c + 1) * 128],
                                                          rhs=k.xcb[:, 0:N], start=True, stop=True), [wbd_b, k.xcb_b], [bank_b[b]])
            r, i_, a, om, ixc = (T_[n][:, 0:N] for n in ("r", "i", "a", "om", "ixc"))
            q = c % 2
            nsig = 2 if k.lean else 3
            sig3 = rig[q][:].rearrange("p (s l) -> p s l", s=3)[:, 0:nsig, 0:N]
            sig_bufs = [TB_["r"], TB_["i"]] + ([] if k.lean else [TB_["x2"]])
            act(r, bank_ap(bRa, N), AF.Exp, [bank_b[bRa], par_b], [TB_["r"]], scale=-1.0, bias=nba[:, c:c + 1])
            act(i_, bank_ap(bRi, N), AF.Exp, [bank_b[bRi], par_b], [TB_["i"]], scale=-1.0, bias=nbx[:, c:c + 1])
            if not k.lean:
                x2, gl = T_["x2"][:, 0:N], T_["gl"][:, 0:N]
                act(x2, x2, AF.Exp, [TB_["x2"]], [TB_["x2"]], scale=-1.0)
            act(sig3, sig3, AF.Ln, sig_bufs + [par_b], sig_bufs, bias=cst[:, 0:1])
            act(sig3, sig3, AF.Exp, sig_bufs, sig_bufs, scale=-1.0)
            act(a, r, AF.Exp, [TB_["r"], par_b], [TB_["a"]], scale=nsp[:, c:c + 1])
            act(om, r, AF.Exp, [TB_["r"], par_b], [TB_["om"]], scale=nsp2[:, c:c + 1])
            if k.fix == "one":
                mset("pool", T_["om"][:, 0:1], 0.0, [TB_["om"]])
            elif k.fix == "flag":
                ts("dve", T_["om"][:, 0:1], T_["om"][:, 0:1], flag[:, 0:1], None, ALU.mult, None,
                   [TB_["om"], par_b], [TB_["om"]])
            tt("pool", ixc, i_, T_["xc"][:, 0:N], ALU.mult, [TB_["i"], TB_["xc"]], [TB_["ixc"]])
            act(om, om, AF.Ln, [TB_["om"], par_b], [TB_["om"]], scale=-1.0, bias=cst[:, 0:1])
            act(om, om, AF.Exp, [TB_["om"]], [TB_["om"]], scale=0.5)
            tt("pool", ixc, om, ixc, ALU.mult, [TB_["om"], TB_["ixc"]], [TB_["ixc"]])
            if not k.lean:
                tt("pool", gl, gl, x2, ALU.mult, [TB_["gl"], TB_["x2"]], [TB_["gl"]])

        def mixer_b_p3(k):
            c, N, Sq, L, T_, TB_, h0_t, h0_bufs = k.c, k.N, k.Sq, k.L, k.T, k.TB, k.h0_t, k.h0_bufs
            for s_ in range(Sq):
                S.op("dve", lambda e, s_=s_: e.tensor_tensor_scan(
                    out=T_["h"][:, s_ * L:(s_ + 1) * L], data0=T_["a"][:, s_ * L:(s_ + 1) * L],
                    data1=T_["ixc"][:, s_ * L:(s_ + 1) * L], initial=h0_t[:, c * Sq + s_:c * Sq + s_ + 1],
                    op0=ALU.mult, op1=ALU.add), [TB_["a"], TB_["ixc"], h0_bufs[c]], [TB_["h"]])
            h = T_["h"][:, 0:N]
            h3 = v3(h, Sq)
            cp("pool", h0_t[:, c * Sq:(c + 1) * Sq], h3[:, :, L - 1], [TB_["h"]], [h0_bufs[c]])
            if k.fix == "mask":
                ts("dve", h0_t[:, c:c + 1], h0_t[:, c:c + 1], flag[:, 0:1], None, ALU.mult, None,
                   [h0_bufs[c], par_b], [h0_bufs[c]])
            if k.so3 is not None:
                cp("pool", k.so3[:, c, 5 * Sq:6 * Sq], h3[:, :, L - 1], [TB_["h"]], [k.so_buf])
            if not k.lean:
                tt("dve", mixT[:, (4 + c) * 512:(4 + c) * 512 + N], h, T_["gl"][:, 0:N], ALU.mult,
                   [TB_["h"], TB_["gl"]], [mixT_bs[4 + c]])

        def mixers_b_gen(args, hooks=(), pre=None):
            def hook(i):
                if i < len(hooks) and hooks[i] is not None:
                    hooks[i]()
            ks = {}
            ks[0] = mixer_b_p1(0, *args)
            if pre is not None:
                pre()
            yield 1
            ks[1] = mixer_b_p1(1, *args)
            hook(0)
            yield 2
            mixer_b_p2(ks[0])
            yield 3
            ks[2] = mixer_b_p1(2, *args)
            yield 4
            hook(1)
            mixer_b_p2(ks[1])
            mixer_b_p3(ks[0])
            yield 5
            ks[3] = mixer_b_p1(3, *args)
            hook(2)
            yield 6
            mixer_b_p2(ks[2])
            mixer_b_p3(ks[1])
            yield 7
            hook(3)
            mixer_b_p2(ks[3])
            mixer_b_p3(ks[2])
            hook(4)
            mixer_b_p3(ks[3])

        def mixers_b(args, hooks=(), pre=None):
            for _ in mixers_b_gen(args, hooks, pre):
                pass

        def mixer_a(c, N, Sq, L, pb_t, pb_bufs, so3, so_buf, hi):
            q = c % 2
            T_ = {n: T[n][q] for n in tnames}
            TB_ = {n: TB[n][q] for n in tnames}
            W = L + 2
            pb3 = pb_t[:, c * Sq * W:(c + 1) * Sq * W].rearrange("p (s l) -> p s l", s=Sq)
            pbb = pb_bufs[c]
            bX = proj(8 + c, N, hi)
            bC = proj(4 + c, N, hi)
            bB = proj(c, N, hi)
            tA, ya = T_["tA"][:, 0:N], T_["ya"][:, 0:N]
            cp("act", tA, bank_ap(bX, N), [bank_b[bX]], [TB_["tA"]])
            tt("dve", pb3[:, :, 2:2 + L], v3(bank_ap(bC, N), Sq), v3(tA, Sq), ALU.mult, [bank_b[bC], TB_["tA"]], [pbb])
            ya3 = v3(ya, Sq)
            ts("dve", ya3, pb3[:, :, 0:L], caw[:, c:c + 1], None, ALU.mult, None, [pbb, par_b], [TB_["ya"]])
            for k in (1, 2):
                stt(ya3, pb3[:, :, k:k + L], caw[:, 4 * k + c:4 * k + c + 1], ya3, ALU.mult, ALU.add,
                    [pbb, par_b, TB_["ya"]], [TB_["ya"]])
            tt("dve", mixT[:, c * 512:c * 512 + N], bank_ap(bB, N), ya, ALU.mult, [bank_b[bB], TB_["ya"]], [mixT_bs[c]])
            if so3 is not None:
                cp("pool", so3[:, c, 0:2 * Sq].rearrange("p (s k) -> p s k", k=2), pb3[:, :, L:L + 2], [pbb], [so_buf])
            if Sq == 1:
                cp("pool", pb3[:, :, 0:2], pb3[:, :, L:L + 2], [pbb], [pbb])

        def epilogue_parts(xs_list, ydst, yrow0):
            J = len(xs_list)
            keep = {}

            def front(j):
                x_ap, x_buf = xs_list[j]
                keep[j] = rstd_of(x_ap, x_buf)

            def back(j):
                x_ap, x_buf = xs_list[j]
                rs, sbf, _ = keep[j]
                stt(x_ap, x_ap, rs, gfb[:], ALU.mult, ALU.mult, [x_buf, sbf, gfb_b], [x_buf])
                ev = S.dma("pool", ydst[yrow0 + j * 128:yrow0 + (j + 1) * 128, :], x_ap, [x_buf], [], "st")
                out_events.append(ev)

            def stage(i):
                if i < J:
                    front(i)
                if 1 <= i <= J:
                    back(i - 1)
            return [(lambda i=i: stage(i)) for i in range(5)]

        def make_mix_gen(N, Sq, L, pb_t, pb_bufs, xb_t, xb_bufs, h0_t, h0_bufs, fix, so3, so_buf, prev_epi, extra0=None,
                         defer_epi=False):
            pe_ = list(prev_epi) if prev_epi else [None] * 5
            if defer_epi:
                def h3():
                    for i in (0, 1, 2):
                        pe_[i]()
                    mixer_a(2, N, Sq, L, pb_t, pb_bufs, so3, so_buf, 0)

                def h4():
                    for i in (3, 4):
                        pe_[i]()
                    mixer_a(3, N, Sq, L, pb_t, pb_bufs, so3, so_buf, 0)
                return mixers_b_gen((N, Sq, L, xb_t, xb_bufs, h0_t, h0_bufs, False, fix, so3, so_buf, 0),
                                    [None, lambda: mixer_a(0, N, Sq, L, pb_t, pb_bufs, so3, so_buf, 0),
                                     lambda: mixer_a(1, N, Sq, L, pb_t, pb_bufs, so3, so_buf, 0), h3, h4], pre=None)

            def mk(i, fn2):
                def f():
                    if extra0 is not None:
                        extra0()
                    if pe_[i] is not None:
                        pe_[i]()
                    if fn2 is not None:
                        fn2()
                return f
            return mixers_b_gen((N, Sq, L, xb_t, xb_bufs, h0_t, h0_bufs, False, fix, so3, so_buf, 0),
                                [mk(1, None), mk(2, None), mk(3, None),
                                 mk(4, lambda: mixer_a(0, N, Sq, L, pb_t, pb_bufs, so3, so_buf, 0)),
                                 lambda: mixer_a(1, N, Sq, L, pb_t, pb_bufs, so3, so_buf, 0)], pre=pe_[0])

        def main_tile(xs_list, N, Sq, L, ydst, yrow0, pb_t, pb_bufs, xb_t, xb_bufs, h0_t, h0_bufs, fix, so3, so_buf,
                      mid_hook, start_hook, prev_epi, extra0=None, cur_gen=None, next_factory=None, after_mix=None):
            J = N // 128
            my_epi = epilogue_parts(xs_list, ydst, yrow0)
            was_prerun = cur_gen is not None
            if cur_gen is None:
                cur_gen = make_mix_gen(N, Sq, L, pb_t, pb_bufs, xb_t, xb_bufs, h0_t, h0_bufs, fix, so3, so_buf, prev_epi, extra0)
            for _ in cur_gen:
                pass
            wslots = [ring_load(CH_OUT + kc) for kc in range(8)]
            if not was_prerun:
                for c in (2, 3):
                    mixer_a(c, N, Sq, L, pb_t, pb_bufs, so3, so_buf, 0)
            if after_mix is not None:
                after_mix()
            P1K = (0, 1, 2, 4, 5, 6)
            for b in sorted(range(2 * J), key=lambda b_: bank_age[b_]):
                j, half = b // 2, b % 2
                S.group("pe", [lambda e, kc=kc, j=j, half=half, b=b: e.matmul(
                    out=bank_ap(b), lhsT=mixT[:, kc * 512 + j * 128:kc * 512 + (j + 1) * 128],
                    rhs=wslots[kc][0][:, half * 512:(half + 1) * 512], start=(kc == 0), stop=False)
                    for kc in P1K], [mixT_bs[kc] for kc in P1K] + [wslots[kc][1] for kc in P1K], [bank_b[b]])
            st["bank"] = 2 * J
            for j in range(J):
                for half in range(2):
                    b = 2 * j + half
                    S.group("pe", [lambda e, kc=kc, j=j, half=half, b=b: e.matmul(
                        out=bank_ap(b), lhsT=mixT[:, kc * 512 + j * 128:kc * 512 + (j + 1) * 128],
                        rhs=wslots[kc][0][:, half * 512:(half + 1) * 512], start=False, stop=(kc == 7))
                        for kc in (3, 7)], [mixT_bs[3], mixT_bs[7], wslots[3][1], wslots[7][1]], [bank_b[b]])
            for j in range(J):
                x_ap, x_buf = xs_list[j]
                tt("dve", x_ap, ps[:, 2 * j * 512:(2 * j + 2) * 512], x_ap, ALU.add,
                   [bank_b[2 * j], bank_b[2 * j + 1], x_buf], [x_buf])
            fr = {}
            for j in range(J):
                fr[j] = norm_front(xs_list, j)
                if j >= 1:
                    norm_back(fr[j - 1], g2t, 1, j - 1)
            norm_back(fr[J - 1], g2t, 1, J - 1)
            if start_hook is not None:
                start_hook()
            for fc in range(32):
                if mid_hook is not None and 8 <= fc <= 16 and fc % 2 == 0:
                    mid_hook((fc - 8) // 2)
                slot, sbuf_ = ring_load(CH_UP + fc)
                b = nextbank()
                S.group("pe", [lambda e, kc=kc, slot=slot, b=b: e.matmul(
                    out=bank_ap(b, N), lhsT=slot[:, kc * 128:(kc + 1) * 128], rhs=hTs[1][:, kc * 512:kc * 512 + N],
                    start=(kc == 0), stop=(kc == 7)) for kc in range(8)], [sbuf_, hT_bs[1]], [bank_b[b]])
                k = st["rl"] % 2
                st["rl"] += 1
                rl_ap = rl[:, k * 512:k * 512 + N]
                act(rl_ap, bank_ap(b, N), AF.Relu, [bank_b[b]], [rl_b[k]])
                tt("dve" if fc % 2 == 0 else "pool", uT[:, fc * 512:fc * 512 + N], rl_ap, rl_ap, ALU.mult, [rl_b[k]], [uT_b[fc]])
            def down_pass(js):
                for kc in range(32):
                    slot, sbuf_ = ring_load(CH_DOWN + kc)
                    fns = []
                    for j in js:
                        for half in range(2):
                            fns.append(lambda e, kc=kc, j=j, half=half, slot=slot: e.matmul(
                                out=bank_ap(2 * j + half), lhsT=uT[:, kc * 512 + j * 128:kc * 512 + (j + 1) * 128],
                                rhs=slot[:, half * 512:(half + 1) * 512], start=(kc == 0), stop=(kc == 31)))
                    S.group("pe", fns, [sbuf_, uT_b[kc]], [bank_b[2 * j + h_] for j in js for h_ in range(2)])
                    yield kc

            def down_adds(js):
                for j in js:
                    x_ap, x_buf = xs_list[j]
                    tt("dve", x_ap, ps[:, 2 * j * 512:(2 * j + 2) * 512], x_ap, ALU.add,
                       [bank_b[2 * j], bank_b[2 * j + 1], x_buf], [x_buf])

            next_gen = None
            if next_factory is not None and J == 4:
                next_gen = next_factory(my_epi)
                allowed["banks"] = [4, 5, 6, 7]
                st["bank"] = 0
                nsteps = 0
                for pi, js in enumerate(((0, 1), (2, 3))):
                    for kc in range(32):
                        slot, sbuf_ = ring_load(CH_DOWN + kc)
                        fns = []
                        for jj, j in enumerate(js):
                            for half in range(2):
                                fns.append(lambda e, kc=kc, j=j, jj=jj, half=half, slot=slot: e.matmul(
                                    out=bank_ap(2 * jj + half), lhsT=uT[:, kc * 512 + j * 128:kc * 512 + (j + 1) * 128],
                                    rhs=slot[:, half * 512:(half + 1) * 512], start=(kc == 0), stop=(kc == 31)))
                        S.group("pe", fns, [sbuf_, uT_b[kc]], [bank_b[i] for i in range(4)])
                        if kc in ((5, 13, 21) if pi == 0 else (8, 20)) and nsteps < 7:
                            next(next_gen)
                            nsteps += 1
                    for jj, j in enumerate(js):
                        x_ap, x_buf = xs_list[j]
                        tt("dve", x_ap, ps[:, 2 * jj * 512:(2 * jj + 2) * 512], x_ap, ALU.add,
                           [bank_b[2 * jj], bank_b[2 * jj + 1], x_buf], [x_buf])
                    if pi == 0:
                        for _ in range(2):
                            next(next_gen)
                            nsteps += 1
                allowed["banks"] = list(range(8))
            else:
                for _ in down_pass(tuple(range(J))):
                    pass
                st["bank"] = 2 * J
                down_adds(tuple(range(J)))
                if next_factory is not None:
                    next_gen = next_factory(my_epi)
            return my_epi, next_gen

        tiles = [("L", xl, lt * 512, 4) for lt in range(4)] + [("M", xp, t * 512, 4) for t in range(4)] + [("M", xs, 0, 2)]
        xs_of = {}

        def load_tile(i):
            if i < len(tiles):
                xs_of[i] = load_x(tiles[i][1], tiles[i][2], tiles[i][3])

        load_tile(0)
        load_tile(1)
        norm_T(xs_of[0], g1t, 0)
        for lt in range(4):
            load_tile(lt + 2)
            hi = lt % 2
            nx, nh = xs_of[lt + 1], (lt + 1) % 2
            kk = {}

            def nstage(i, nx=nx, nh=nh, kk=kk):
                convert_some(2 if cvp["i"] < 52 else 0)
                if i < 4:
                    kk[i] = norm_front(nx, i)
                if 1 <= i <= 4:
                    norm_back(kk[i - 1], g1t, nh, i - 1)
            mixers_b((512, 1, 512, xbuf_p, xb_p_b, h0_p, h0_p_b, True, "one" if lt == 0 else ("mask" if lt == 3 else None), None, None, hi),
                     [(lambda i=i, ns=nstage: ns(i)) for i in range(1, 5)], pre=lambda ns=nstage: ns(0))
            if lt == 3:
                for c in range(4):
                    q = c % 2
                    bC, bX = proj(4 + c, 2, hi, 510), proj(8 + c, 2, hi, 510)
                    cp("act", T["tA"][q][:, 0:2], bank_ap(bX, 2), [bank_b[bX]], [TB["tA"][q]])
                    tt("dve", pbuf_p[:, c * 514:c * 514 + 2], bank_ap(bC, 2), T["tA"][q][:, 0:2], ALU.mult,
                       [bank_b[bC], TB["tA"][q]], [pb_p_b[c]])
        for c in range(4):
            ts("dve", pbuf_p[:, c * 514:c * 514 + 2], pbuf_p[:, c * 514:c * 514 + 2], flag[:, 0:1], None, ALU.mult, None,
               [pb_p_b[c], par_b], [pb_p_b[c]])

        soT_p3 = soT_p[:].rearrange("p (c r) -> p c r", r=6)
        soT_s3 = soT_s[:].rearrange("p (c r) -> p c r", r=24)
        def tile_args(t):
            if t < 4:
                return (512, 1, 512, pbuf_p, pb_p_b, xbuf_p, xb_p_b, h0_p, h0_p_b, "flag" if t == 0 else None,
                        soT_p3 if t == 3 else None, soT_p_b)
            return (256, 4, 64, pbuf_s, pb_s_b, xbuf_s, xb_s_b, h0_s, h0_s_b, None, soT_s3, soT_s_b)

        def emit_state_outputs():
            for (soT3, sbuf_, so_t, so_buf2, nr, dst) in ((soT_p3, soT_p_b, so_p, so_p_b, 6, stp_d),
                                                         (soT_s3, soT_s_b, so_s, so_s_b, 24, sts_d)):
                for c in range(4):
                    b = nextbank()
                    S.op("pe", lambda e, c=c, b=b, soT3=soT3, nr=nr: e.transpose(
                        out=ps[0:nr, b * 512:b * 512 + 128], in_=soT3[:, c, :], identity=identf[:]),
                        [sbuf_, ident_b], [bank_b[b]])
                    cp("dve", so_t[0:nr, c * 128:(c + 1) * 128], ps[0:nr, b * 512:b * 512 + 128], [bank_b[b]], [so_buf2])
                out_events.append(S.dma("sp", dst, so_t[0:nr, :], [so_buf2], [], "sto"))

        epi, gen = None, None
        for t in range(5):
            ti = 4 + t
            mid = None
            if t < 4:
                kk2 = {}

                def mid(i, kk2=kk2, ti=ti, t=t):
                    if t == 0:
                        convert_some(4)
                    nx = xs_of[ti + 1]
                    if i < len(nx):
                        kk2[i] = norm_front(nx, i)
                    if 1 <= i <= len(nx):
                        norm_back(kk2[i - 1], g1t, 0, i - 1)
            end = (lambda ti=ti: load_tile(ti + 1)) if 1 <= t < 4 else None
            N_, Sq_, L_, pbt, pbb_, xbt, xbb_, h0t, h0b, fix_, so3_, sob_ = tile_args(t)
            nf = None
            if t < 4:
                a_ = tile_args(t + 1)
                nf = (lambda epi_, a_=a_: make_mix_gen(a_[0], a_[1], a_[2], a_[3], a_[4], a_[5], a_[6], a_[7], a_[8], a_[9],
                                                       a_[10], a_[11], epi_, None, True))
            epi, gen = main_tile(xs_of[ti], N_, Sq_, L_, yp if t < 4 else ys, (t * 512) if t < 4 else 0,
                                 pbt, pbb_, xbt, xbb_, h0t, h0b, fix_, so3_, sob_, mid, end, epi,
                                 (lambda: convert_some(5)) if t == 0 else None, gen, nf,
                                 emit_state_outputs if t == 4 else None)
        for f_ in epi:
            f_()

        S.wait_events("sp", out_events)
        S.emit()
    return nc


_NC_CACHE = {}


def kernel(x_prompt, x_sample, state_conv_a, state_lru_conv, state_lru_h, norm1_g, w_in, conv_a_w, lru_conv_w,
           lru_conv_b, lru_wa, lru_ba, lru_wx, lru_bx, lru_a_param, w_out, norm2_g, w_up, w_down, norm_f_g):
    f = lambda a: np.ascontiguousarray(np.asarray(a, dtype=np.float32))
    x_prompt, x_sample = f(x_prompt), f(x_sample)
    if "nc" not in _NC_CACHE:
        _NC_CACHE["nc"] = build_program()
    nc = _NC_CACHE["nc"]
    shared = {
        "w_in": f(w_in[0]), "w_out": f(w_out[0]), "w_up": f(w_up[0]), "w_down": f(w_down[0]),
        "g1": f(norm1_g[0]), "g2": f(norm2_g[0]), "gf": f(norm_f_g),
        "caw": f(conv_a_w[0]), "lcw": f(lru_conv_w[0]), "lcb": f(lru_conv_b[0]),
        "wa": f(lru_wa[0]), "wx": f(lru_wx[0]), "ba": f(np.reshape(lru_ba[0], -1)), "bx": f(np.reshape(lru_bx[0], -1)),
        "apar": f(lru_a_param[0]),
    }
    sca, slc, slh = f(state_conv_a[0]), f(state_lru_conv[0]), f(state_lru_h[0])
    in_maps = []
    for c in range(NCORES):
        s, hf = c // 2, c % 2
        flag = np.zeros((128, 2), np.float32)
        flag[:, 0] = float(hf)
        flag[:, 1] = 1.0 - float(hf)
        m = dict(shared)
        m.update({
            "xp": f(x_prompt[s, hf * 2048:(hf + 1) * 2048]),
            "xl": f(x_prompt[s, 0:2048]),
            "xs": f(x_sample[4 * c:4 * c + 4].reshape(256, D)),
            "flag": flag,
            "sca": f(sca[4 * c:4 * c + 4].reshape(8, 512)),
            "slc": f(slc[4 * c:4 * c + 4].reshape(12, 512)),
            "slh": f(slh[4 * c:4 * c + 4].reshape(4, 512)),
        })
        in_maps.append(m)
    res = run_bass_kernel_spmd(nc, in_maps, core_ids=list(range(NCORES)))
    rs = res.results
    y_prompt = np.stack([np.concatenate([rs[2 * s]["yp"], rs[2 * s + 1]["yp"]], axis=0) for s in range(4)], axis=0)
    y_sample = np.concatenate([rs[c]["ys"].reshape(4, 64, D) for c in range(NCORES)], axis=0)
    stp = np.stack([rs[2 * s + 1]["stp"] for s in range(4)], axis=0)
    new_a_p = stp[:, 0:2][None]
    new_l_p = stp[:, 2:5][None]
    new_h_p = stp[:, 5][None]
    sts = np.stack([rs[c]["sts"] for c in range(NCORES)], axis=0)
    new_a_s = sts[:, 0:8].reshape(32, 2, 512)[None]
    new_l_s = sts[:, 8:20].reshape(32, 3, 512)[None]
    new_h_s = sts[:, 20:24].reshape(32, 512)[None]
    out = (y_prompt, y_sample, new_a_p, new_l_p, new_h_p, new_a_s, new_l_s, new_h_s)
    return tuple(np.ascontiguousarray(o, dtype=np.float32) for o in out)
```
